# Optimizing a Trainium2 kernel written in Bass

```python
import jax, jax.numpy as jnp
from jax import lax
import numpy as np

D_MODEL = 1024
BATCH = 4
SEQ = 8192
DEPTH = 2

HG_WIDTH = D_MODEL // 2
HG_DK = 128
HG_HEADS = HG_WIDTH // HG_DK
HG_CHUNK = 64
RW_WIDTH = D_MODEL // 2
RW_HEAD = 64
RW_HEADS = RW_WIDTH // RW_HEAD
RW_DECAY_RANK = 64
RW_ICL_RANK = 64
RW_VRES_RANK = 32
N_BRANCH = 2
HG_COLS = 4 * HG_WIDTH
RW_COLS = 4 * RW_WIDTH + RW_DECAY_RANK + RW_ICL_RANK
GATE_COLS = N_BRANCH * D_MODEL
N_IN = HG_COLS + RW_COLS + GATE_COLS
DN_ALPHA = (2 * DEPTH) ** 0.25
DN_BETA = (8 * DEPTH) ** -0.25
LN_EPS = 1e-5
RMS_EPS = 1e-6
GN_EPS = 64e-5
L2_EPS = 1e-12
LB_FLOOR = 1e-30

kernel_name = "hgrn2_rwkv7_gated_hybrid"


def layer_norm(x, g, b):
    xf = x.astype(jnp.float32)
    mu = xf.mean(-1, keepdims=True)
    var = jnp.mean(jnp.square(xf - mu), -1, keepdims=True)
    return ((xf - mu) * lax.rsqrt(var + LN_EPS)).astype(x.dtype) * g + b


def heads(t, n):
    return t.reshape(t.shape[:-1] + (-1, n))


def token_shift(p, mu):
    prev = jnp.pad(p[:, :-1], ((0, 0), (1, 0), (0, 0)))
    return p + (prev - p) * mu


def hgrn2_chunked(q, k, v, log_f):
    B, T, H, K = q.shape
    V = v.shape[-1]
    C = HG_CHUNK
    N = T // C

    def to_chunks(a):
        return a.reshape(B, N, C, H, a.shape[-1]).transpose(1, 0, 3, 2, 4)

    qc, kc, vc, gc = (to_chunks(a) for a in (q, k, v, log_f))
    causal = jnp.tril(jnp.ones((C, C), dtype=bool))[:, :, None]

    def step(S, inp):
        qi, ki, vi, gi = inp
        b = jnp.cumsum(gi, axis=2)
        diff = b[:, :, :, None, :] - b[:, :, None, :, :]
        decay = jnp.where(causal, jnp.exp(jnp.where(causal, diff, 0.0)), 0.0)
        scores = jnp.einsum('bhtk,bhtsk,bhsk->bhts', qi, decay, ki)
        o = (jnp.einsum('bhts,bhsv->bhtv', scores, vi)
             + jnp.einsum('bhtk,bhkv->bhtv', qi * jnp.exp(b), S))
        b_last = b[:, :, -1:, :]
        S = (jnp.exp(b_last[:, :, 0, :])[..., None] * S
             + jnp.einsum('bhsk,bhsv->bhkv', ki * jnp.exp(b_last - b), vi))
        return S, o

    S0 = jnp.zeros((B, H, K, V), q.dtype)
    _, o = lax.scan(step, S0, (qc, kc, vc, gc))
    return o.transpose(1, 0, 3, 2, 4).reshape(B, T, H, V)


def rwkv7_scan(r, w, k, v, kk, a):
    B, T, H, N = r.shape

    def step(S, inp):
        r_t, w_t, k_t, v_t, kk_t, a_t = inp
        sa = jnp.einsum('bhvk,bhk->bhv', S, -kk_t)
        S = (S * w_t[:, :, None, :] + sa[..., None] * (kk_t * a_t)[:, :, None, :]
             + v_t[..., None] * k_t[:, :, None, :])
        y = jnp.einsum('bhvk,bhk->bhv', S, r_t)
        return S, y

    xs = tuple(jnp.moveaxis(t, 1, 0) for t in (r, w, k, v, kk, a))
    S0 = jnp.zeros((B, H, N, N), r.dtype)
    _, y = lax.scan(step, S0, xs)
    return jnp.moveaxis(y, 0, 1)


def setup_inputs(seed: int = 0) -> dict:
    key = jax.random.key(seed)
    ks = iter(jax.random.split(key, 32))
    nrm = lambda shape, s: jax.random.normal(next(ks), shape, jnp.float32) * s
    uni = lambda shape, lo, hi: jax.random.uniform(next(ks), shape, jnp.float32, lo, hi)
    return {
        "x": nrm((BATCH, SEQ, D_MODEL), 1.0),
        "c": nrm((BATCH, D_MODEL), 1.0),
        "w_ada": nrm((DEPTH, D_MODEL, 3 * D_MODEL), 0.1 * D_MODEL ** -0.5),
        "b_ada": nrm((DEPTH, 3 * D_MODEL), 0.01),
        "w_in": nrm((DEPTH, D_MODEL, N_IN), D_MODEL ** -0.5),
        "hg_lower_bounds": nrm((DEPTH, HG_WIDTH), 0.5),
        "hg_norm_g": 1.0 + nrm((DEPTH, HG_WIDTH), 0.01),
        "rw_mu": uni((DEPTH, RW_COLS), 0.0, 1.0),
        "rw_w0": uni((DEPTH, RW_WIDTH), -6.0, 0.0),
        "rw_w_up": nrm((DEPTH, RW_DECAY_RANK, RW_WIDTH), 0.1 * RW_DECAY_RANK ** -0.5),
        "rw_a0": nrm((DEPTH, RW_WIDTH), 0.1),
        "rw_a_up": nrm((DEPTH, RW_ICL_RANK, RW_WIDTH), 0.1 * RW_ICL_RANK ** -0.5),
        "rw_k_k": 0.85 + nrm((DEPTH, RW_WIDTH), 0.02),
        "rw_k_a": 1.0 + nrm((DEPTH, RW_WIDTH), 0.02),
        "rw_r_k": nrm((DEPTH, RW_HEADS, RW_HEAD), 0.1),
        "rw_v0": 1.0 + nrm((DEPTH - 1, RW_WIDTH), 0.1),
        "rw_v_down": nrm((DEPTH - 1, D_MODEL, RW_VRES_RANK), D_MODEL ** -0.5),
        "rw_v_up": nrm((DEPTH - 1, RW_VRES_RANK, RW_WIDTH), 0.1 * RW_VRES_RANK ** -0.5),
        "rw_gn_g": 1.0 + nrm((DEPTH, RW_WIDTH), 0.01),
        "rw_gn_b": nrm((DEPTH, RW_WIDTH), 0.01),
        "w_branch_hg": nrm((DEPTH, HG_WIDTH, D_MODEL), DN_BETA * HG_WIDTH ** -0.5),
        "w_branch_rw": nrm((DEPTH, RW_WIDTH, D_MODEL), DN_BETA * RW_WIDTH ** -0.5),
        "w_out": nrm((DEPTH, D_MODEL, D_MODEL), DN_BETA * D_MODEL ** -0.5),
        "ln_g": 1.0 + nrm((DEPTH, D_MODEL), 0.01),
        "ln_b": nrm((DEPTH, D_MODEL), 0.01),
    }


def reference(x, c, w_ada, b_ada, w_in, hg_lower_bounds, hg_norm_g, rw_mu, rw_w0, rw_w_up,
              rw_a0, rw_a_up, rw_k_k, rw_k_a, rw_r_k, rw_v0, rw_v_down, rw_v_up, rw_gn_g, rw_gn_b,
              w_branch_hg, w_branch_rw, w_out, ln_g, ln_b):
    f32 = jnp.float32
    B, T, _ = x.shape
    lb_soft = jax.nn.softmax(hg_lower_bounds.astype(f32), axis=0)
    lower_bounds = jnp.cumsum(lb_soft, axis=0) - lb_soft[0]
    v_first = None
    for l in range(DEPTH):
        cond = jax.nn.silu(c) @ w_ada[l] + b_ada[l]
        shift, scale, gate_raw = jnp.split(cond, 3, axis=-1)
        h = x * (1.0 + scale[:, None]) + shift[:, None]
        proj = h @ w_in[l]
        p_hg, p_rw, p_gate = jnp.split(proj, [HG_COLS, HG_COLS + RW_COLS], axis=-1)

        p_hg = p_hg.astype(f32)
        q, f_pre, i_val, z_hg = jnp.split(p_hg, 4, axis=-1)
        lb = lower_bounds[l]
        log_f = jnp.logaddexp(jnp.log(jnp.maximum(lb, LB_FLOOR)),
                              jnp.log1p(-lb) + jax.nn.log_sigmoid(f_pre))
        k_hg = (1.0 - lb) * jax.nn.sigmoid(-f_pre)
        o_hg = hgrn2_chunked(heads(q, HG_DK), heads(k_hg, HG_DK), heads(i_val, HG_DK), heads(log_f, HG_DK))
        o_hg = o_hg * lax.rsqrt(jnp.mean(jnp.square(o_hg), -1, keepdims=True) + RMS_EPS)
        o_hg = o_hg.reshape(B, T, HG_WIDTH) * hg_norm_g[l] * jax.nn.silu(z_hg)
        y_hg = o_hg.astype(x.dtype) @ w_branch_hg[l]

        p_rw = token_shift(p_rw.astype(f32), rw_mu[l])
        r, k, v, z_rw, w_down, a_down = jnp.split(
            p_rw, [RW_WIDTH, 2 * RW_WIDTH, 3 * RW_WIDTH, 4 * RW_WIDTH, 4 * RW_WIDTH + RW_DECAY_RANK], axis=-1)
        w_raw = -jax.nn.softplus(-(rw_w0[l] + jnp.tanh(w_down) @ rw_w_up[l])) - 0.5
        decay = jnp.exp(-jnp.exp(w_raw))
        if l == 0:
            v_first = v
        else:
            v_mix = jax.nn.sigmoid(rw_v0[l - 1] + (h.astype(f32) @ rw_v_down[l - 1]) @ rw_v_up[l - 1])
            v = v + (v_first - v) * v_mix
        a = jax.nn.sigmoid(rw_a0[l] + a_down @ rw_a_up[l])
        kk = heads(k * rw_k_k[l], RW_HEAD)
        kk = kk / jnp.maximum(jnp.sqrt(jnp.sum(jnp.square(kk), -1, keepdims=True)), L2_EPS)
        k = k * (1.0 + (a - 1.0) * rw_k_a[l])
        r_h, k_h, v_h = heads(r, RW_HEAD), heads(k, RW_HEAD), heads(v, RW_HEAD)
        y_rw = rwkv7_scan(r_h, heads(decay, RW_HEAD), k_h, v_h, kk, heads(a, RW_HEAD))
        mu = y_rw.mean(-1, keepdims=True)
        var = jnp.mean(jnp.square(y_rw - mu), -1, keepdims=True)
        y_rw = ((y_rw - mu) * lax.rsqrt(var + GN_EPS)).reshape(B, T, RW_WIDTH) * rw_gn_g[l] + rw_gn_b[l]
        bonus = jnp.sum(r_h * k_h * rw_r_k[l], -1, keepdims=True) * v_h
        y_rw = (y_rw + bonus.reshape(B, T, RW_WIDTH)) * jax.nn.silu(z_rw)
        y_rw = y_rw.astype(x.dtype) @ w_branch_rw[l]

        g_hg, g_rw = jnp.split(p_gate, N_BRANCH, axis=-1)
        merged = jax.nn.sigmoid(g_hg) * y_hg + jax.nn.sigmoid(g_rw) * y_rw
        out = merged @ w_out[l]
        x = layer_norm(DN_ALPHA * x + (1.0 + gate_raw[:, None]) * out, ln_g[l], ln_b[l])
    return x
```

```python
import contextlib
import math
import numpy as np
import concourse.bass as bass
import concourse.mybir as mybir
from concourse.bass_utils import run_bass_kernel_spmd

F32 = mybir.dt.float32
BF16 = mybir.dt.bfloat16
AF = mybir.ActivationFunctionType
ALU = mybir.AluOpType
AX = mybir.AxisListType

ENGS = ("pe", "dve", "act", "pool", "sp")
EPOCH = 7000
NDMASEM = 8

D = 1024
NIN = 6272
DN_ALPHA = 4.0 ** 0.25
LN_EPS = 1e-5
RMS_EPS = 1e-6
GN_EPS = 64e-5
NB = 256
import os
CUT = int(os.environ.get('KCUT', '0'))
SKIP = int(os.environ.get('KSKIP', '0'))
SKIPM = int(os.environ.get('KSKIPM', '0'))
KV = int(os.environ.get('KV', '0'))
MERGE_PR = int(os.environ.get('KMERGE', '1'))
CH = 64


class Res:
    __slots__ = ("last_w", "readers", "name", "excl")

    def __init__(self, name="", excl=False):
        self.last_w = None
        self.readers = []
        self.name = name
        self.excl = excl


class Tl:
    def __init__(self, ap, name):
        self.ap = ap
        self.res = Res(name)
        self.name = name
        self.sub = {}

    def v3(self, a, b):
        return self.ap.rearrange("p (a b) -> p a b", a=a, b=b)

    def v4(self, a, b, c):
        return self.ap.rearrange("p (a b c) -> p a b c", a=a, b=b, c=c)

    def r(self, key):
        x = self.sub.get(key)
        if x is None:
            x = Res(f"{self.name}:{key}")
            self.sub[key] = x
        return x


class Sched:
    def __init__(self, nc, stack):
        self.nc = nc
        self.stack = stack
        self.q = {e: [] for e in ENGS}
        self.cnt = {e: 0 for e in ENGS}
        self.seen = {e: {} for e in ENGS}
        self.esem = {}
        self.dsem = {}
        self.dcnt = {}
        self.dnext = {e: 0 for e in ENGS}
        self.same_engine_sync = {"pe": False, "dve": True, "act": True, "pool": True, "sp": True}
        self.nops = 0

    def sem(self, name):
        return self.stack.enter_context(self.nc.semaphore(name))

    def R(self, obj):
        if isinstance(obj, Res):
            return obj
        return obj.res

    def _esem(self, eng, count):
        ep = (count - 1) // EPOCH
        k = (eng, ep)
        if k not in self.esem:
            self.esem[k] = self.sem(f"e_{eng}_{ep}")
        return self.esem[k], count - ep * EPOCH

    def _add_wait(self, waits, eng, dep):
        key, val = dep
        if key[0] == "eng" and key[1] == eng and not self.same_engine_sync[eng]:
            return
        if self.seen[eng].get(key, 0) >= val:
            return
        if val > waits.get(key, 0):
            waits[key] = val

    def _lower_waits(self, eng, waits):
        wl = []
        for key, val in waits.items():
            self.seen[eng][key] = val
            if key[0] == "eng":
                wl.append(self._esem(key[1], val))
            else:
                wl.append((self.dsem[(key[1], key[2])], val))
        return wl

    def op(self, eng, fn, reads=(), writes=(), dma=False):
        reads = [self.R(r) for r in reads]
        writes = [self.R(w) for w in writes]
        ex = [r for r in reads if r.excl]
        if ex:
            reads = [r for r in reads if not r.excl]
            writes = writes + [r for r in ex if r not in writes]
        waits = {}
        for r in reads:
            if r.last_w is not None:
                self._add_wait(waits, eng, r.last_w)
        for w in writes:
            if w.last_w is not None:
                self._add_wait(waits, eng, w.last_w)
            for rd in w.readers:
                self._add_wait(waits, eng, rd)
        if dma:
            idx = self.dnext[eng]
            self.dnext[eng] = (idx + 1) % NDMASEM
            dk = (eng, idx)
            if dk not in self.dsem:
                self.dsem[dk] = self.sem(f"d_{eng}_{idx}")
                self.dcnt[dk] = 0
            key = ("dma", eng, idx)
            if self.dcnt[dk] > 0:
                self._add_wait(waits, eng, (key, self.dcnt[dk] * 16))
            self.dcnt[dk] += 1
            done = (key, self.dcnt[dk] * 16)
            inc = (self.dsem[dk], 16)
        else:
            self.cnt[eng] += 1
            c = self.cnt[eng]
            done = (("eng", eng), c)
            inc = (self._esem(eng, c)[0], 1)
        wl = self._lower_waits(eng, waits)
        self.q[eng].append((fn, wl, inc))
        self.nops += 1
        for r in reads:
            r.readers.append(done)
            if len(r.readers) > 64:
                best = {}
                for k, v in r.readers:
                    if v > best.get(k, 0):
                        best[k] = v
                r.readers = list(best.items())
        for w in writes:
            w.last_w = done
            w.readers = []
        return done

    def pe(self, fn, reads=(), writes=()):
        return self.op("pe", fn, reads, writes)

    def dve(self, fn, reads=(), writes=()):
        return self.op("dve", fn, reads, writes)

    def act(self, fn, reads=(), writes=()):
        return self.op("act", fn, reads, writes)

    def pool(self, fn, reads=(), writes=()):
        return self.op("pool", fn, reads, writes)

    def dma(self, eng, out, in_, reads=(), writes=(), **kw):
        return self.op(eng, lambda e: e.dma_start(out=out, in_=in_, **kw), reads, writes, dma=True)

    def wait_all(self, eng, deps):
        waits = {}
        for d in deps:
            self._add_wait(waits, eng, d)
        wl = self._lower_waits(eng, waits)
        if wl:
            self.q[eng].append((None, wl, None))

    def barrier(self):
        deps = [(("eng", e), self.cnt[e]) for e in ENGS if self.cnt[e] > 0]
        deps += [(("dma", k[0], k[1]), v * 16) for k, v in self.dcnt.items()]
        for e in ENGS:
            self.wait_all(e, deps)

    def emit(self):
        nc = self.nc
        with nc.Block() as block:
            def run(eng_name):
                def body(eng):
                    for fn, wl, inc in self.q[eng_name]:
                        for sm, v in wl:
                            eng.wait_ge(sm, v)
                        if fn is not None:
                            fn(eng).then_inc(inc[0], inc[1])
                return body
            block.sync(run("sp"))
            block.tensor(run("pe"))
            block.vector(run("dve"))
            block.scalar(run("act"))
            block.gpsimd(run("pool"))


def MM(out, lhsT, rhs, st=True, sp=True):
    return lambda e: e.matmul(out, lhsT, rhs, start=st, stop=sp)


def TR(out, in_, ident):
    return lambda e: e.transpose(out, in_, ident)


def ACTF(out, in_, func, bias=0.0, scale=1.0):
    return lambda e: e.activation(out=out, in_=in_, func=func, bias=bias, scale=scale)


def CP(out, in_):
    return lambda e: e.tensor_copy(out=out, in_=in_)


def ACP(out, in_):
    return lambda e: e.copy(out=out, in_=in_)


def TT(out, a, b, op):
    return lambda e: e.tensor_tensor(out=out, in0=a, in1=b, op=op)


def TS(out, in0, s1, s2=None, op0=ALU.mult, op1=ALU.add):
    if s2 is None:
        return lambda e: e.tensor_scalar(out=out, in0=in0, scalar1=s1, scalar2=None, op0=op0)
    return lambda e: e.tensor_scalar(out=out, in0=in0, scalar1=s1, scalar2=s2, op0=op0, op1=op1)


def STT(out, in0, scalar, in1, op0, op1):
    return lambda e: e.scalar_tensor_tensor(out=out, in0=in0, scalar=scalar, in1=in1, op0=op0, op1=op1)


def MS(ap, v):
    return lambda e: e.memset(ap, v)


def RED(out, in_, op=ALU.add):
    return lambda e: e.tensor_reduce(out=out, in_=in_, axis=AX.X, op=op)


def RCP(out, in_):
    return lambda e: e.reciprocal(out=out, in_=in_)


def SCAN(out, d0, d1):
    return lambda e: e.tensor_tensor_scan(out=out, data0=d0, data1=d1, initial=0.0, op0=ALU.mult, op1=ALU.add)


def BNS(out, in_):
    return lambda e: e.bn_stats(out=out, in_=in_)


def BNA(out, in_):
    return lambda e: e.bn_aggr(out=out, in_=in_)


C_ID = 0
C_BLK = 128
C_HSEL = 256
C_LE = 258
C_LT = 322
C_GT = 386
C_SM = 450
C_N = 450 + 256


def make_consts():
    cst = np.zeros((128, C_N), np.float32)
    p = np.arange(128)[:, None]
    j = np.arange(128)[None, :]
    cst[:, C_ID:C_ID + 128] = (p == j)
    cst[:, C_BLK:C_BLK + 128] = (p // 64 == j // 64)
    cst[:, C_HSEL:C_HSEL + 2] = (p // 64 == np.arange(2)[None, :])
    j64 = np.arange(64)[None, :]
    cst[:, C_LE:C_LE + 64] = ((p % 64) <= j64)
    cst[:, C_LT:C_LT + 64] = ((p % 64) < j64)
    cst[:, C_GT:C_GT + 64] = ((p % 64) > j64)
    cst[:, C_SM:C_SM + 256] = ((np.arange(256)[None, :] % 64) != 0)
    return cst


class Arena:
    def __init__(self, tensor, words):
        self.t = tensor
        self.words = words
        self.off = 0
        self.gen = 0

    def reset(self):
        self.off = 0
        self.gen += 1

    def f32(self, name, cols, parts=128):
        assert self.off + cols <= self.words, (name, self.off, cols, self.words)
        ap = self.t[0:parts, self.off:self.off + cols]
        self.off += (cols + 15) // 16 * 16
        return Tl(ap, f"{name}_{self.gen}")

    def bf(self, name, cols, parts=128):
        w = (cols + 1) // 2
        assert self.off + w <= self.words, (name, self.off, w, self.words)
        ap = self.t[0:parts, self.off:self.off + w].bitcast(BF16)[:, 0:cols]
        self.off += (w + 15) // 16 * 16
        return Tl(ap, f"{name}_{self.gen}")


WEIGHT_NAMES = ["w_ada", "b_ada", "w_in", "hg_lower_bounds", "hg_norm_g", "rw_mu", "rw_w0", "rw_w_up",
                "rw_a0", "rw_a_up", "rw_k_k", "rw_k_a", "rw_r_k", "rw_v0", "rw_v_down", "rw_v_up",
                "rw_gn_g", "rw_gn_b", "w_branch_hg", "w_branch_rw", "w_out", "ln_g", "ln_b"]
WEIGHT_SHAPES = {
    "w_ada": [2, 1024, 3072], "b_ada": [2, 3072], "w_in": [2, 1024, NIN], "hg_lower_bounds": [2, 512],
    "hg_norm_g": [2, 512], "rw_mu": [2, 2176], "rw_w0": [2, 512], "rw_w_up": [2, 64, 512], "rw_a0": [2, 512],
    "rw_a_up": [2, 64, 512], "rw_k_k": [2, 512], "rw_k_a": [2, 512], "rw_r_k": [2, 512], "rw_v0": [1, 512],
    "rw_v_down": [1, 1024, 32], "rw_v_up": [1, 32, 512], "rw_gn_g": [2, 512], "rw_gn_b": [2, 512],
    "w_branch_hg": [2, 512, 1024], "w_branch_rw": [2, 512, 1024], "w_out": [2, 1024, 1024],
    "ln_g": [2, 1024], "ln_b": [2, 1024],
}

ARENA_WORDS = 45500


def build(T, NL=2, dbg=False, stop_after=None):
    assert T % NB == 0
    NBLK = T // NB
    nc = bass.Bass("TRN2", target_bir_lowering=False)
    x_in = nc.dram_tensor("x", [T, D], F32, kind="ExternalInput").ap()
    c_in = nc.dram_tensor("c", [1, D], F32, kind="ExternalInput").ap()
    cst_in = nc.dram_tensor("consts", [128, C_N], F32, kind="ExternalInput").ap()
    W = {n: nc.dram_tensor(n, WEIGHT_SHAPES[n], F32, kind="ExternalInput").ap() for n in WEIGHT_NAMES}
    out = nc.dram_tensor("out", [T, D], F32, kind="ExternalOutput").ap()
    sk = "ExternalOutput" if dbg else "Internal"
    PF = nc.dram_tensor("PF", [26, 128, T], F32, kind=sk).ap()
    PT = nc.dram_tensor("PT", [T, 3072], F32, kind=sk).ap()
    OM = nc.dram_tensor("OM", [T, 1024], F32, kind=sk).ap()
    VF = nc.dram_tensor("VF", [4, 128, T], F32, kind=sk).ap()
    X1 = nc.dram_tensor("X1", [T, D], F32, kind=sk).ap()

    with contextlib.ExitStack() as stack:
        s = Sched(nc, stack)

        def sbt(name, cols, dtype=F32, parts=128):
            t = stack.enter_context(nc.sbuf_tensor(name, [128, cols], dtype))
            return Tl(t[0:parts, :], name)

        arena_t = stack.enter_context(nc.sbuf_tensor("arena", [128, ARENA_WORDS], F32))
        ar = Arena(arena_t, ARENA_WORDS)
        banks = []
        for i in range(8):
            t = stack.enter_context(nc.psum_tensor(f"bank{i}", [128, 512], F32))
            banks.append(Tl(t[:, :], f"bank{i}"))
            banks[-1].res.excl = True
        bank_i = [0]

        def bank():
            b = banks[bank_i[0] % 8]
            bank_i[0] += 1
            return b

        dq = [0]

        def dmaq():
            dq[0] += 1
            return "sp" if dq[0] % 2 else "pool"

        ev = [0]

        def evac(out_ap, in_ap, reads, writes):
            ev[0] += 1
            if ev[0] % 2:
                return s.dve(CP(out_ap, in_ap), reads, writes)
            return s.act(ACP(out_ap, in_ap), reads, writes)

        cst = sbt("cst", C_N)
        s.dma("sp", cst.ap, cst_in, writes=[cst])
        ident = cst.ap[:, C_ID:C_ID + 128]
        id16 = sbt("id16", 128, BF16)
        s.dve(CP(id16.ap, ident), [cst], [id16])
        hsel16 = sbt("hsel16", 2, BF16)
        s.dve(CP(hsel16.ap, cst.ap[:, C_HSEL:C_HSEL + 2]), [cst], [hsel16])
        blk32 = cst.ap[:, C_BLK:C_BLK + 128]
        blk16 = sbt("blk16", 128, BF16)
        s.dve(CP(blk16.ap, cst.ap[:, C_BLK:C_BLK + 128]), [cst], [blk16])
        scanmask = cst.ap[:, C_SM:C_SM + 256]
        hgmask = sbt("hgmask", 256)
        rwmXY = sbt("rwmXY", 512)
        rwmZ = sbt("rwmZ", 256)
        for h in range(4):
            s.pool(CP(hgmask.ap[:, h * 64:(h + 1) * 64], cst.ap[:, C_LE:C_LE + 64]), [cst], [hgmask])
            s.pool(CP(rwmXY.ap[:, h * 128:h * 128 + 64], cst.ap[:, C_LT:C_LT + 64]), [cst], [rwmXY])
            s.pool(CP(rwmXY.ap[:, h * 128 + 64:h * 128 + 128], cst.ap[:, C_LE:C_LE + 64]), [cst], [rwmXY])
            s.pool(CP(rwmZ.ap[:, h * 64:(h + 1) * 64], cst.ap[:, C_GT:C_GT + 64]), [cst], [rwmZ])

        cols = sbt("cols", 96)
        K_SH, K_SC = 0, 8
        K_LB, K_OMLB, K_NOMLB = 16, 20, 24
        K_W0, K_A0, K_KK, K_KA, K_OMKA, K_RK, K_V0 = 28, 32, 36, 40, 44, 48, 52
        K_MU, K_OMU = 56, 73
        tmp_cols = sbt("tmp_cols", 64)
        gate_b = sbt("gate_b", 1024)
        hgg_b = sbt("hgg_b", 512)
        gng_b = sbt("gng_b", 512)
        gnb_b = sbt("gnb_b", 512)
        lng_b = sbt("lng_b", 1024)
        lnb_b = sbt("lnb_b", 1024)
        lora16 = sbt("lora16", 512, BF16)
        vup16 = sbt("vup16", 512, BF16)

        def colload(dst_c, src_ap_1d, n):
            s.dma(dmaq(), cols.ap[:, dst_c:dst_c + n], src_ap_1d.rearrange("(k p) -> p k", p=128),
                  writes=[cols], allow_slow_non_contiguous=True)

        def layer_setup(l):
            ar.reset()
            colload(K_W0, W["rw_w0"][l], 4)
            colload(K_A0, W["rw_a0"][l], 4)
            colload(K_KK, W["rw_k_k"][l], 4)
            colload(K_KA, W["rw_k_a"][l], 4)
            colload(K_RK, W["rw_r_k"][l], 4)
            colload(K_MU, W["rw_mu"][l], 17)
            if l >= 1:
                colload(K_V0, W["rw_v0"][l - 1], 4)
            s.dve(TS(cols.ap[:, K_OMU:K_OMU + 17], cols.ap[:, K_MU:K_MU + 17], -1.0, 1.0), [cols], [cols])
            s.dve(TS(cols.ap[:, K_OMKA:K_OMKA + 4], cols.ap[:, K_KA:K_KA + 4], -1.0, 1.0), [cols], [cols])
            if l == 0:
                s.dve(MS(cols.ap[:, K_LB:K_LB + 4], 0.0), [], [cols])
            else:
                s.dma(dmaq(), tmp_cols.ap[:, 0:4], W["hg_lower_bounds"][0].rearrange("(k p) -> p k", p=128),
                      writes=[tmp_cols], allow_slow_non_contiguous=True)
                s.dma(dmaq(), tmp_cols.ap[:, 4:8], W["hg_lower_bounds"][1].rearrange("(k p) -> p k", p=128),
                      writes=[tmp_cols], allow_slow_non_contiguous=True)
                s.dve(TT(tmp_cols.ap[:, 8:12], tmp_cols.ap[:, 4:8], tmp_cols.ap[:, 0:4], ALU.subtract),
                      [tmp_cols], [tmp_cols])
                s.act(ACTF(cols.ap[:, K_LB:K_LB + 4], tmp_cols.ap[:, 8:12], AF.Sigmoid), [tmp_cols], [cols])
            s.dve(TS(cols.ap[:, K_OMLB:K_OMLB + 4], cols.ap[:, K_LB:K_LB + 4], -1.0, 1.0), [cols], [cols])
            s.dve(TS(cols.ap[:, K_NOMLB:K_NOMLB + 4], cols.ap[:, K_OMLB:K_OMLB + 4], -1.0), [cols], [cols])
            if CUT == 1:
                s.barrier(); return
            s.dma(dmaq(), hgg_b.ap, W["hg_norm_g"][l].partition_broadcast(128), writes=[hgg_b])
            s.dma(dmaq(), gng_b.ap, W["rw_gn_g"][l].partition_broadcast(128), writes=[gng_b])
            s.dma(dmaq(), gnb_b.ap, W["rw_gn_b"][l].partition_broadcast(128), writes=[gnb_b])
            s.dma(dmaq(), lng_b.ap, W["ln_g"][l].partition_broadcast(128), writes=[lng_b])
            s.dma(dmaq(), lnb_b.ap, W["ln_b"][l].partition_broadcast(128), writes=[lnb_b])
            if CUT == 2:
                s.barrier(); return
            st = ar.f32("lst", 512)
            s.dma(dmaq(), st.ap[0:64, :], W["rw_w_up"][l], writes=[st])
            s.dma(dmaq(), st.ap[64:128, :], W["rw_a_up"][l], writes=[st])
            s.dve(CP(lora16.ap, st.ap), [st], [lora16])
            if l >= 1:
                st2 = ar.f32("lst2", 512)
                s.dma(dmaq(), st2.ap[0:32, :], W["rw_v_up"][l - 1], writes=[st2])
                s.dve(CP(vup16.ap[0:32, :], st2.ap[0:32, :]), [st2], [vup16])
            if CUT == 3:
                s.barrier(); return
            ccol = ar.f32("ccol", 8)
            s.dma(dmaq(), ccol.ap, c_in[0].rearrange("(k p) -> p k", p=128), writes=[ccol],
                  allow_slow_non_contiguous=True)
            scol = ar.f32("scol", 8)
            s.act(ACTF(scol.ap, ccol.ap, AF.Silu), [ccol], [scol])
            srep = ar.f32("srep", 8 * 128)
            s.dve(CP(srep.v3(8, 128), scol.ap.unsqueeze(2).to_broadcast([128, 8, 128])), [scol], [srep])
            bcol = ar.f32("bcol", 24)
            s.dma(dmaq(), bcol.ap, W["b_ada"][l].rearrange("(k p) -> p k", p=128), writes=[bcol],
                  allow_slow_non_contiguous=True)
            brow = ar.f32("brow", 3072)
            s.dma(dmaq(), brow.ap, W["b_ada"][l].partition_broadcast(128), writes=[brow])
            crow = ar.f32("crow", 2048)
            ctmp = ar.f32("ctmp", 1024)
            if CUT == 4:
                s.barrier(); return
            wst = [ar.f32(f"adaw{i}", 8 * 512) for i in range(2)]
            for cb in range(6):
                wt = wst[cb % 2]
                s.dma(dmaq(), wt.v3(8, 512),
                      W["w_ada"][l][:, cb * 512:(cb + 1) * 512].rearrange("(k p) c -> p k c", p=128), writes=[wt])
                pb = bank()
                for kc in range(8):
                    s.pe(MM(pb.ap[:, 0:512], srep.v3(8, 128)[:, kc, :], wt.v3(8, 512)[:, kc, :],
                            kc == 0, kc == 7), [wt, srep], [pb])
                if cb < 4:
                    s.dve(TT(crow.ap[:, cb * 512:(cb + 1) * 512], pb.ap[:, 0:512], brow.ap[:, cb * 512:(cb + 1) * 512],
                             ALU.add), [pb, brow], [crow])
                else:
                    g0 = (cb - 4) * 512
                    s.dve(TT(gate_b.ap[:, g0:g0 + 512], pb.ap[:, 0:512], brow.ap[:, cb * 512:(cb + 1) * 512], ALU.add),
                          [pb, brow], [gate_b])
            for which in range(2):
                s.dve(TT(ctmp.v3(8, 128), crow.ap[:, which * 1024:(which + 1) * 1024].rearrange("p (k c) -> p k c", k=8),
                         ident.unsqueeze(1).to_broadcast([128, 8, 128]), ALU.mult), [crow, cst], [ctmp])
                dstc = K_SH if which == 0 else K_SC
                s.dve(RED(cols.ap[:, dstc:dstc + 8], ctmp.v3(8, 128)), [ctmp], [cols])
            s.dve(TS(cols.ap[:, K_SC:K_SC + 8], cols.ap[:, K_SC:K_SC + 8], 1.0, None, ALU.add), [cols], [cols])
            s.dve(TS(gate_b.ap, gate_b.ap, 1.0, None, ALU.add), [gate_b], [gate_b])
            s.barrier()

        def fmcol(ci):
            if ci < 4:
                return ci * 128
            if ci < 8:
                return 512 + (ci - 4) * 128
            return 2048 + (ci - 8) * 128

        def tmcol(g):
            return (1024, 1536, 4224, 4736, 5248, 5760)[g]

        def pass_A(l, xin):
            ar.reset()
            w16 = ar.bf("w16", 8 * NIN)
            w16v = w16.v3(8, NIN)
            hT = [ar.bf(f"hT{i}", 8 * NB) for i in range(2)]
            xt = [ar.f32(f"xt{i}", 1024) for i in range(2)]
            wst = [ar.f32(f"wst{i}", 1568) for i in range(3)]
            fst = [ar.f32(f"fst{i}", 512) for i in range(3)]
            tst = [ar.f32(f"tst{i}", 3072) for i in range(2)]
            vd16 = ar.bf("vd16", 8 * 32) if l >= 1 else None
            n = 0
            for kc in range(8):
                for qd in range(4):
                    st = wst[n % 3]
                    s.dma(dmaq(), st.ap, W["w_in"][l][kc * 128:(kc + 1) * 128, qd * 1568:(qd + 1) * 1568],
                          writes=[st])
                    dst = w16v[:, kc, qd * 1568:(qd + 1) * 1568]
                    eng = ("dve", "act", "pool")[n % 3]
                    if eng == "act":
                        s.act(ACP(dst, st.ap), [st], [w16.r((kc, qd))])
                    else:
                        s.op(eng, CP(dst, st.ap), [st], [w16.r((kc, qd))])
                    n += 1
            if l >= 1:
                vst = ar.f32("vst", 8 * 32)
                s.dma(dmaq(), vst.v3(8, 32), W["rw_v_down"][l - 1].rearrange("(k p) c -> p k c", p=128),
                      writes=[vst])
                s.dve(CP(vd16.ap, vst.ap), [vst], [vd16])

            def wres(kc, c0, c1):
                return [w16.r((kc, q)) for q in range(c0 // 1568, (c1 - 1) // 1568 + 1)]

            nfm = 25
            fi = 0
            for blk in range(NBLK):
                t0 = blk * NB
                h = hT[blk % 2]
                hv = h.v3(8, NB)
                for tt in range(2):
                    xtile = xt[tt]
                    s.dma("sp", xtile.ap, xin[t0 + tt * 128:t0 + (tt + 1) * 128, :], writes=[xtile])
                    for half in range(2):
                        pb = bank()
                        for j in range(4):
                            kc = half * 4 + j
                            s.pe(TR(pb.ap[:, j * 128:(j + 1) * 128], xtile.ap[:, kc * 128:(kc + 1) * 128], ident),
                                 [xtile, cst], [pb])
                        for j in range(4):
                            kc = half * 4 + j
                            s.act(ACTF(hv[:, kc, tt * 128:(tt + 1) * 128], pb.ap[:, j * 128:(j + 1) * 128],
                                       AF.Identity, bias=cols.ap[:, K_SH + kc:K_SH + kc + 1],
                                       scale=cols.ap[:, K_SC + kc:K_SC + kc + 1]), [pb, cols], [h])
                for pr in range((nfm + 1) // 2):
                    cis = [ci for ci in (2 * pr, 2 * pr + 1) if ci < nfm]
                    pb = bank()
                    for j, ci in enumerate(cis):
                        c0 = fmcol(ci)
                        for kc in range(8):
                            s.pe(MM(pb.ap[:, j * NB:(j + 1) * NB], w16v[:, kc, c0:c0 + 128], hv[:, kc, :],
                                    kc == 0, kc == 7), wres(kc, c0, c0 + 128) + [h], [pb])
                    st = fst[fi % 3]
                    fi += 1
                    nn = len(cis) * NB
                    evac(st.ap[:, 0:nn], pb.ap[:, 0:nn], [pb], [st])
                    s.dma(dmaq(), PF[cis[0]:cis[0] + len(cis), :, t0:t0 + NB].rearrange("c p t -> p c t"),
                          st.ap[:, 0:nn].rearrange("p (c t) -> p c t", c=len(cis)), reads=[st])
                if l >= 1:
                    pb = bank()
                    for kc in range(8):
                        s.pe(MM(pb.ap[0:32, 0:NB], vd16.v3(8, 32)[:, kc, :], hv[:, kc, :], kc == 0, kc == 7),
                             [vd16, h], [pb])
                    st = fst[fi % 3]
                    fi += 1
                    evac(st.ap[0:32, 0:NB], pb.ap[0:32, 0:NB], [pb], [st])
                    s.dma(dmaq(), PF[25, 0:32, t0:t0 + NB], st.ap[0:32, 0:NB], reads=[st])
                for tt in range(2):
                    ts_ = tst[tt]
                    for g in range(6):
                        pb = bank()
                        c0 = tmcol(g)
                        for kc in range(8):
                            s.pe(MM(pb.ap[:, 0:512], hv[:, kc, tt * 128:(tt + 1) * 128], w16v[:, kc, c0:c0 + 512],
                                    kc == 0, kc == 7), wres(kc, c0, c0 + 512) + [h], [pb])
                        evac(ts_.ap[:, g * 512:(g + 1) * 512], pb.ap[:, 0:512], [pb], [ts_.r(g)])
                    s.dma(dmaq(), PT[t0 + tt * 128:t0 + (tt + 1) * 128, :], ts_.ap,
                          reads=[ts_.r(g) for g in range(6)], writes=[ts_.r(g) for g in range(0)])
            s.barrier()

        def pass_B1(l):
            ar.reset()
            NCH = NB // CH
            S32 = [ar.f32(f"S32_{h}", 128) for h in range(4)]
            S16 = [ar.bf(f"S16_{h}", 128) for h in range(4)]
            for h in range(4):
                s.dve(MS(S32[h].ap, 0.0), [], [S32[h]])
                s.dve(MS(S16[h].ap, 0.0), [], [S16[h]])
            qf = [ar.f32(f"qf{i}", 8 * NB) for i in range(2)]
            tmv = [[ar.f32(f"tmv{i}_{tt}", 1024) for tt in range(2)] for i in range(2)]
            sig_ = [ar.f32(f"sig{i}", 4 * NB) for i in range(2)]
            fv_ = [ar.f32(f"fv{i}", 4 * NB) for i in range(2)]
            lgf_ = [ar.f32(f"lgf{i}", 4 * NB) for i in range(2)]
            khg_ = [ar.f32(f"khg{i}", 4 * NB) for i in range(2)]
            bcs_ = [ar.f32(f"bcs{i}", 4 * NB) for i in range(2)]
            eb_ = [ar.f32(f"eb{i}", 4 * NB) for i in range(2)]
            enb_ = [ar.f32(f"enb{i}", 4 * NB) for i in range(2)]
            kt32_ = [ar.f32(f"kt32{i}", 4 * NB) for i in range(2)]
            q16_ = [ar.bf(f"q16{i}", 4 * NB) for i in range(2)]
            k16_ = [ar.bf(f"k16{i}", 4 * NB) for i in range(2)]
            kh16_ = [ar.bf(f"kh16{i}", 4 * NB) for i in range(2)]
            v16 = [ar.bf(f"v16_{tt}", 512) for tt in range(2)]
            sT16 = [ar.bf(f"sT16_{tt}", 256) for tt in range(2)]
            khtm = [ar.bf(f"khtm_{tt}", 512) for tt in range(2)]
            osb_ = [ar.f32(f"osb{i}", 512) for i in range(2)]
            osq_ = [ar.f32(f"osq{i}", 512) for i in range(2)]
            ss_ = [ar.f32(f"ss{i}", 8) for i in range(2)]
            zs_ = [ar.f32(f"zs{i}", 512) for i in range(2)]
            ofin = [ar.f32(f"ofin{i}", 512) for i in range(2)]
            for blk in range(NBLK):
                t0 = blk * NB
                sig = sig_[blk % 2]
                fv = fv_[blk % 2]
                lgf = lgf_[blk % 2]
                khg = khg_[blk % 2]
                bcs = bcs_[blk % 2]
                eb = eb_[blk % 2]
                enb = enb_[blk % 2]
                kt32 = kt32_[blk % 2]
                q16 = q16_[blk % 2]
                k16 = k16_[blk % 2]
                kh16 = kh16_[blk % 2]
                qt = qf[blk % 2]
                qv = qt.v3(8, NB)
                s.dma("sp", qv, PF[0:8, :, t0:t0 + NB].rearrange("c p t -> p c t"), writes=[qt])
                tm = tmv[blk % 2]
                for tt in range(2):
                    s.dma("pool", tm[tt].ap, PT[t0 + tt * 128:t0 + (tt + 1) * 128, 0:1024], writes=[tm[tt]])
                s.act(ACTF(sig.ap, qt.ap[:, 4 * NB:8 * NB], AF.Sigmoid), [qt], [sig])
                for h in range(4):
                    sl = slice(h * NB, (h + 1) * NB)
                    s.dve(TS(fv.ap[:, sl], sig.ap[:, sl], cols.ap[:, K_OMLB + h:K_OMLB + h + 1],
                             cols.ap[:, K_LB + h:K_LB + h + 1]), [sig, cols], [fv])
                    s.pool(TS(khg.ap[:, sl], sig.ap[:, sl], cols.ap[:, K_NOMLB + h:K_NOMLB + h + 1],
                              cols.ap[:, K_OMLB + h:K_OMLB + h + 1]), [sig, cols], [khg])
                s.act(ACTF(lgf.ap, fv.ap, AF.Ln), [fv], [lgf])
                for h in range(4):
                    sl = slice(h * NB, (h + 1) * NB)
                    s.dve(SCAN(bcs.ap[:, sl], scanmask, lgf.ap[:, sl]), [lgf, cst], [bcs])
                s.act(ACTF(eb.ap, bcs.ap, AF.Exp), [bcs], [eb])
                s.act(ACTF(enb.ap, bcs.ap, AF.Exp, scale=-1.0), [bcs], [enb])
                s.dve(TT(q16.ap, qt.ap[:, 0:4 * NB], eb.ap, ALU.mult), [qt, eb], [q16])
                s.dve(TT(kt32.ap, khg.ap, enb.ap, ALU.mult), [khg, enb], [kt32])
                s.act(ACP(k16.ap, kt32.ap), [kt32], [k16])
                e4 = eb.v4(4, NCH, CH)
                s.dve(TT(kh16.v4(4, NCH, CH), kt32.v4(4, NCH, CH), e4[:, :, :, CH - 1:CH].to_broadcast([128, 4, NCH, CH]),
                         ALU.mult), [kt32, eb], [kh16])
                for tt in range(2):
                    osb, osq, ss, zs = osb_[tt], osq_[tt], ss_[tt], zs_[tt]
                    vt = tm[tt]
                    s.act(ACP(v16[tt].ap, vt.ap[:, 0:512]), [vt], [v16[tt]])
                    pbs = bank()
                    pbk = bank()
                    pbk16 = pbk.ap.bitcast(BF16)
                    for par in range(2):
                        cc = tt * 2 + par
                        ps_ = slice(par * 64, (par + 1) * 64)
                        for h in range(4):
                            c0 = h * NB + cc * CH
                            s.pe(MM(pbs.ap[ps_, h * 64:(h + 1) * 64], k16.ap[:, c0:c0 + CH], q16.ap[:, c0:c0 + CH]),
                                 [k16, q16], [pbs])
                            s.pe(TR(pbk16[ps_, h * 128:(h + 1) * 128], kh16.ap[:, c0:c0 + CH], id16.ap),
                                 [kh16, id16], [pbk])
                    s.dve(TT(sT16[tt].ap, pbs.ap[:, 0:256], hgmask.ap, ALU.mult), [pbs, hgmask], [sT16[tt]])
                    s.act(ACP(khtm[tt].ap, pbk16[:, 0:512]), [pbk], [khtm[tt]])
                    pbo = bank()
                    for par in range(2):
                        cc = tt * 2 + par
                        ps_ = slice(par * 64, (par + 1) * 64)
                        for h in range(4):
                            c0 = h * NB + cc * CH
                            hs = slice(h * 128, (h + 1) * 128)
                            s.pe(MM(pbo.ap[ps_, hs], sT16[tt].ap[ps_, h * 64:(h + 1) * 64], v16[tt].ap[ps_, hs],
                                    True, False), [sT16[tt], v16[tt]], [pbo])
                            s.pe(MM(pbo.ap[ps_, hs], q16.ap[:, c0:c0 + CH], S16[h].ap, False, True),
                                 [q16, S16[h]], [pbo])
                        pkv = bank()
                        for h in range(4):
                            hs = slice(h * 128, (h + 1) * 128)
                            s.pe(MM(pkv.ap[:, hs], khtm[tt].ap[ps_, hs], v16[tt].ap[ps_, hs]),
                                 [khtm[tt], v16[tt]], [pkv])
                        for h in range(4):
                            c0 = h * NB + cc * CH
                            hs = slice(h * 128, (h + 1) * 128)
                            s.dve(STT(S32[h].ap, S32[h].ap, eb.ap[:, c0 + CH - 1:c0 + CH], pkv.ap[:, hs],
                                      ALU.mult, ALU.add), [S32[h], eb, pkv], [S32[h]])
                            s.act(ACP(S16[h].ap, S32[h].ap), [S32[h]], [S16[h]])
                    s.act(ACP(osb.ap, pbo.ap[:, 0:512]), [pbo], [osb])
                    s.pool(TT(osq.ap, osb.ap, osb.ap, ALU.mult), [osb], [osq])
                    s.dve(RED(ss.ap[:, 0:4], osq.v3(4, 128)), [osq], [ss])
                    s.dve(TS(ss.ap[:, 0:4], ss.ap[:, 0:4], 1.0 / 128.0, RMS_EPS), [ss], [ss])
                    s.act(ACTF(ss.ap[:, 0:4], ss.ap[:, 0:4], AF.Sqrt), [ss], [ss])
                    s.dve(RCP(ss.ap[:, 4:8], ss.ap[:, 0:4]), [ss], [ss])
                    s.act(ACTF(zs.ap, vt.ap[:, 512:1024], AF.Silu), [vt], [zs])
                    s.pool(TT(zs.ap, zs.ap, hgg_b.ap, ALU.mult), [zs, hgg_b], [zs])
                    of = ofin[tt]
                    s.dve(TT(of.v3(4, 128), osb.v3(4, 128), ss.ap[:, 4:8].unsqueeze(2).to_broadcast([128, 4, 128]),
                             ALU.mult), [osb, ss], [of])
                    s.dve(TT(of.ap, of.ap, zs.ap, ALU.mult), [of, zs], [of])
                    s.dma("sp", OM[t0 + tt * 128:t0 + (tt + 1) * 128, 0:512], of.ap, reads=[of])
            s.barrier()

        def pass_B2(l):
            ar.reset()
            NCH = NB // CH
            W4 = 4 * NB
            H32 = ar.f32("H32", 512)
            H16 = ar.bf("H16", 512)
            s.dve(MS(H32.ap, 0.0), [], [H32])
            s.dve(MS(H16.ap, 0.0), [], [H16])
            rwl = [ar.f32(f"rwl{i}", 17 * (NB + 1)) for i in range(1)]
            sh = ar.f32("sh", 17 * NB)
            shv = sh.v3(17, NB)
            lw = ar.f32("lw", W4)
            av = ar.f32("av", W4)
            bcs = ar.f32("bcs", W4)
            t1 = ar.f32("t1", W4)
            t2 = ar.f32("t2", W4)
            t3 = ar.f32("t3", W4)
            eb = ar.f32("eb", W4)
            enb = ar.f32("enb", W4)
            ebx = ar.f32("ebx", W4)
            egb = ar.f32("egb", W4)
            kkn = ar.f32("kkn", W4)
            kmod = ar.f32("kmod", W4)
            uu = ar.f32("uu", W4)
            PR16 = ar.bf("PR16", 2 * W4)
            PT16 = Tl(PR16.ap[:, 0:W4], "PT16v")
            RT16 = Tl(PR16.ap[:, W4:2 * W4], "RT16v")
            UT16 = ar.bf("UT16", W4)
            KT16 = ar.bf("KT16", W4)
            UTz = [ar.bf(f"UTz{i}", W4) for i in range(2)]
            KTz = [ar.bf(f"KTz{i}", W4) for i in range(2)]
            PTz = [ar.bf(f"PTz{i}", W4) for i in range(2)]
            for zt in UTz + KTz + PTz:
                s.pool(MS(zt.ap, 0.0), [], [zt])
            UG16 = ar.bf("UG16", W4)
            KG16 = ar.bf("KG16", W4)
            rk16 = ar.bf("rk16", W4)
            lo16 = ar.bf("lo16", NB)
            vr16 = ar.bf("vr16", NB) if l >= 1 else None
            vrs = ar.f32("vrs", NB) if l >= 1 else None
            vfl = ar.f32("vfl", W4) if l >= 1 else None
            vtm32 = ar.f32("vtm32", 512)
            vtm16 = ar.bf("vtm16", 512)
            zsl_ = [ar.f32(f"zsl{i}", 512) for i in range(2)]
            bon_ = [ar.f32(f"bon{i}", 512) for i in range(2)]
            AXY = [ar.bf(f"AXY{g}", 1024) for g in range(2)]
            RT32 = ar.f32("RT32", W4)
            XY = [[ar.bf(f"XY{g}_{i}", 512) for i in range(2)] for g in range(2)]
            RR = [[ar.bf(f"RR{g}_{i}", 512) for i in range(2)] for g in range(2)]
            QT16_ = [ar.bf(f"QT16{i}", 512) for i in range(2)]
            MT16_ = [ar.bf(f"MT16{i}", 512) for i in range(2)]
            UKG = [ar.bf(f"UKG{g}", 512) for g in range(2)]
            YL32_ = [ar.f32(f"YL32{i}", 512) for i in range(2)]
            N32_ = [ar.f32(f"N32{i}", 512) for i in range(2)]
            Ht = ar.f32("Ht", 256)
            ysb_ = [ar.f32(f"ysb{i}", 512) for i in range(2)]
            yc_ = [ar.f32(f"yc{i}", 512) for i in range(2)]
            ysq_ = [ar.f32("ysq", 512)] * 2
            st8_ = [ar.f32(f"st8{i}", 32) for i in range(2)]
            yfin = [ar.f32(f"yfin{i}", 512) for i in range(2)]
            cvf = cols.ap
            for blk in range(NBLK):
                t0 = blk * NB
                rl = rwl[0]
                rlv = rl.v3(17, NB + 1)
                if t0 == 0:
                    s.dve(MS(rlv[:, :, 0:1], 0.0), [], [rl])
                    s.dma("sp", rlv[:, :, 1:NB + 1], PF[8:25, :, 0:NB].rearrange("c p t -> p c t"), writes=[rl])
                else:
                    s.dma("sp", rlv, PF[8:25, :, t0 - 1:t0 + NB].rearrange("c p t -> p c t"), writes=[rl])
                for kc in range(17):
                    s.act(ACTF(shv[:, kc, :], rlv[:, kc, 1:NB + 1], AF.Identity,
                               scale=cvf[:, K_OMU + kc:K_OMU + kc + 1]), [rl, cols], [sh.r(kc)])
                    s.dve(STT(shv[:, kc, :], rlv[:, kc, 0:NB], cvf[:, K_MU + kc:K_MU + kc + 1], shv[:, kc, :],
                              ALU.mult, ALU.add), [rl, cols, sh.r(kc)], [sh.r(kc)])
                rsl = lambda kc: shv[:, kc, :]
                R_ = sh.ap[:, 0:W4]
                K_ = sh.ap[:, W4:2 * W4]
                V_ = sh.ap[:, 2 * W4:3 * W4]
                Z_ = sh.ap[:, 3 * W4:4 * W4]
                rR = [sh.r(kc) for kc in range(0, 4)]
                rK = [sh.r(kc) for kc in range(4, 8)]
                rV = [sh.r(kc) for kc in range(8, 12)]
                rZ = [sh.r(kc) for kc in range(12, 16)]
                s.act(ACTF(lo16.ap[0:64, :], shv[0:64, 16, :], AF.Tanh), [sh.r(16)], [lo16])
                s.dve(CP(lo16.ap[64:128, :], shv[64:128, 16, :]), [sh.r(16)], [lo16])
                pw = [bank(), bank()]
                pa = [bank(), bank()]
                for kc in range(4):
                    s.pe(MM(pw[kc // 2].ap[:, (kc % 2) * NB:(kc % 2 + 1) * NB], lora16.ap[0:64, kc * 128:(kc + 1) * 128],
                            lo16.ap[0:64, :]), [lora16, lo16], [pw[kc // 2]])
                    s.pe(MM(pa[kc // 2].ap[:, (kc % 2) * NB:(kc % 2 + 1) * NB], lora16.ap[64:128, kc * 128:(kc + 1) * 128],
                            lo16.ap[64:128, :]), [lora16, lo16], [pa[kc // 2]])
                for kc in range(4):
                    sl = slice(kc * NB, (kc + 1) * NB)
                    psl = slice((kc % 2) * NB, (kc % 2 + 1) * NB)
                    s.act(ACTF(lw.ap[:, sl], pw[kc // 2].ap[:, psl], AF.Sigmoid, bias=cvf[:, K_W0 + kc:K_W0 + kc + 1]),
                          [pw[kc // 2], cols], [lw])
                    s.act(ACTF(av.ap[:, sl], pa[kc // 2].ap[:, psl], AF.Sigmoid, bias=cvf[:, K_A0 + kc:K_A0 + kc + 1]),
                          [pa[kc // 2], cols], [av])
                s.dve(TS(lw.ap, lw.ap, -math.exp(-0.5), None, ALU.mult), [lw], [lw])
                if l == 0:
                    s.dma("pool", VF[:, :, t0:t0 + NB].rearrange("c p t -> p c t"), V_.rearrange("p (c t) -> p c t", c=4),
                          reads=rV)
                else:
                    s.dma("pool", vrs.ap[0:32, :], PF[25, 0:32, t0:t0 + NB], writes=[vrs])
                    s.dma("pool", vfl.v3(4, NB), VF[:, :, t0:t0 + NB].rearrange("c p t -> p c t"), writes=[vfl])
                    s.dve(CP(vr16.ap[0:32, :], vrs.ap[0:32, :]), [vrs], [vr16])
                    pv = [bank(), bank()]
                    for kc in range(4):
                        s.pe(MM(pv[kc // 2].ap[:, (kc % 2) * NB:(kc % 2 + 1) * NB], vup16.ap[0:32, kc * 128:(kc + 1) * 128],
                                vr16.ap[0:32, :]), [vup16, vr16], [pv[kc // 2]])
                    for kc in range(4):
                        sl = slice(kc * NB, (kc + 1) * NB)
                        psl = slice((kc % 2) * NB, (kc % 2 + 1) * NB)
                        s.act(ACTF(t1.ap[:, sl], pv[kc // 2].ap[:, psl], AF.Sigmoid,
                                   bias=cvf[:, K_V0 + kc:K_V0 + kc + 1]), [pv[kc // 2], cols], [t1])
                    s.pool(TT(vfl.ap, vfl.ap, V_, ALU.subtract), [vfl] + rV, [vfl])
                    s.dve(TT(vfl.ap, vfl.ap, t1.ap, ALU.mult), [vfl, t1], [vfl])
                    s.pool(TT(V_, V_, vfl.ap, ALU.add), [vfl] + rV, rV)
                for kc in range(4):
                    sl = slice(kc * NB, (kc + 1) * NB)
                    s.dve(SCAN(bcs.ap[:, sl], scanmask, lw.ap[:, sl]), [lw, cst], [bcs])
                s.dve(TT(t2.ap, bcs.ap, lw.ap, ALU.subtract), [bcs, lw], [t2])
                b4 = bcs.v4(4, NCH, CH)
                s.dve(TT(t3.v4(4, NCH, CH), b4[:, :, :, CH - 1:CH].to_broadcast([128, 4, NCH, CH]), b4, ALU.subtract),
                      [bcs], [t3])
                s.act(ACTF(eb.ap, bcs.ap, AF.Exp), [bcs], [eb])
                s.act(ACTF(enb.ap, bcs.ap, AF.Exp, scale=-1.0), [bcs], [enb])
                s.act(ACTF(ebx.ap, t2.ap, AF.Exp), [t2], [ebx])
                s.act(ACTF(egb.ap, t3.ap, AF.Exp), [t3], [egb])
                for kc in range(4):
                    sl = slice(kc * NB, (kc + 1) * NB)
                    s.act(ACTF(t1.ap[:, sl], K_[:, sl], AF.Square, scale=cvf[:, K_KK + kc:K_KK + kc + 1]),
                          [rK[kc], cols], [t1])
                pq = [bank(), bank()]
                for kc in range(4):
                    s.pe(MM(pq[kc // 2].ap[:, (kc % 2) * NB:(kc % 2 + 1) * NB], blk32, t1.ap[:, kc * NB:(kc + 1) * NB]),
                         [cst, t1], [pq[kc // 2]])
                for hf in range(2):
                    sl = slice(hf * 2 * NB, (hf + 1) * 2 * NB)
                    s.dve(TS(t2.ap[:, sl], pq[hf].ap[:, 0:2 * NB], 1e-24, None, ALU.max), [pq[hf]], [t2])
                s.act(ACTF(t2.ap, t2.ap, AF.Ln), [t2], [t2])
                s.act(ACTF(t3.ap, t2.ap, AF.Exp, scale=-0.5), [t2], [t3])
                for kc in range(4):
                    sl = slice(kc * NB, (kc + 1) * NB)
                    s.dve(STT(kkn.ap[:, sl], K_[:, sl], cvf[:, K_KK + kc:K_KK + kc + 1], t3.ap[:, sl], ALU.mult, ALU.mult),
                          [rK[kc], cols, t3], [kkn])
                    s.pool(TS(t1.ap[:, sl], av.ap[:, sl], cvf[:, K_KA + kc:K_KA + kc + 1],
                              cvf[:, K_OMKA + kc:K_OMKA + kc + 1]), [av, cols], [t1])
                s.dve(TT(kmod.ap, K_, t1.ap, ALU.mult), rK + [t1], [kmod])
                s.dve(TT(uu.ap, av.ap, kkn.ap, ALU.mult), [av, kkn], [uu])
                s.dve(STT(PT16.ap, kkn.ap, -1.0, ebx.ap, ALU.mult, ALU.mult), [kkn, ebx], [PT16])
                for hh_ in range(2):
                    hp_ = slice(hh_ * 64, (hh_ + 1) * 64)
                    s.dve(TT(UTz[hh_].ap[hp_, :], uu.ap[hp_, :], enb.ap[hp_, :], ALU.mult), [uu, enb], [UTz[hh_]])
                    s.dve(TT(KTz[hh_].ap[hp_, :], kmod.ap[hp_, :], enb.ap[hp_, :], ALU.mult), [kmod, enb], [KTz[hh_]])
                    s.act(ACP(PTz[hh_].ap[hp_, :], PT16.ap[hp_, :]), [PT16], [PTz[hh_]])
                s.pool(TT(RT32.ap, R_, eb.ap, ALU.mult), rR + [eb], [RT32])
                s.act(ACP(RT16.ap, RT32.ap), [RT32], [RT16])
                s.dve(TT(UG16.ap, uu.ap, egb.ap, ALU.mult), [uu, egb], [UG16])
                s.dve(TT(KG16.ap, kmod.ap, egb.ap, ALU.mult), [kmod, egb], [KG16])
                s.dve(TT(t2.ap, R_, kmod.ap, ALU.mult), rR + [kmod], [t2])
                for kc in range(4):
                    sl = slice(kc * NB, (kc + 1) * NB)
                    s.act(ACTF(rk16.ap[:, sl], t2.ap[:, sl], AF.Identity, scale=cvf[:, K_RK + kc:K_RK + kc + 1]),
                          [t2, cols], [rk16])

                if CUT == 10:
                    s.barrier(); return
                for tt in range(2):
                    zsl, bon, QT16, MT16, YL32 = zsl_[tt], bon_[tt], QT16_[tt], MT16_[tt], YL32_[tt]
                    N32, ysb, yc, ysq, st8 = N32_[tt], ysb_[tt], yc_[tt], ysq_[tt], st8_[tt]
                    tsl = lambda kc: slice(kc * NB + tt * 128, kc * NB + (tt + 1) * 128)
                    pvt = bank()
                    pzt = bank()
                    for kc in range(4):
                        s.pe(TR(pvt.ap[:, kc * 128:(kc + 1) * 128], V_[:, tsl(kc)], ident), rV + [cst], [pvt])
                    for kc in range(4):
                        s.pe(TR(pzt.ap[:, kc * 128:(kc + 1) * 128], Z_[:, tsl(kc)], ident), rZ + [cst], [pzt])
                    if CUT == 111:
                        s.barrier(); return
                    s.dve(CP(vtm32.ap, pvt.ap[:, 0:512]), [pvt], [vtm32])
                    s.act(ACP(vtm16.ap, pvt.ap[:, 0:512]), [pvt], [vtm16])
                    s.act(ACTF(zsl.ap, pzt.ap[:, 0:512], AF.Silu), [pzt], [zsl])
                    if CUT == 112:
                        s.barrier(); return
                    pst = bank()
                    for kc in range(4):
                        s.pe(MM(pst.ap[:, kc * 128:(kc + 1) * 128], rk16.ap[:, tsl(kc)], blk16.ap), [rk16, blk16], [pst])
                    s.dve(TT(bon.ap, pst.ap[:, 0:512], vtm32.ap, ALU.mult), [pst, vtm32], [bon])

                    if CUT == 11:
                        s.barrier(); return
                    def fm(tile16, h, par):
                        kc, hh = h // 2, h % 2
                        c0 = kc * NB + tt * 128 + par * 64
                        return tile16.ap[hh * 64:(hh + 1) * 64, c0:c0 + 64]

                    def ch(tile16, h, par):
                        kc = h // 2
                        c0 = kc * NB + tt * 128 + par * 64
                        return tile16.ap[:, c0:c0 + 64]

                    ids = lambda hh: id16.ap[:, hh * 64:(hh + 1) * 64]
                    PRv = PR16.ap.rearrange("p (two w) -> p two w", two=2)
                    G = [dict(), dict()]
                    rd = [PT16, RT16] + UTz + KTz + PTz
                    for g in range(2):
                        pX, pY, pZ = bank(), bank(), bank()
                        for par in range(2):
                            ps_ = slice(par * 64, (par + 1) * 64)
                            for hi in range(4):
                                h = g * 4 + hi
                                hh = h % 2
                                c0 = (h // 2) * NB + tt * 128 + par * 64
                                pr = PRv[:, :, c0:c0 + 64]
                                ox = pX.ap[ps_, hi * 128:(hi + 1) * 128].rearrange("p (a b) -> p a b", a=2)
                                oy = pY.ap[ps_, hi * 128:(hi + 1) * 128].rearrange("p (a b) -> p a b", a=2)
                                if MERGE_PR:
                                    s.pe(MM(ox, ch(UTz[hh], h, par), pr), rd, [pX])
                                    s.pe(MM(oy, ch(KTz[hh], h, par), pr), rd, [pY])
                                else:
                                    s.pe(MM(pX.ap[ps_, hi * 128:hi * 128 + 64], ch(UTz[hh], h, par), ch(PT16, h, par)), rd, [pX])
                                    s.pe(MM(pX.ap[ps_, hi * 128 + 64:hi * 128 + 128], ch(UTz[hh], h, par), ch(RT16, h, par)), rd, [pX])
                                    s.pe(MM(pY.ap[ps_, hi * 128:hi * 128 + 64], ch(KTz[hh], h, par), ch(PT16, h, par)), rd, [pY])
                                    s.pe(MM(pY.ap[ps_, hi * 128 + 64:hi * 128 + 128], ch(KTz[hh], h, par), ch(RT16, h, par)), rd, [pY])
                                s.pe(MM(pZ.ap[ps_, hi * 64:(hi + 1) * 64], ch(PTz[hh], h, par), ch(UTz[hh], h, par)), rd, [pZ])
                        G[g]["pA"] = (pX, pY, pZ)
                    for g in range(2):
                        pX, pY, pZ = G[g]["pA"]
                        A = AXY[g]
                        s.dve(TT(A.ap[:, 0:512], pX.ap[:, 0:512], rwmXY.ap, ALU.mult), [pX, rwmXY], [A.r(0)])
                        s.dve(TT(A.ap[:, 512:1024], pY.ap[:, 0:512], rwmXY.ap, ALU.mult), [pY, rwmXY], [A.r(1)])
                        xy = XY[g][0]
                        s.dve(TT(xy.v3(4, 128)[:, :, 0:64], pZ.ap[:, 0:256].rearrange("p (h c) -> p h c", h=4),
                                 rwmZ.v3(4, 64), ALU.mult), [pZ, rwmZ], [xy.r(0)])
                        s.act(ACP(xy.v3(4, 128)[:, :, 64:128], A.ap[:, 0:512].rearrange("p (h c) -> p h c", h=4)[:, :, 0:64]),
                              [A.r(0)], [xy.r(1)])
                    for g in range(2):
                        A = AXY[g]
                        pR = bank()
                        for par in range(2):
                            ps_ = slice(par * 64, (par + 1) * 64)
                            for hi in range(4):
                                h = g * 4 + hi
                                s.pe(MM(pR.ap[ps_, hi * 128:hi * 128 + 64], ch(PT16, h, par), ids(h % 2)), [PT16, id16], [pR])
                                s.pe(MM(pR.ap[ps_, hi * 128 + 64:hi * 128 + 128],
                                        A.ap[ps_, 512 + hi * 128:512 + hi * 128 + 64],
                                        vtm16.ap[ps_, h * 64:(h + 1) * 64]), [A.r(1), vtm16], [pR])
                        G[g]["pR"] = pR
                    for g in range(2):
                        evac(RR[g][0].ap, G[g]["pR"].ap[:, 0:512], [G[g]["pR"]], [RR[g][0]])
                    cur = 0
                    for j in range(6):
                        for g in range(2):
                            xy = XY[g][cur]
                            rr = RR[g][cur]
                            pR = bank()
                            for par in range(2):
                                ps_ = slice(par * 64, (par + 1) * 64)
                                for hi in range(4):
                                    s.pe(MM(pR.ap[ps_, hi * 128:(hi + 1) * 128], xy.ap[ps_, hi * 128 + 64:hi * 128 + 128],
                                            rr.ap[ps_, hi * 128:(hi + 1) * 128]), [xy.r(1), rr], [pR])
                            G[g]["pR"] = pR
                            if j < 5:
                                pS = bank()
                                for par in range(2):
                                    ps_ = slice(par * 64, (par + 1) * 64)
                                    for hi in range(4):
                                        X_ = xy.ap[ps_, hi * 128:hi * 128 + 64]
                                        Y_ = xy.ap[ps_, hi * 128 + 64:hi * 128 + 128]
                                        s.pe(MM(pS.ap[ps_, hi * 128:hi * 128 + 64], Y_, X_), [xy.r(0), xy.r(1)], [pS])
                                        s.pe(MM(pS.ap[ps_, hi * 128 + 64:hi * 128 + 128], X_, Y_), [xy.r(0), xy.r(1)], [pS])
                                G[g]["pS"] = pS
                        for g in range(2):
                            rr = RR[g][cur]
                            rrn = RR[g][1 - cur]
                            xyn = XY[g][1 - cur]
                            if j < 5:
                                s.act(ACP(xyn.ap, G[g]["pS"].ap[:, 0:512]), [G[g]["pS"]], [xyn.r(0), xyn.r(1)])
                            s.dve(TT(rrn.ap, G[g]["pR"].ap[:, 0:512], rr.ap, ALU.add), [G[g]["pR"], rr], [rrn])
                        cur = 1 - cur
                    for g in range(2):
                        pU = bank()
                        for par in range(2):
                            ps_ = slice(par * 64, (par + 1) * 64)
                            for hi in range(4):
                                h = g * 4 + hi
                                s.pe(MM(pU.ap[ps_, hi * 128:hi * 128 + 64], ch(UG16, h, par), ids(h % 2)), [UG16, id16], [pU])
                                s.pe(MM(pU.ap[ps_, hi * 128 + 64:hi * 128 + 128], ch(KG16, h, par), ids(h % 2)), [KG16, id16], [pU])
                        G[g]["pU"] = pU
                    for g in range(2):
                        s.act(ACP(UKG[g].ap, G[g]["pU"].ap[:, 0:512]), [G[g]["pU"]], [UKG[g]])
                    for g in range(2):
                        A = AXY[g]
                        wz = RR[g][cur]
                        ukg = UKG[g]
                        pQs = [bank(), bank()]
                        pNs = [bank(), bank()]
                        pYL = bank()
                        for par in range(2):
                            ps_ = slice(par * 64, (par + 1) * 64)
                            pQ = pQs[par]
                            pN = pNs[par]
                            for hi in range(4):
                                h = g * 4 + hi
                                kcl, hh = hi // 2, hi % 2
                                hb = slice(hh * 64, (hh + 1) * 64)
                                Wm = wz.ap[ps_, hi * 128:hi * 128 + 64]
                                Zm = wz.ap[ps_, hi * 128 + 64:hi * 128 + 128]
                                AruT = A.ap[ps_, hi * 128 + 64:hi * 128 + 128]
                                ArkT = A.ap[ps_, 512 + hi * 128 + 64:512 + hi * 128 + 128]
                                Ug = ukg.ap[ps_, hi * 128:hi * 128 + 64]
                                Kg = ukg.ap[ps_, hi * 128 + 64:hi * 128 + 128]
                                Vm = vtm16.ap[ps_, h * 64:(h + 1) * 64]
                                s.pe(MM(pQ.ap[hb, kcl * 64:(kcl + 1) * 64], Wm, AruT), [wz, A.r(0)], [pQ])
                                s.pe(MM(pQ.ap[hb, 128 + kcl * 64:128 + (kcl + 1) * 64], Wm, Ug), [wz, ukg], [pQ])
                                s.pe(MM(pN.ap[hb, kcl * 64:(kcl + 1) * 64], Ug, Zm, True, False), [ukg, wz], [pN])
                                s.pe(MM(pN.ap[hb, kcl * 64:(kcl + 1) * 64], Kg, Vm, False, True), [ukg, vtm16], [pN])
                                s.pe(MM(pYL.ap[ps_, hi * 64:(hi + 1) * 64], AruT, Zm, True, False), [A.r(0), wz], [pYL])
                                s.pe(MM(pYL.ap[ps_, hi * 64:(hi + 1) * 64], ArkT, Vm, False, True), [A.r(1), vtm16], [pYL])
                        gs = slice(g * 256, (g + 1) * 256)
                        for par in range(2):
                            qo = QT16.ap[:, gs].rearrange("p (k a t) -> p k a t", k=2, a=2)[:, :, par, :]
                            rt = RT32.v3(4, NB)[:, 2 * g:2 * g + 2, tt * 128 + par * 64:tt * 128 + (par + 1) * 64]
                            s.dve(TT(qo, pQs[par].ap[:, 0:128].rearrange("p (k t) -> p k t", k=2), rt, ALU.add),
                                  [pQs[par], RT32], [QT16.r(g)])
                            mo = MT16.ap[:, g * 256 + par * 128:g * 256 + (par + 1) * 128]
                            s.act(ACP(mo, pQs[par].ap[:, 128:256]), [pQs[par]], [MT16.r(g)])
                            no = N32.ap[:, g * 256 + par * 128:g * 256 + (par + 1) * 128]
                            s.act(ACP(no, pNs[par].ap[:, 0:128]), [pNs[par]], [N32.r(g)])
                        s.dve(CP(YL32.ap[:, gs], pYL.ap[:, 0:256]), [pYL], [YL32.r(g)])
                    pYh = [bank(), bank()]
                    Hv = H32.ap[:, 0:256]
                    H16v = H16.ap[:, 0:256]
                    for par in range(2):
                        ps_ = slice(par * 64, (par + 1) * 64)
                        pH = bank()
                        for h in range(8):
                            kc, hh = h // 2, h % 2
                            g, kcl = kc // 2, kc % 2
                            hb = slice(hh * 64, (hh + 1) * 64)
                            qc = g * 256 + kcl * 128 + par * 64
                            mc = g * 256 + par * 128 + kcl * 64
                            Hh = H16v[hb, kc * 64:(kc + 1) * 64]
                            s.pe(MM(pYh[hh].ap[ps_, kc * 64:(kc + 1) * 64], QT16.ap[hb, qc:qc + 64], Hh), [QT16.r(g), H16], [pYh[hh]])
                            s.pe(MM(pH.ap[hb, kc * 64:(kc + 1) * 64], MT16.ap[hb, mc:mc + 64], Hh), [MT16.r(g), H16], [pH])
                        clast = tt * 128 + par * 64 + 63
                        gcb = eb.v3(4, NB)[:, :, clast:clast + 1].to_broadcast([128, 4, 64])
                        s.pool(TT(Ht.v3(4, 64), H32.ap[:, 0:256].rearrange("p (k v) -> p k v", k=4), gcb, ALU.mult), [H32, eb], [Ht])
                        n4 = N32.ap.rearrange("p (g a k v) -> p g a k v", g=2, a=2, k=2)[:, :, par]
                        s.pool(TT(Ht.ap.rearrange("p (g k v) -> p g k v", g=2, k=2), Ht.ap.rearrange("p (g k v) -> p g k v", g=2, k=2),
                                  n4, ALU.add), [Ht, N32.r(0), N32.r(1)], [Ht])
                        s.dve(TT(Hv, pH.ap[:, 0:256], Ht.ap, ALU.add), [pH, Ht], [H32])
                        s.act(ACP(H16v, Hv), [H32], [H16])
                    if CUT == 13:
                        s.barrier(); return
                    for hh in range(2):
                        yo = ysb.ap.rearrange("p (k a v) -> p k a v", k=4, a=2)[:, :, hh, :]
                        yl = YL32.ap.rearrange("p (k a v) -> p k a v", k=4, a=2)[:, :, hh, :]
                        s.dve(TT(yo, pYh[hh].ap[:, 0:256].rearrange("p (k v) -> p k v", k=4), yl, ALU.add),
                              [pYh[hh], YL32.r(0), YL32.r(1)], [ysb])
                    s.dve(RED(st8.ap[:, 0:8], ysb.v3(8, 64)), [ysb], [st8])
                    s.dve(TS(st8.ap[:, 0:8], st8.ap[:, 0:8], -1.0 / 64.0, None, ALU.mult), [st8], [st8])
                    s.dve(TT(yc.v3(8, 64), ysb.v3(8, 64), st8.ap[:, 0:8].unsqueeze(2).to_broadcast([128, 8, 64]), ALU.add),
                          [ysb, st8], [yc])
                    s.pool(TT(ysq.ap, yc.ap, yc.ap, ALU.mult), [yc], [ysq])
                    s.dve(RED(st8.ap[:, 8:16], ysq.v3(8, 64)), [ysq], [st8])
                    s.dve(TS(st8.ap[:, 8:16], st8.ap[:, 8:16], 1.0 / 64.0, GN_EPS), [st8], [st8])
                    s.act(ACTF(st8.ap[:, 8:16], st8.ap[:, 8:16], AF.Sqrt), [st8], [st8])
                    s.dve(RCP(st8.ap[:, 16:24], st8.ap[:, 8:16]), [st8], [st8])
                    s.dve(TT(yc.v3(8, 64), yc.v3(8, 64), st8.ap[:, 16:24].unsqueeze(2).to_broadcast([128, 8, 64]), ALU.mult),
                          [yc, st8], [yc])
                    s.pool(TT(yc.ap, yc.ap, gng_b.ap, ALU.mult), [yc, gng_b], [yc])
                    s.pool(TT(yc.ap, yc.ap, gnb_b.ap, ALU.add), [yc, gnb_b], [yc])
                    s.pool(TT(yc.ap, yc.ap, bon.ap, ALU.add), [yc, bon], [yc])
                    yf = yfin[tt]
                    s.dve(TT(yf.ap, yc.ap, zsl.ap, ALU.mult), [yc, zsl], [yf])
                    s.dma("sp", OM[t0 + tt * 128:t0 + (tt + 1) * 128, 512:1024], yf.ap, reads=[yf])
            s.barrier()

        def pass_C(l, xin, xout):
            ar.reset()
            wA = ar.bf("wA", 8 * 1024)
            wO = ar.bf("wO", 8 * 1024)
            wst = [ar.f32(f"cwst{i}", 1024) for i in range(3)]
            n = 0
            for kc in range(8):
                for which in range(2):
                    st = wst[n % 3]
                    if which == 0:
                        src = (W["w_branch_hg"][l] if kc < 4 else W["w_branch_rw"][l])[(kc % 4) * 128:(kc % 4 + 1) * 128, :]
                        dst = wA.v3(8, 1024)[:, kc, :]
                        dr = wA.r(kc)
                    else:
                        src = W["w_out"][l][kc * 128:(kc + 1) * 128, :]
                        dst = wO.v3(8, 1024)[:, kc, :]
                        dr = wO.r(kc)
                    s.dma(dmaq(), st.ap, src, writes=[st])
                    if n % 2:
                        s.dve(CP(dst, st.ap), [st], [dr])
                    else:
                        s.act(ACP(dst, st.ap), [st], [dr])
                    n += 1
            om = [ar.f32(f"om{i}", 1024) for i in range(2)]
            xt = [ar.f32(f"cx{i}", 1024) for i in range(2)]
            gt = [ar.f32(f"gt{i}", 2048) for i in range(2)]
            omT_ = [ar.bf(f"omT{i}", 1024) for i in range(2)]
            mg_ = [ar.f32(f"mg{i}", 1024) for i in range(2)]
            m2_ = [ar.f32(f"m2{i}", 1024) for i in range(2)]
            mT_ = [ar.bf(f"mT{i}", 1024) for i in range(2)]
            zr_ = [ar.f32(f"zr{i}", 1024) for i in range(2)]
            bst_ = [ar.f32(f"bst{i}", 16) for i in range(2)]
            xo = [ar.f32(f"xo{i}", 1024) for i in range(2)]
            for it in range(T // 128):
                r0 = it * 128
                omT, mg, m2, mT, zr, bst = omT_[it % 2], mg_[it % 2], m2_[it % 2], mT_[it % 2], zr_[it % 2], bst_[it % 2]
                o_ = om[it % 2]
                x_ = xt[it % 2]
                g_ = gt[it % 2]
                s.dma("sp", o_.ap, OM[r0:r0 + 128, :], writes=[o_])
                s.dma("pool", x_.ap, xin[r0:r0 + 128, :], writes=[x_])
                s.dma("sp", g_.ap, PT[r0:r0 + 128, 1024:3072], writes=[g_])
                for half in range(2):
                    pb = bank()
                    for j in range(4):
                        kc = half * 4 + j
                        s.pe(TR(pb.ap[:, j * 128:(j + 1) * 128], o_.ap[:, kc * 128:(kc + 1) * 128], ident), [o_, cst], [pb])
                    evac(omT.ap[:, half * 512:(half + 1) * 512], pb.ap[:, 0:512], [pb], [omT.r(half)])
                s.act(ACTF(g_.ap, g_.ap, AF.Sigmoid), [g_], [g_])
                omv = omT.v3(8, 128)
                for hf in range(2):
                    cs_ = slice(hf * 512, (hf + 1) * 512)
                    pa_ = bank()
                    pb_ = bank()
                    for kc in range(4):
                        s.pe(MM(pa_.ap[:, 0:512], omv[:, kc, :], wA.v3(8, 1024)[:, kc, cs_], kc == 0, kc == 3),
                             [omT.r(0), wA.r(kc)], [pa_])
                    for kc in range(4, 8):
                        s.pe(MM(pb_.ap[:, 0:512], omv[:, kc, :], wA.v3(8, 1024)[:, kc, cs_], kc == 4, kc == 7),
                             [omT.r(1), wA.r(kc)], [pb_])
                    s.dve(TT(mg.ap[:, cs_], pa_.ap[:, 0:512], g_.ap[:, hf * 512:(hf + 1) * 512], ALU.mult), [pa_, g_], [mg.r(hf)])
                    s.dve(TT(m2.ap[:, cs_], pb_.ap[:, 0:512], g_.ap[:, 1024 + hf * 512:1024 + (hf + 1) * 512], ALU.mult),
                          [pb_, g_], [m2.r(hf)])
                    s.pool(TT(mg.ap[:, cs_], mg.ap[:, cs_], m2.ap[:, cs_], ALU.add), [mg.r(hf), m2.r(hf)], [mg.r(hf)])
                for half in range(2):
                    pb = bank()
                    for j in range(4):
                        kc = half * 4 + j
                        s.pe(TR(pb.ap[:, j * 128:(j + 1) * 128], mg.ap[:, kc * 128:(kc + 1) * 128], ident), [mg.r(half), cst], [pb])
                    evac(mT.ap[:, half * 512:(half + 1) * 512], pb.ap[:, 0:512], [pb], [mT.r(half)])
                mv = mT.v3(8, 128)
                for hf in range(2):
                    cs_ = slice(hf * 512, (hf + 1) * 512)
                    po_ = bank()
                    for kc in range(8):
                        s.pe(MM(po_.ap[:, 0:512], mv[:, kc, :], wO.v3(8, 1024)[:, kc, cs_], kc == 0, kc == 7),
                             [mT.r(kc // 4), wO.r(kc)], [po_])
                    s.dve(TT(zr.ap[:, cs_], po_.ap[:, 0:512], gate_b.ap[:, cs_], ALU.mult), [po_, gate_b], [zr.r(hf)])
                    s.dve(STT(zr.ap[:, cs_], x_.ap[:, cs_], DN_ALPHA, zr.ap[:, cs_], ALU.mult, ALU.add), [x_, zr.r(hf)], [zr.r(hf)])
                    s.dve(BNS(bst.ap[:, hf * 6:(hf + 1) * 6], zr.ap[:, cs_]), [zr.r(hf)], [bst])
                s.dve(BNA(bst.ap[:, 12:14], bst.ap[:, 0:12]), [bst], [bst])
                s.dve(TS(bst.ap[:, 14:15], bst.ap[:, 13:14], LN_EPS, None, ALU.add), [bst], [bst])
                s.act(ACTF(bst.ap[:, 14:15], bst.ap[:, 14:15], AF.Sqrt), [bst], [bst])
                s.dve(RCP(bst.ap[:, 15:16], bst.ap[:, 14:15]), [bst], [bst])
                xo_ = xo[it % 2]
                s.dve(TS(xo_.ap, zr.ap, bst.ap[:, 12:13], bst.ap[:, 15:16], ALU.subtract, ALU.mult),
                      [zr.r(0), zr.r(1), bst], [xo_])
                s.pool(TT(xo_.ap, xo_.ap, lng_b.ap, ALU.mult), [xo_, lng_b], [xo_])
                s.pool(TT(xo_.ap, xo_.ap, lnb_b.ap, ALU.add), [xo_, lnb_b], [xo_])
                s.dma("sp", xout[r0:r0 + 128, :], xo_.ap, reads=[xo_])
            s.barrier()

        s.barrier()
        stages = []
        for l in range(NL):
            xin = x_in if l == 0 else X1
            xout = out if l == NL - 1 else X1
            stages += [("setup", l), ("A", l), ("B1", l), ("B2", l), ("C", l)]
        for name, l in stages:
            xin = x_in if l == 0 else X1
            xout = out if l == NL - 1 else X1
            if name == "setup":
                layer_setup(l)
            elif name == "A":
                pass_A(l, xin)
            elif name == "B1":
                pass_B1(l)
            elif name == "B2":
                pass_B2(l)
            else:
                pass_C(l, xin, xout)
            if stop_after == (name, l):
                break
        s.barrier()
        s.emit()
    return nc, s


_CACHE = {}


def _prep_weights(inputs):
    m = {}
    for n in WEIGHT_NAMES:
        a = np.ascontiguousarray(np.asarray(inputs[n], dtype=np.float32))
        m[n] = a.reshape(WEIGHT_SHAPES[n])
    m["consts"] = make_consts()
    return m


def kernel(**inputs):
    x = np.asarray(inputs["x"], dtype=np.float32)
    c = np.asarray(inputs["c"], dtype=np.float32)
    B, T, _ = x.shape
    key = (T,)
    if key not in _CACHE:
        _CACHE[key] = build(T)[0]
    nc = _CACHE[key]
    wm = _prep_weights(inputs)
    in_maps = []
    for core in range(8):
        b = core % B
        mm = dict(wm)
        mm["x"] = np.ascontiguousarray(x[b])
        mm["c"] = np.ascontiguousarray(c[b:b + 1])
        in_maps.append(mm)
    res = run_bass_kernel_spmd(nc, in_maps, core_ids=list(range(8)))
    return np.stack([res.results[b]["out"] for b in range(B)], axis=0).astype(np.float32)
```

```python
import contextlib
import math
import numpy as np
import concourse.bass as bass
import concourse.mybir as mybir
from concourse.bass_utils import run_bass_kernel_spmd

F32 = mybir.dt.float32
BF16 = mybir.dt.bfloat16
AF = mybir.ActivationFunctionType
ALU = mybir.AluOpType
AX = mybir.AxisListType

ENGS = ("pe", "dve", "act", "pool", "sp")
EPOCH = 7000
NDMASEM = 8

D = 1024
NIN = 6272
DN_ALPHA = 4.0 ** 0.25
LN_EPS = 1e-5
RMS_EPS = 1e-6
GN_EPS = 64e-5
NB = 256
import os
CUT = int(os.environ.get('KCUT', '0'))
SKIP = int(os.environ.get('KSKIP', '0'))
SKIPM = int(os.environ.get('KSKIPM', '0'))
KV = int(os.environ.get('KV', '0'))
MERGE_PR = int(os.environ.get('KMERGE', '1'))
CH = 64


class Res:
    __slots__ = ("last_w", "readers", "name", "excl")

    def __init__(self, name="", excl=False):
        self.last_w = None
        self.readers = []
        self.name = name
        self.excl = excl


class Tl:
    def __init__(self, ap, name):
        self.ap = ap
        self.res = Res(name)
        self.name = name
        self.sub = {}

    def v3(self, a, b):
        return self.ap.rearrange("p (a b) -> p a b", a=a, b=b)

    def v4(self, a, b, c):
        return self.ap.rearrange("p (a b c) -> p a b c", a=a, b=b, c=c)

    def r(self, key):
        x = self.sub.get(key)
        if x is None:
            x = Res(f"{self.name}:{key}")
            self.sub[key] = x
        return x


class Sched:
    def __init__(self, nc, stack):
        self.nc = nc
        self.stack = stack
        self.q = {e: [] for e in ENGS}
        self.cnt = {e: 0 for e in ENGS}
        self.seen = {e: {} for e in ENGS}
        self.esem = {}
        self.dsem = {}
        self.dcnt = {}
        self.dnext = {e: 0 for e in ENGS}
        self.same_engine_sync = {"pe": False, "dve": True, "act": True, "pool": True, "sp": True}
        self.nops = 0

    def sem(self, name):
        return self.stack.enter_context(self.nc.semaphore(name))

    def R(self, obj):
        if isinstance(obj, Res):
            return obj
        return obj.res

    def _esem(self, eng, count):
        ep = (count - 1) // EPOCH
        k = (eng, ep)
        if k not in self.esem:
            self.esem[k] = self.sem(f"e_{eng}_{ep}")
        return self.esem[k], count - ep * EPOCH

    def _add_wait(self, waits, eng, dep):
        key, val = dep
        if key[0] == "eng" and key[1] == eng and not self.same_engine_sync[eng]:
            return
        if self.seen[eng].get(key, 0) >= val:
            return
        if val > waits.get(key, 0):
            waits[key] = val

    def _lower_waits(self, eng, waits):
        wl = []
        for key, val in waits.items():
            self.seen[eng][key] = val
            if key[0] == "eng":
                wl.append(self._esem(key[1], val))
            else:
                wl.append((self.dsem[(key[1], key[2])], val))
        return wl

    def op(self, eng, fn, reads=(), writes=(), dma=False):
        reads = [self.R(r) for r in reads]
        writes = [self.R(w) for w in writes]
        ex = [r for r in reads if r.excl]
        if ex:
            reads = [r for r in reads if not r.excl]
            writes = writes + [r for r in ex if r not in writes]
        waits = {}
        for r in reads:
            if r.last_w is not None:
                self._add_wait(waits, eng, r.last_w)
        for w in writes:
            if w.last_w is not None:
                self._add_wait(waits, eng, w.last_w)
            for rd in w.readers:
                self._add_wait(waits, eng, rd)
        if dma:
            idx = self.dnext[eng]
            self.dnext[eng] = (idx + 1) % NDMASEM
            dk = (eng, idx)
            if dk not in self.dsem:
                self.dsem[dk] = self.sem(f"d_{eng}_{idx}")
                self.dcnt[dk] = 0
            key = ("dma", eng, idx)
            if self.dcnt[dk] > 0:
                self._add_wait(waits, eng, (key, self.dcnt[dk] * 16))
            self.dcnt[dk] += 1
            done = (key, self.dcnt[dk] * 16)
            inc = (self.dsem[dk], 16)
        else:
            self.cnt[eng] += 1
            c = self.cnt[eng]
            done = (("eng", eng), c)
            inc = (self._esem(eng, c)[0], 1)
        wl = self._lower_waits(eng, waits)
        self.q[eng].append((fn, wl, inc))
        self.nops += 1
        for r in reads:
            r.readers.append(done)
            if len(r.readers) > 64:
                best = {}
                for k, v in r.readers:
                    if v > best.get(k, 0):
                        best[k] = v
                r.readers = list(best.items())
        for w in writes:
            w.last_w = done
            w.readers = []
        return done

    def pe(self, fn, reads=(), writes=()):
        return self.op("pe", fn, reads, writes)

    def dve(self, fn, reads=(), writes=()):
        return self.op("dve", fn, reads, writes)

    def act(self, fn, reads=(), writes=()):
        return self.op("act", fn, reads, writes)

    def pool(self, fn, reads=(), writes=()):
        return self.op("pool", fn, reads, writes)

    def dma(self, eng, out, in_, reads=(), writes=(), **kw):
        return self.op(eng, lambda e: e.dma_start(out=out, in_=in_, **kw), reads, writes, dma=True)

    def wait_all(self, eng, deps):
        waits = {}
        for d in deps:
            self._add_wait(waits, eng, d)
        wl = self._lower_waits(eng, waits)
        if wl:
            self.q[eng].append((None, wl, None))

    def barrier(self):
        deps = [(("eng", e), self.cnt[e]) for e in ENGS if self.cnt[e] > 0]
        deps += [(("dma", k[0], k[1]), v * 16) for k, v in self.dcnt.items()]
        for e in ENGS:
            self.wait_all(e, deps)

    def emit(self):
        nc = self.nc
        with nc.Block() as block:
            def run(eng_name):
                def body(eng):
                    for fn, wl, inc in self.q[eng_name]:
                        for sm, v in wl:
                            eng.wait_ge(sm, v)
                        if fn is not None:
                            fn(eng).then_inc(inc[0], inc[1])
                return body
            block.sync(run("sp"))
            block.tensor(run("pe"))
            block.vector(run("dve"))
            block.scalar(run("act"))
            block.gpsimd(run("pool"))


def MM(out, lhsT, rhs, st=True, sp=True):
    return lambda e: e.matmul(out, lhsT, rhs, start=st, stop=sp)


def TR(out, in_, ident):
    return lambda e: e.transpose(out, in_, ident)


def ACTF(out, in_, func, bias=0.0, scale=1.0):
    return lambda e: e.activation(out=out, in_=in_, func=func, bias=bias, scale=scale)


def CP(out, in_):
    return lambda e: e.tensor_copy(out=out, in_=in_)


def ACP(out, in_):
    return lambda e: e.copy(out=out, in_=in_)


def TT(out, a, b, op):
    return lambda e: e.tensor_tensor(out=out, in0=a, in1=b, op=op)


def TS(out, in0, s1, s2=None, op0=ALU.mult, op1=ALU.add):
    if s2 is None:
        return lambda e: e.tensor_scalar(out=out, in0=in0, scalar1=s1, scalar2=None, op0=op0)
    return lambda e: e.tensor_scalar(out=out, in0=in0, scalar1=s1, scalar2=s2, op0=op0, op1=op1)


def STT(out, in0, scalar, in1, op0, op1):
    return lambda e: e.scalar_tensor_tensor(out=out, in0=in0, scalar=scalar, in1=in1, op0=op0, op1=op1)


def MS(ap, v):
    return lambda e: e.memset(ap, v)


def RED(out, in_, op=ALU.add):
    return lambda e: e.tensor_reduce(out=out, in_=in_, axis=AX.X, op=op)


def RCP(out, in_):
    return lambda e: e.reciprocal(out=out, in_=in_)


def SCAN(out, d0, d1):
    return lambda e: e.tensor_tensor_scan(out=out, data0=d0, data1=d1, initial=0.0, op0=ALU.mult, op1=ALU.add)


def BNS(out, in_):
    return lambda e: e.bn_stats(out=out, in_=in_)


def BNA(out, in_):
    return lambda e: e.bn_aggr(out=out, in_=in_)


C_ID = 0
C_BLK = 128
C_HSEL = 256
C_LE = 258
C_LT = 322
C_GT = 386
C_SM = 450
C_N = 450 + 256


def make_consts():
    cst = np.zeros((128, C_N), np.float32)
    p = np.arange(128)[:, None]
    j = np.arange(128)[None, :]
    cst[:, C_ID:C_ID + 128] = (p == j)
    cst[:, C_BLK:C_BLK + 128] = (p // 64 == j // 64)
    cst[:, C_HSEL:C_HSEL + 2] = (p // 64 == np.arange(2)[None, :])
    j64 = np.arange(64)[None, :]
    cst[:, C_LE:C_LE + 64] = ((p % 64) <= j64)
    cst[:, C_LT:C_LT + 64] = ((p % 64) < j64)
    cst[:, C_GT:C_GT + 64] = ((p % 64) > j64)
    cst[:, C_SM:C_SM + 256] = ((np.arange(256)[None, :] % 64) != 0)
    return cst


class Arena:
    def __init__(self, tensor, words):
        self.t = tensor
        self.words = words
        self.off = 0
        self.gen = 0

    def reset(self):
        self.off = 0
        self.gen += 1

    def f32(self, name, cols, parts=128):
        assert self.off + cols <= self.words, (name, self.off, cols, self.words)
        ap = self.t[0:parts, self.off:self.off + cols]
        self.off += (cols + 15) // 16 * 16
        return Tl(ap, f"{name}_{self.gen}")

    def bf(self, name, cols, parts=128):
        w = (cols + 1) // 2
        assert self.off + w <= self.words, (name, self.off, w, self.words)
        ap = self.t[0:parts, self.off:self.off + w].bitcast(BF16)[:, 0:cols]
        self.off += (w + 15) // 16 * 16
        return Tl(ap, f"{name}_{self.gen}")


WEIGHT_NAMES = ["w_ada", "b_ada", "w_in", "hg_lower_bounds", "hg_norm_g", "rw_mu", "rw_w0", "rw_w_up",
                "rw_a0", "rw_a_up", "rw_k_k", "rw_k_a", "rw_r_k", "rw_v0", "rw_v_down", "rw_v_up",
                "rw_gn_g", "rw_gn_b", "w_branch_hg", "w_branch_rw", "w_out", "ln_g", "ln_b"]
WEIGHT_SHAPES = {
    "w_ada": [2, 1024, 3072], "b_ada": [2, 3072], "w_in": [2, 1024, NIN], "hg_lower_bounds": [2, 512],
    "hg_norm_g": [2, 512], "rw_mu": [2, 2176], "rw_w0": [2, 512], "rw_w_up": [2, 64, 512], "rw_a0": [2, 512],
    "rw_a_up": [2, 64, 512], "rw_k_k": [2, 512], "rw_k_a": [2, 512], "rw_r_k": [2, 512], "rw_v0": [1, 512],
    "rw_v_down": [1, 1024, 32], "rw_v_up": [1, 32, 512], "rw_gn_g": [2, 512], "rw_gn_b": [2, 512],
    "w_branch_hg": [2, 512, 1024], "w_branch_rw": [2, 512, 1024], "w_out": [2, 1024, 1024],
    "ln_g": [2, 1024], "ln_b": [2, 1024],
}

ARENA_WORDS = 45500


def build(T, NL=2, dbg=False, stop_after=None):
    assert T % NB == 0
    NBLK = T // NB
    nc = bass.Bass("TRN2", target_bir_lowering=False)
    x_in = nc.dram_tensor("x", [T, D], F32, kind="ExternalInput").ap()
    c_in = nc.dram_tensor("c", [1, D], F32, kind="ExternalInput").ap()
    cst_in = nc.dram_tensor("consts", [128, C_N], F32, kind="ExternalInput").ap()
    W = {n: nc.dram_tensor(n, WEIGHT_SHAPES[n], F32, kind="ExternalInput").ap() for n in WEIGHT_NAMES}
    out = nc.dram_tensor("out", [T, D], F32, kind="ExternalOutput").ap()
    sk = "ExternalOutput" if dbg else "Internal"
    PF = nc.dram_tensor("PF", [26, 128, T], F32, kind=sk).ap()
    PT = nc.dram_tensor("PT", [T, 3072], F32, kind=sk).ap()
    OM = nc.dram_tensor("OM", [T, 1024], F32, kind=sk).ap()
    VF = nc.dram_tensor("VF", [4, 128, T], F32, kind=sk).ap()
    X1 = nc.dram_tensor("X1", [T, D], F32, kind=sk).ap()

    with contextlib.ExitStack() as stack:
        s = Sched(nc, stack)

        def sbt(name, cols, dtype=F32, parts=128):
            t = stack.enter_context(nc.sbuf_tensor(name, [128, cols], dtype))
            return Tl(t[0:parts, :], name)

        arena_t = stack.enter_context(nc.sbuf_tensor("arena", [128, ARENA_WORDS], F32))
        ar = Arena(arena_t, ARENA_WORDS)
        banks = []
        for i in range(8):
            t = stack.enter_context(nc.psum_tensor(f"bank{i}", [128, 512], F32))
            banks.append(Tl(t[:, :], f"bank{i}"))
            banks[-1].res.excl = True
        bank_i = [0]

        def bank():
            b = banks[bank_i[0] % 8]
            bank_i[0] += 1
            return b

        dq = [0]

        def dmaq():
            dq[0] += 1
            return "sp" if dq[0] % 2 else "pool"

        ev = [0]

        def evac(out_ap, in_ap, reads, writes):
            ev[0] += 1
            if ev[0] % 2:
                return s.dve(CP(out_ap, in_ap), reads, writes)
            return s.act(ACP(out_ap, in_ap), reads, writes)

        cst = sbt("cst", C_N)
        s.dma("sp", cst.ap, cst_in, writes=[cst])
        ident = cst.ap[:, C_ID:C_ID + 128]
        id16 = sbt("id16", 128, BF16)
        s.dve(CP(id16.ap, ident), [cst], [id16])
        hsel16 = sbt("hsel16", 2, BF16)
        s.dve(CP(hsel16.ap, cst.ap[:, C_HSEL:C_HSEL + 2]), [cst], [hsel16])
        blk32 = cst.ap[:, C_BLK:C_BLK + 128]
        blk16 = sbt("blk16", 128, BF16)
        s.dve(CP(blk16.ap, cst.ap[:, C_BLK:C_BLK + 128]), [cst], [blk16])
        scanmask = cst.ap[:, C_SM:C_SM + 256]
        hgmask = sbt("hgmask", 256)
        rwmXY = sbt("rwmXY", 512)
        rwmZ = sbt("rwmZ", 256)
        for h in range(4):
            s.pool(CP(hgmask.ap[:, h * 64:(h + 1) * 64], cst.ap[:, C_LE:C_LE + 64]), [cst], [hgmask])
            s.pool(CP(rwmXY.ap[:, h * 128:h * 128 + 64], cst.ap[:, C_LT:C_LT + 64]), [cst], [rwmXY])
            s.pool(CP(rwmXY.ap[:, h * 128 + 64:h * 128 + 128], cst.ap[:, C_LE:C_LE + 64]), [cst], [rwmXY])
            s.pool(CP(rwmZ.ap[:, h * 64:(h + 1) * 64], cst.ap[:, C_GT:C_GT + 64]), [cst], [rwmZ])

        cols = sbt("cols", 96)
        K_SH, K_SC = 0, 8
        K_LB, K_OMLB, K_NOMLB = 16, 20, 24
        K_W0, K_A0, K_KK, K_KA, K_OMKA, K_RK, K_V0 = 28, 32, 36, 40, 44, 48, 52
        K_MU, K_OMU = 56, 73
        tmp_cols = sbt("tmp_cols", 64)
        gate_b = sbt("gate_b", 1024)
        hgg_b = sbt("hgg_b", 512)
        gng_b = sbt("gng_b", 512)
        gnb_b = sbt("gnb_b", 512)
        lng_b = sbt("lng_b", 1024)
        lnb_b = sbt("lnb_b", 1024)
        lora16 = sbt("lora16", 512, BF16)
        vup16 = sbt("vup16", 512, BF16)

        def colload(dst_c, src_ap_1d, n):
            s.dma(dmaq(), cols.ap[:, dst_c:dst_c + n], src_ap_1d.rearrange("(k p) -> p k", p=128),
                  writes=[cols], allow_slow_non_contiguous=True)

        def layer_setup(l):
            ar.reset()
            colload(K_W0, W["rw_w0"][l], 4)
            colload(K_A0, W["rw_a0"][l], 4)
            colload(K_KK, W["rw_k_k"][l], 4)
            colload(K_KA, W["rw_k_a"][l], 4)
            colload(K_RK, W["rw_r_k"][l], 4)
            colload(K_MU, W["rw_mu"][l], 17)
            if l >= 1:
                colload(K_V0, W["rw_v0"][l - 1], 4)
            s.dve(TS(cols.ap[:, K_OMU:K_OMU + 17], cols.ap[:, K_MU:K_MU + 17], -1.0, 1.0), [cols], [cols])
            s.dve(TS(cols.ap[:, K_OMKA:K_OMKA + 4], cols.ap[:, K_KA:K_KA + 4], -1.0, 1.0), [cols], [cols])
            if l == 0:
                s.dve(MS(cols.ap[:, K_LB:K_LB + 4], 0.0), [], [cols])
            else:
                s.dma(dmaq(), tmp_cols.ap[:, 0:4], W["hg_lower_bounds"][0].rearrange("(k p) -> p k", p=128),
                      writes=[tmp_cols], allow_slow_non_contiguous=True)
                s.dma(dmaq(), tmp_cols.ap[:, 4:8], W["hg_lower_bounds"][1].rearrange("(k p) -> p k", p=128),
                      writes=[tmp_cols], allow_slow_non_contiguous=True)
                s.dve(TT(tmp_cols.ap[:, 8:12], tmp_cols.ap[:, 4:8], tmp_cols.ap[:, 0:4], ALU.subtract),
                      [tmp_cols], [tmp_cols])
                s.act(ACTF(cols.ap[:, K_LB:K_LB + 4], tmp_cols.ap[:, 8:12], AF.Sigmoid), [tmp_cols], [cols])
            s.dve(TS(cols.ap[:, K_OMLB:K_OMLB + 4], cols.ap[:, K_LB:K_LB + 4], -1.0, 1.0), [cols], [cols])
            s.dve(TS(cols.ap[:, K_NOMLB:K_NOMLB + 4], cols.ap[:, K_OMLB:K_OMLB + 4], -1.0), [cols], [cols])
            if CUT == 1:
                s.barrier(); return
            s.dma(dmaq(), hgg_b.ap, W["hg_norm_g"][l].partition_broadcast(128), writes=[hgg_b])
            s.dma(dmaq(), gng_b.ap, W["rw_gn_g"][l].partition_broadcast(128), writes=[gng_b])
            s.dma(dmaq(), gnb_b.ap, W["rw_gn_b"][l].partition_broadcast(128), writes=[gnb_b])
            s.dma(dmaq(), lng_b.ap, W["ln_g"][l].partition_broadcast(128), writes=[lng_b])
            s.dma(dmaq(), lnb_b.ap, W["ln_b"][l].partition_broadcast(128), writes=[lnb_b])
            if CUT == 2:
                s.barrier(); return
            st = ar.f32("lst", 512)
            s.dma(dmaq(), st.ap[0:64, :], W["rw_w_up"][l], writes=[st])
            s.dma(dmaq(), st.ap[64:128, :], W["rw_a_up"][l], writes=[st])
            s.dve(CP(lora16.ap, st.ap), [st], [lora16])
            if l >= 1:
                st2 = ar.f32("lst2", 512)
                s.dma(dmaq(), st2.ap[0:32, :], W["rw_v_up"][l - 1], writes=[st2])
                s.dve(CP(vup16.ap[0:32, :], st2.ap[0:32, :]), [st2], [vup16])
            if CUT == 3:
                s.barrier(); return
            ccol = ar.f32("ccol", 8)
            s.dma(dmaq(), ccol.ap, c_in[0].rearrange("(k p) -> p k", p=128), writes=[ccol],
                  allow_slow_non_contiguous=True)
            scol = ar.f32("scol", 8)
            s.act(ACTF(scol.ap, ccol.ap, AF.Silu), [ccol], [scol])
            srep = ar.f32("srep", 8 * 128)
            s.dve(CP(srep.v3(8, 128), scol.ap.unsqueeze(2).to_broadcast([128, 8, 128])), [scol], [srep])
            bcol = ar.f32("bcol", 24)
            s.dma(dmaq(), bcol.ap, W["b_ada"][l].rearrange("(k p) -> p k", p=128), writes=[bcol],
                  allow_slow_non_contiguous=True)
            brow = ar.f32("brow", 3072)
            s.dma(dmaq(), brow.ap, W["b_ada"][l].partition_broadcast(128), writes=[brow])
            crow = ar.f32("crow", 2048)
            ctmp = ar.f32("ctmp", 1024)
            if CUT == 4:
                s.barrier(); return
            wst = [ar.f32(f"adaw{i}", 8 * 512) for i in range(2)]
            for cb in range(6):
                wt = wst[cb % 2]
                s.dma(dmaq(), wt.v3(8, 512),
                      W["w_ada"][l][:, cb * 512:(cb + 1) * 512].rearrange("(k p) c -> p k c", p=128), writes=[wt])
                pb = bank()
                for kc in range(8):
                    s.pe(MM(pb.ap[:, 0:512], srep.v3(8, 128)[:, kc, :], wt.v3(8, 512)[:, kc, :],
                            kc == 0, kc == 7), [wt, srep], [pb])
                if cb < 4:
                    s.dve(TT(crow.ap[:, cb * 512:(cb + 1) * 512], pb.ap[:, 0:512], brow.ap[:, cb * 512:(cb + 1) * 512],
                             ALU.add), [pb, brow], [crow])
                else:
                    g0 = (cb - 4) * 512
                    s.dve(TT(gate_b.ap[:, g0:g0 + 512], pb.ap[:, 0:512], brow.ap[:, cb * 512:(cb + 1) * 512], ALU.add),
                          [pb, brow], [gate_b])
            for which in range(2):
                s.dve(TT(ctmp.v3(8, 128), crow.ap[:, which * 1024:(which + 1) * 1024].rearrange("p (k c) -> p k c", k=8),
                         ident.unsqueeze(1).to_broadcast([128, 8, 128]), ALU.mult), [crow, cst], [ctmp])
                dstc = K_SH if which == 0 else K_SC
                s.dve(RED(cols.ap[:, dstc:dstc + 8], ctmp.v3(8, 128)), [ctmp], [cols])
            s.dve(TS(cols.ap[:, K_SC:K_SC + 8], cols.ap[:, K_SC:K_SC + 8], 1.0, None, ALU.add), [cols], [cols])
            s.dve(TS(gate_b.ap, gate_b.ap, 1.0, None, ALU.add), [gate_b], [gate_b])
            s.barrier()

        def fmcol(ci):
            if ci < 4:
                return ci * 128
            if ci < 8:
                return 512 + (ci - 4) * 128
            return 2048 + (ci - 8) * 128

        def tmcol(g):
            return (1024, 1536, 4224, 4736, 5248, 5760)[g]

        def pass_A(l, xin):
            ar.reset()
            w16 = ar.bf("w16", 8 * NIN)
            w16v = w16.v3(8, NIN)
            hT = [ar.bf(f"hT{i}", 8 * NB) for i in range(2)]
            xt = [ar.f32(f"xt{i}", 1024) for i in range(2)]
            wst = [ar.f32(f"wst{i}", 1568) for i in range(3)]
            fst = [ar.f32(f"fst{i}", 512) for i in range(3)]
            tst = [ar.f32(f"tst{i}", 3072) for i in range(2)]
            vd16 = ar.bf("vd16", 8 * 32) if l >= 1 else None
            n = 0
            for kc in range(8):
                for qd in range(4):
                    st = wst[n % 3]
                    s.dma(dmaq(), st.ap, W["w_in"][l][kc * 128:(kc + 1) * 128, qd * 1568:(qd + 1) * 1568],
                          writes=[st])
                    dst = w16v[:, kc, qd * 1568:(qd + 1) * 1568]
                    eng = ("dve", "act", "pool")[n % 3]
                    if eng == "act":
                        s.act(ACP(dst, st.ap), [st], [w16.r((kc, qd))])
                    else:
                        s.op(eng, CP(dst, st.ap), [st], [w16.r((kc, qd))])
                    n += 1
            if l >= 1:
                vst = ar.f32("vst", 8 * 32)
                s.dma(dmaq(), vst.v3(8, 32), W["rw_v_down"][l - 1].rearrange("(k p) c -> p k c", p=128),
                      writes=[vst])
                s.dve(CP(vd16.ap, vst.ap), [vst], [vd16])

            def wres(kc, c0, c1):
                return [w16.r((kc, q)) for q in range(c0 // 1568, (c1 - 1) // 1568 + 1)]

            nfm = 25
            fi = 0
            for blk in range(NBLK):
                t0 = blk * NB
                h = hT[blk % 2]
                hv = h.v3(8, NB)
                for tt in range(2):
                    xtile = xt[tt]
                    s.dma("sp", xtile.ap, xin[t0 + tt * 128:t0 + (tt + 1) * 128, :], writes=[xtile])
                    for half in range(2):
                        pb = bank()
                        for j in range(4):
                            kc = half * 4 + j
                            s.pe(TR(pb.ap[:, j * 128:(j + 1) * 128], xtile.ap[:, kc * 128:(kc + 1) * 128], ident),
                                 [xtile, cst], [pb])
                        for j in range(4):
                            kc = half * 4 + j
                            s.act(ACTF(hv[:, kc, tt * 128:(tt + 1) * 128], pb.ap[:, j * 128:(j + 1) * 128],
                                       AF.Identity, bias=cols.ap[:, K_SH + kc:K_SH + kc + 1],
                                       scale=cols.ap[:, K_SC + kc:K_SC + kc + 1]), [pb, cols], [h])
                for pr in range((nfm + 1) // 2):
                    cis = [ci for ci in (2 * pr, 2 * pr + 1) if ci < nfm]
                    pb = bank()
                    for j, ci in enumerate(cis):
                        c0 = fmcol(ci)
                        for kc in range(8):
                            s.pe(MM(pb.ap[:, j * NB:(j + 1) * NB], w16v[:, kc, c0:c0 + 128], hv[:, kc, :],
                                    kc == 0, kc == 7), wres(kc, c0, c0 + 128) + [h], [pb])
                    st = fst[fi % 3]
                    fi += 1
                    nn = len(cis) * NB
                    evac(st.ap[:, 0:nn], pb.ap[:, 0:nn], [pb], [st])
                    s.dma("pool", PF[cis[0]:cis[0] + len(cis), :, t0:t0 + NB].rearrange("c p t -> p c t"),
                          st.ap[:, 0:nn].rearrange("p (c t) -> p c t", c=len(cis)), reads=[st])
                if l >= 1:
                    pb = bank()
                    for kc in range(8):
                        s.pe(MM(pb.ap[0:32, 0:NB], vd16.v3(8, 32)[:, kc, :], hv[:, kc, :], kc == 0, kc == 7),
                             [vd16, h], [pb])
                    st = fst[fi % 3]
                    fi += 1
                    evac(st.ap[0:32, 0:NB], pb.ap[0:32, 0:NB], [pb], [st])
                    s.dma("pool", PF[25, 0:32, t0:t0 + NB], st.ap[0:32, 0:NB], reads=[st])
                for tt in range(2):
                    ts_ = tst[tt]
                    for g in range(6):
                        pb = bank()
                        c0 = tmcol(g)
                        for kc in range(8):
                            s.pe(MM(pb.ap[:, 0:512], hv[:, kc, tt * 128:(tt + 1) * 128], w16v[:, kc, c0:c0 + 512],
                                    kc == 0, kc == 7), wres(kc, c0, c0 + 512) + [h], [pb])
                        evac(ts_.ap[:, g * 512:(g + 1) * 512], pb.ap[:, 0:512], [pb], [ts_.r(g)])
                    s.dma("pool", PT[t0 + tt * 128:t0 + (tt + 1) * 128, :], ts_.ap,
                          reads=[ts_.r(g) for g in range(6)], writes=[ts_.r(g) for g in range(0)])
            s.barrier()

        def pass_B1(l):
            ar.reset()
            NCH = NB // CH
            S32 = [ar.f32(f"S32_{h}", 128) for h in range(4)]
            S16 = [ar.bf(f"S16_{h}", 128) for h in range(4)]
            for h in range(4):
                s.dve(MS(S32[h].ap, 0.0), [], [S32[h]])
                s.dve(MS(S16[h].ap, 0.0), [], [S16[h]])
            qf = [ar.f32(f"qf{i}", 8 * NB) for i in range(2)]
            tmv = [[ar.f32(f"tmv{i}_{tt}", 1024) for tt in range(2)] for i in range(2)]
            sig_ = [ar.f32(f"sig{i}", 4 * NB) for i in range(2)]
            fv_ = [ar.f32(f"fv{i}", 4 * NB) for i in range(2)]
            lgf_ = [ar.f32(f"lgf{i}", 4 * NB) for i in range(2)]
            khg_ = [ar.f32(f"khg{i}", 4 * NB) for i in range(2)]
            bcs_ = [ar.f32(f"bcs{i}", 4 * NB) for i in range(2)]
            eb_ = [ar.f32(f"eb{i}", 4 * NB) for i in range(2)]
            enb_ = [ar.f32(f"enb{i}", 4 * NB) for i in range(2)]
            kt32_ = [ar.f32(f"kt32{i}", 4 * NB) for i in range(2)]
            q16_ = [ar.bf(f"q16{i}", 4 * NB) for i in range(2)]
            k16_ = [ar.bf(f"k16{i}", 4 * NB) for i in range(2)]
            kh16_ = [ar.bf(f"kh16{i}", 4 * NB) for i in range(2)]
            v16 = [ar.bf(f"v16_{tt}", 512) for tt in range(2)]
            sT16 = [ar.bf(f"sT16_{tt}", 256) for tt in range(2)]
            khtm = [ar.bf(f"khtm_{tt}", 512) for tt in range(2)]
            osb_ = [ar.f32(f"osb{i}", 512) for i in range(2)]
            osq_ = [ar.f32(f"osq{i}", 512) for i in range(2)]
            ss_ = [ar.f32(f"ss{i}", 8) for i in range(2)]
            zs_ = [ar.f32(f"zs{i}", 512) for i in range(2)]
            ofin = [ar.f32(f"ofin{i}", 512) for i in range(2)]
            for blk in range(NBLK):
                t0 = blk * NB
                sig = sig_[blk % 2]
                fv = fv_[blk % 2]
                lgf = lgf_[blk % 2]
                khg = khg_[blk % 2]
                bcs = bcs_[blk % 2]
                eb = eb_[blk % 2]
                enb = enb_[blk % 2]
                kt32 = kt32_[blk % 2]
                q16 = q16_[blk % 2]
                k16 = k16_[blk % 2]
                kh16 = kh16_[blk % 2]
                qt = qf[blk % 2]
                qv = qt.v3(8, NB)
                s.dma("sp", qv, PF[0:8, :, t0:t0 + NB].rearrange("c p t -> p c t"), writes=[qt])
                tm = tmv[blk % 2]
                for tt in range(2):
                    s.dma("sp", tm[tt].ap, PT[t0 + tt * 128:t0 + (tt + 1) * 128, 0:1024], writes=[tm[tt]])
                s.act(ACTF(sig.ap, qt.ap[:, 4 * NB:8 * NB], AF.Sigmoid), [qt], [sig])
                for h in range(4):
                    sl = slice(h * NB, (h + 1) * NB)
                    s.dve(TS(fv.ap[:, sl], sig.ap[:, sl], cols.ap[:, K_OMLB + h:K_OMLB + h + 1],
                             cols.ap[:, K_LB + h:K_LB + h + 1]), [sig, cols], [fv])
                    s.pool(TS(khg.ap[:, sl], sig.ap[:, sl], cols.ap[:, K_NOMLB + h:K_NOMLB + h + 1],
                              cols.ap[:, K_OMLB + h:K_OMLB + h + 1]), [sig, cols], [khg])
                s.act(ACTF(lgf.ap, fv.ap, AF.Ln), [fv], [lgf])
                for h in range(4):
                    sl = slice(h * NB, (h + 1) * NB)
                    s.dve(SCAN(bcs.ap[:, sl], scanmask, lgf.ap[:, sl]), [lgf, cst], [bcs])
                s.act(ACTF(eb.ap, bcs.ap, AF.Exp), [bcs], [eb])
                s.act(ACTF(enb.ap, bcs.ap, AF.Exp, scale=-1.0), [bcs], [enb])
                s.dve(TT(q16.ap, qt.ap[:, 0:4 * NB], eb.ap, ALU.mult), [qt, eb], [q16])
                s.dve(TT(kt32.ap, khg.ap, enb.ap, ALU.mult), [khg, enb], [kt32])
                s.act(ACP(k16.ap, kt32.ap), [kt32], [k16])
                e4 = eb.v4(4, NCH, CH)
                s.dve(TT(kh16.v4(4, NCH, CH), kt32.v4(4, NCH, CH), e4[:, :, :, CH - 1:CH].to_broadcast([128, 4, NCH, CH]),
                         ALU.mult), [kt32, eb], [kh16])
                for tt in range(2):
                    osb, osq, ss, zs = osb_[tt], osq_[tt], ss_[tt], zs_[tt]
                    vt = tm[tt]
                    s.act(ACP(v16[tt].ap, vt.ap[:, 0:512]), [vt], [v16[tt]])
                    pbs = bank()
                    pbk = bank()
                    pbk16 = pbk.ap.bitcast(BF16)
                    for par in range(2):
                        cc = tt * 2 + par
                        ps_ = slice(par * 64, (par + 1) * 64)
                        for h in range(4):
                            c0 = h * NB + cc * CH
                            s.pe(MM(pbs.ap[ps_, h * 64:(h + 1) * 64], k16.ap[:, c0:c0 + CH], q16.ap[:, c0:c0 + CH]),
                                 [k16, q16], [pbs])
                            s.pe(TR(pbk16[ps_, h * 128:(h + 1) * 128], kh16.ap[:, c0:c0 + CH], id16.ap),
                                 [kh16, id16], [pbk])
                    s.dve(TT(sT16[tt].ap, pbs.ap[:, 0:256], hgmask.ap, ALU.mult), [pbs, hgmask], [sT16[tt]])
                    s.act(ACP(khtm[tt].ap, pbk16[:, 0:512]), [pbk], [khtm[tt]])
                    pbo = bank()
                    for par in range(2):
                        cc = tt * 2 + par
                        ps_ = slice(par * 64, (par + 1) * 64)
                        for h in range(4):
                            c0 = h * NB + cc * CH
                            hs = slice(h * 128, (h + 1) * 128)
                            s.pe(MM(pbo.ap[ps_, hs], sT16[tt].ap[ps_, h * 64:(h + 1) * 64], v16[tt].ap[ps_, hs],
                                    True, False), [sT16[tt], v16[tt]], [pbo])
                            s.pe(MM(pbo.ap[ps_, hs], q16.ap[:, c0:c0 + CH], S16[h].ap, False, True),
                                 [q16, S16[h]], [pbo])
                        pkv = bank()
                        for h in range(4):
                            hs = slice(h * 128, (h + 1) * 128)
                            s.pe(MM(pkv.ap[:, hs], khtm[tt].ap[ps_, hs], v16[tt].ap[ps_, hs]),
                                 [khtm[tt], v16[tt]], [pkv])
                        for h in range(4):
                            c0 = h * NB + cc * CH
                            hs = slice(h * 128, (h + 1) * 128)
                            s.dve(STT(S32[h].ap, S32[h].ap, eb.ap[:, c0 + CH - 1:c0 + CH], pkv.ap[:, hs],
                                      ALU.mult, ALU.add), [S32[h], eb, pkv], [S32[h]])
                            s.act(ACP(S16[h].ap, S32[h].ap), [S32[h]], [S16[h]])
                    s.act(ACP(osb.ap, pbo.ap[:, 0:512]), [pbo], [osb])
                    s.pool(TT(osq.ap, osb.ap, osb.ap, ALU.mult), [osb], [osq])
                    s.dve(RED(ss.ap[:, 0:4], osq.v3(4, 128)), [osq], [ss])
                    s.dve(TS(ss.ap[:, 0:4], ss.ap[:, 0:4], 1.0 / 128.0, RMS_EPS), [ss], [ss])
                    s.act(ACTF(ss.ap[:, 0:4], ss.ap[:, 0:4], AF.Sqrt), [ss], [ss])
                    s.dve(RCP(ss.ap[:, 4:8], ss.ap[:, 0:4]), [ss], [ss])
                    s.act(ACTF(zs.ap, vt.ap[:, 512:1024], AF.Silu), [vt], [zs])
                    s.pool(TT(zs.ap, zs.ap, hgg_b.ap, ALU.mult), [zs, hgg_b], [zs])
                    of = ofin[tt]
                    s.dve(TT(of.v3(4, 128), osb.v3(4, 128), ss.ap[:, 4:8].unsqueeze(2).to_broadcast([128, 4, 128]),
                             ALU.mult), [osb, ss], [of])
                    s.dve(TT(of.ap, of.ap, zs.ap, ALU.mult), [of, zs], [of])
                    s.dma("pool", OM[t0 + tt * 128:t0 + (tt + 1) * 128, 0:512], of.ap, reads=[of])
            s.barrier()

        def pass_B2(l):
            ar.reset()
            NCH = NB // CH
            W4 = 4 * NB
            H32 = ar.f32("H32", 512)
            H16 = ar.bf("H16", 512)
            s.dve(MS(H32.ap, 0.0), [], [H32])
            s.dve(MS(H16.ap, 0.0), [], [H16])
            rwl = [ar.f32(f"rwl{i}", 17 * (NB + 1)) for i in range(1)]
            sh = ar.f32("sh", 17 * NB)
            shv = sh.v3(17, NB)
            lw = ar.f32("lw", W4)
            av = ar.f32("av", W4)
            bcs = ar.f32("bcs", W4)
            t1 = ar.f32("t1", W4)
            t2 = ar.f32("t2", W4)
            t3 = ar.f32("t3", W4)
            eb = ar.f32("eb", W4)
            enb = ar.f32("enb", W4)
            ebx = ar.f32("ebx", W4)
            egb = ar.f32("egb", W4)
            kkn = ar.f32("kkn", W4)
            kmod = ar.f32("kmod", W4)
            uu = ar.f32("uu", W4)
            PR16 = ar.bf("PR16", 2 * W4)
            PT16 = Tl(PR16.ap[:, 0:W4], "PT16v")
            RT16 = Tl(PR16.ap[:, W4:2 * W4], "RT16v")
            UT16 = ar.bf("UT16", W4)
            KT16 = ar.bf("KT16", W4)
            UTz = [ar.bf(f"UTz{i}", W4) for i in range(2)]
            KTz = [ar.bf(f"KTz{i}", W4) for i in range(2)]
            PTz = [ar.bf(f"PTz{i}", W4) for i in range(2)]
            for zt in UTz + KTz + PTz:
                s.pool(MS(zt.ap, 0.0), [], [zt])
            UG16 = ar.bf("UG16", W4)
            KG16 = ar.bf("KG16", W4)
            rk16 = ar.bf("rk16", W4)
            lo16 = ar.bf("lo16", NB)
            vr16 = ar.bf("vr16", NB) if l >= 1 else None
            vrs = ar.f32("vrs", NB) if l >= 1 else None
            vfl = ar.f32("vfl", W4) if l >= 1 else None
            vtm32 = ar.f32("vtm32", 512)
            vtm16 = ar.bf("vtm16", 512)
            zsl_ = [ar.f32(f"zsl{i}", 512) for i in range(2)]
            bon_ = [ar.f32(f"bon{i}", 512) for i in range(2)]
            AXY = [ar.bf(f"AXY{g}", 1024) for g in range(2)]
            RT32 = ar.f32("RT32", W4)
            XY = [[ar.bf(f"XY{g}_{i}", 512) for i in range(2)] for g in range(2)]
            RR = [[ar.bf(f"RR{g}_{i}", 512) for i in range(2)] for g in range(2)]
            QT16_ = [ar.bf(f"QT16{i}", 512) for i in range(2)]
            MT16_ = [ar.bf(f"MT16{i}", 512) for i in range(2)]
            UKG = [ar.bf(f"UKG{g}", 512) for g in range(2)]
            YL32_ = [ar.f32(f"YL32{i}", 512) for i in range(2)]
            N32_ = [ar.f32(f"N32{i}", 512) for i in range(2)]
            Ht = ar.f32("Ht", 256)
            ysb_ = [ar.f32(f"ysb{i}", 512) for i in range(2)]
            yc_ = [ar.f32(f"yc{i}", 512) for i in range(2)]
            ysq_ = [ar.f32("ysq", 512)] * 2
            st8_ = [ar.f32(f"st8{i}", 32) for i in range(2)]
            yfin = [ar.f32(f"yfin{i}", 512) for i in range(2)]
            cvf = cols.ap
            for blk in range(NBLK):
                t0 = blk * NB
                rl = rwl[0]
                rlv = rl.v3(17, NB + 1)
                if t0 == 0:
                    s.dve(MS(rlv[:, :, 0:1], 0.0), [], [rl])
                    s.dma("sp", rlv[:, :, 1:NB + 1], PF[8:25, :, 0:NB].rearrange("c p t -> p c t"), writes=[rl])
                else:
                    s.dma("sp", rlv, PF[8:25, :, t0 - 1:t0 + NB].rearrange("c p t -> p c t"), writes=[rl])
                for kc in range(17):
                    s.act(ACTF(shv[:, kc, :], rlv[:, kc, 1:NB + 1], AF.Identity,
                               scale=cvf[:, K_OMU + kc:K_OMU + kc + 1]), [rl, cols], [sh.r(kc)])
                    s.dve(STT(shv[:, kc, :], rlv[:, kc, 0:NB], cvf[:, K_MU + kc:K_MU + kc + 1], shv[:, kc, :],
                              ALU.mult, ALU.add), [rl, cols, sh.r(kc)], [sh.r(kc)])
                rsl = lambda kc: shv[:, kc, :]
                R_ = sh.ap[:, 0:W4]
                K_ = sh.ap[:, W4:2 * W4]
                V_ = sh.ap[:, 2 * W4:3 * W4]
                Z_ = sh.ap[:, 3 * W4:4 * W4]
                rR = [sh.r(kc) for kc in range(0, 4)]
                rK = [sh.r(kc) for kc in range(4, 8)]
                rV = [sh.r(kc) for kc in range(8, 12)]
                rZ = [sh.r(kc) for kc in range(12, 16)]
                s.act(ACTF(lo16.ap[0:64, :], shv[0:64, 16, :], AF.Tanh), [sh.r(16)], [lo16])
                s.dve(CP(lo16.ap[64:128, :], shv[64:128, 16, :]), [sh.r(16)], [lo16])
                pw = [bank(), bank()]
                pa = [bank(), bank()]
                for kc in range(4):
                    s.pe(MM(pw[kc // 2].ap[:, (kc % 2) * NB:(kc % 2 + 1) * NB], lora16.ap[0:64, kc * 128:(kc + 1) * 128],
                            lo16.ap[0:64, :]), [lora16, lo16], [pw[kc // 2]])
                    s.pe(MM(pa[kc // 2].ap[:, (kc % 2) * NB:(kc % 2 + 1) * NB], lora16.ap[64:128, kc * 128:(kc + 1) * 128],
                            lo16.ap[64:128, :]), [lora16, lo16], [pa[kc // 2]])
                for kc in range(4):
                    sl = slice(kc * NB, (kc + 1) * NB)
                    psl = slice((kc % 2) * NB, (kc % 2 + 1) * NB)
                    s.act(ACTF(lw.ap[:, sl], pw[kc // 2].ap[:, psl], AF.Sigmoid, bias=cvf[:, K_W0 + kc:K_W0 + kc + 1]),
                          [pw[kc // 2], cols], [lw])
                    s.act(ACTF(av.ap[:, sl], pa[kc // 2].ap[:, psl], AF.Sigmoid, bias=cvf[:, K_A0 + kc:K_A0 + kc + 1]),
                          [pa[kc // 2], cols], [av])
                s.dve(TS(lw.ap, lw.ap, -math.exp(-0.5), None, ALU.mult), [lw], [lw])
                if l == 0:
                    s.dma("pool", VF[:, :, t0:t0 + NB].rearrange("c p t -> p c t"), V_.rearrange("p (c t) -> p c t", c=4),
                          reads=rV)
                else:
                    s.dma("sp", vrs.ap[0:32, :], PF[25, 0:32, t0:t0 + NB], writes=[vrs])
                    s.dma("sp", vfl.v3(4, NB), VF[:, :, t0:t0 + NB].rearrange("c p t -> p c t"), writes=[vfl])
                    s.dve(CP(vr16.ap[0:32, :], vrs.ap[0:32, :]), [vrs], [vr16])
                    pv = [bank(), bank()]
                    for kc in range(4):
                        s.pe(MM(pv[kc // 2].ap[:, (kc % 2) * NB:(kc % 2 + 1) * NB], vup16.ap[0:32, kc * 128:(kc + 1) * 128],
                                vr16.ap[0:32, :]), [vup16, vr16], [pv[kc // 2]])
                    for kc in range(4):
                        sl = slice(kc * NB, (kc + 1) * NB)
                        psl = slice((kc % 2) * NB, (kc % 2 + 1) * NB)
                        s.act(ACTF(t1.ap[:, sl], pv[kc // 2].ap[:, psl], AF.Sigmoid,
                                   bias=cvf[:, K_V0 + kc:K_V0 + kc + 1]), [pv[kc // 2], cols], [t1])
                    s.pool(TT(vfl.ap, vfl.ap, V_, ALU.subtract), [vfl] + rV, [vfl])
                    s.dve(TT(vfl.ap, vfl.ap, t1.ap, ALU.mult), [vfl, t1], [vfl])
                    s.pool(TT(V_, V_, vfl.ap, ALU.add), [vfl] + rV, rV)
                for kc in range(4):
                    sl = slice(kc * NB, (kc + 1) * NB)
                    s.dve(SCAN(bcs.ap[:, sl], scanmask, lw.ap[:, sl]), [lw, cst], [bcs])
                s.dve(TT(t2.ap, bcs.ap, lw.ap, ALU.subtract), [bcs, lw], [t2])
                b4 = bcs.v4(4, NCH, CH)
                s.dve(TT(t3.v4(4, NCH, CH), b4[:, :, :, CH - 1:CH].to_broadcast([128, 4, NCH, CH]), b4, ALU.subtract),
                      [bcs], [t3])
                s.act(ACTF(eb.ap, bcs.ap, AF.Exp), [bcs], [eb])
                s.act(ACTF(enb.ap, bcs.ap, AF.Exp, scale=-1.0), [bcs], [enb])
                s.act(ACTF(ebx.ap, t2.ap, AF.Exp), [t2], [ebx])
                s.act(ACTF(egb.ap, t3.ap, AF.Exp), [t3], [egb])
                for kc in range(4):
                    sl = slice(kc * NB, (kc + 1) * NB)
                    s.act(ACTF(t1.ap[:, sl], K_[:, sl], AF.Square, scale=cvf[:, K_KK + kc:K_KK + kc + 1]),
                          [rK[kc], cols], [t1])
                pq = [bank(), bank()]
                for kc in range(4):
                    s.pe(MM(pq[kc // 2].ap[:, (kc % 2) * NB:(kc % 2 + 1) * NB], blk32, t1.ap[:, kc * NB:(kc + 1) * NB]),
                         [cst, t1], [pq[kc // 2]])
                for hf in range(2):
                    sl = slice(hf * 2 * NB, (hf + 1) * 2 * NB)
                    s.dve(TS(t2.ap[:, sl], pq[hf].ap[:, 0:2 * NB], 1e-24, None, ALU.max), [pq[hf]], [t2])
                s.act(ACTF(t2.ap, t2.ap, AF.Ln), [t2], [t2])
                s.act(ACTF(t3.ap, t2.ap, AF.Exp, scale=-0.5), [t2], [t3])
                for kc in range(4):
                    sl = slice(kc * NB, (kc + 1) * NB)
                    s.dve(STT(kkn.ap[:, sl], K_[:, sl], cvf[:, K_KK + kc:K_KK + kc + 1], t3.ap[:, sl], ALU.mult, ALU.mult),
                          [rK[kc], cols, t3], [kkn])
                    s.pool(TS(t1.ap[:, sl], av.ap[:, sl], cvf[:, K_KA + kc:K_KA + kc + 1],
                              cvf[:, K_OMKA + kc:K_OMKA + kc + 1]), [av, cols], [t1])
                s.dve(TT(kmod.ap, K_, t1.ap, ALU.mult), rK + [t1], [kmod])
                s.dve(TT(uu.ap, av.ap, kkn.ap, ALU.mult), [av, kkn], [uu])
                s.dve(STT(PT16.ap, kkn.ap, -1.0, ebx.ap, ALU.mult, ALU.mult), [kkn, ebx], [PT16])
                for hh_ in range(2):
                    hp_ = slice(hh_ * 64, (hh_ + 1) * 64)
                    s.dve(TT(UTz[hh_].ap[hp_, :], uu.ap[hp_, :], enb.ap[hp_, :], ALU.mult), [uu, enb], [UTz[hh_]])
                    s.dve(TT(KTz[hh_].ap[hp_, :], kmod.ap[hp_, :], enb.ap[hp_, :], ALU.mult), [kmod, enb], [KTz[hh_]])
                    s.act(ACP(PTz[hh_].ap[hp_, :], PT16.ap[hp_, :]), [PT16], [PTz[hh_]])
                s.pool(TT(RT32.ap, R_, eb.ap, ALU.mult), rR + [eb], [RT32])
                s.act(ACP(RT16.ap, RT32.ap), [RT32], [RT16])
                s.dve(TT(UG16.ap, uu.ap, egb.ap, ALU.mult), [uu, egb], [UG16])
                s.dve(TT(KG16.ap, kmod.ap, egb.ap, ALU.mult), [kmod, egb], [KG16])
                s.dve(TT(t2.ap, R_, kmod.ap, ALU.mult), rR + [kmod], [t2])
                for kc in range(4):
                    sl = slice(kc * NB, (kc + 1) * NB)
                    s.act(ACTF(rk16.ap[:, sl], t2.ap[:, sl], AF.Identity, scale=cvf[:, K_RK + kc:K_RK + kc + 1]),
                          [t2, cols], [rk16])

                if CUT == 10:
                    s.barrier(); return
                for tt in range(2):
                    zsl, bon, QT16, MT16, YL32 = zsl_[tt], bon_[tt], QT16_[tt], MT16_[tt], YL32_[tt]
                    N32, ysb, yc, ysq, st8 = N32_[tt], ysb_[tt], yc_[tt], ysq_[tt], st8_[tt]
                    tsl = lambda kc: slice(kc * NB + tt * 128, kc * NB + (tt + 1) * 128)
                    pvt = bank()
                    pzt = bank()
                    for kc in range(4):
                        s.pe(TR(pvt.ap[:, kc * 128:(kc + 1) * 128], V_[:, tsl(kc)], ident), rV + [cst], [pvt])
                    for kc in range(4):
                        s.pe(TR(pzt.ap[:, kc * 128:(kc + 1) * 128], Z_[:, tsl(kc)], ident), rZ + [cst], [pzt])
                    if CUT == 111:
                        s.barrier(); return
                    s.dve(CP(vtm32.ap, pvt.ap[:, 0:512]), [pvt], [vtm32])
                    s.act(ACP(vtm16.ap, pvt.ap[:, 0:512]), [pvt], [vtm16])
                    s.act(ACTF(zsl.ap, pzt.ap[:, 0:512], AF.Silu), [pzt], [zsl])
                    if CUT == 112:
                        s.barrier(); return
                    pst = bank()
                    for kc in range(4):
                        s.pe(MM(pst.ap[:, kc * 128:(kc + 1) * 128], rk16.ap[:, tsl(kc)], blk16.ap), [rk16, blk16], [pst])
                    s.dve(TT(bon.ap, pst.ap[:, 0:512], vtm32.ap, ALU.mult), [pst, vtm32], [bon])

                    if CUT == 11:
                        s.barrier(); return
                    def fm(tile16, h, par):
                        kc, hh = h // 2, h % 2
                        c0 = kc * NB + tt * 128 + par * 64
                        return tile16.ap[hh * 64:(hh + 1) * 64, c0:c0 + 64]

                    def ch(tile16, h, par):
                        kc = h // 2
                        c0 = kc * NB + tt * 128 + par * 64
                        return tile16.ap[:, c0:c0 + 64]

                    ids = lambda hh: id16.ap[:, hh * 64:(hh + 1) * 64]
                    PRv = PR16.ap.rearrange("p (two w) -> p two w", two=2)
                    G = [dict(), dict()]
                    rd = [PT16, RT16] + UTz + KTz + PTz
                    for g in range(2):
                        pX, pY, pZ = bank(), bank(), bank()
                        for par in range(2):
                            ps_ = slice(par * 64, (par + 1) * 64)
                            for hi in range(4):
                                h = g * 4 + hi
                                hh = h % 2
                                c0 = (h // 2) * NB + tt * 128 + par * 64
                                pr = PRv[:, :, c0:c0 + 64]
                                ox = pX.ap[ps_, hi * 128:(hi + 1) * 128].rearrange("p (a b) -> p a b", a=2)
                                oy = pY.ap[ps_, hi * 128:(hi + 1) * 128].rearrange("p (a b) -> p a b", a=2)
                                if MERGE_PR:
                                    s.pe(MM(ox, ch(UTz[hh], h, par), pr), rd, [pX])
                                    s.pe(MM(oy, ch(KTz[hh], h, par), pr), rd, [pY])
                                else:
                                    s.pe(MM(pX.ap[ps_, hi * 128:hi * 128 + 64], ch(UTz[hh], h, par), ch(PT16, h, par)), rd, [pX])
                                    s.pe(MM(pX.ap[ps_, hi * 128 + 64:hi * 128 + 128], ch(UTz[hh], h, par), ch(RT16, h, par)), rd, [pX])
                                    s.pe(MM(pY.ap[ps_, hi * 128:hi * 128 + 64], ch(KTz[hh], h, par), ch(PT16, h, par)), rd, [pY])
                                    s.pe(MM(pY.ap[ps_, hi * 128 + 64:hi * 128 + 128], ch(KTz[hh], h, par), ch(RT16, h, par)), rd, [pY])
                                s.pe(MM(pZ.ap[ps_, hi * 64:(hi + 1) * 64], ch(PTz[hh], h, par), ch(UTz[hh], h, par)), rd, [pZ])
                        G[g]["pA"] = (pX, pY, pZ)
                    for g in range(2):
                        pX, pY, pZ = G[g]["pA"]
                        A = AXY[g]
                        s.dve(TT(A.ap[:, 0:512], pX.ap[:, 0:512], rwmXY.ap, ALU.mult), [pX, rwmXY], [A.r(0)])
                        s.dve(TT(A.ap[:, 512:1024], pY.ap[:, 0:512], rwmXY.ap, ALU.mult), [pY, rwmXY], [A.r(1)])
                        xy = XY[g][0]
                        s.dve(TT(xy.v3(4, 128)[:, :, 0:64], pZ.ap[:, 0:256].rearrange("p (h c) -> p h c", h=4),
                                 rwmZ.v3(4, 64), ALU.mult), [pZ, rwmZ], [xy.r(0)])
                        s.act(ACP(xy.v3(4, 128)[:, :, 64:128], A.ap[:, 0:512].rearrange("p (h c) -> p h c", h=4)[:, :, 0:64]),
                              [A.r(0)], [xy.r(1)])
                    for g in range(2):
                        A = AXY[g]
                        pR = bank()
                        for par in range(2):
                            ps_ = slice(par * 64, (par + 1) * 64)
                            for hi in range(4):
                                h = g * 4 + hi
                                s.pe(MM(pR.ap[ps_, hi * 128:hi * 128 + 64], ch(PT16, h, par), ids(h % 2)), [PT16, id16], [pR])
                                s.pe(MM(pR.ap[ps_, hi * 128 + 64:hi * 128 + 128],
                                        A.ap[ps_, 512 + hi * 128:512 + hi * 128 + 64],
                                        vtm16.ap[ps_, h * 64:(h + 1) * 64]), [A.r(1), vtm16], [pR])
                        G[g]["pR"] = pR
                    for g in range(2):
                        evac(RR[g][0].ap, G[g]["pR"].ap[:, 0:512], [G[g]["pR"]], [RR[g][0]])
                    cur = 0
                    for j in range(6):
                        for g in range(2):
                            xy = XY[g][cur]
                            rr = RR[g][cur]
                            pR = bank()
                            for par in range(2):
                                ps_ = slice(par * 64, (par + 1) * 64)
                                for hi in range(4):
                                    s.pe(MM(pR.ap[ps_, hi * 128:(hi + 1) * 128], xy.ap[ps_, hi * 128 + 64:hi * 128 + 128],
                                            rr.ap[ps_, hi * 128:(hi + 1) * 128]), [xy.r(1), rr], [pR])
                            G[g]["pR"] = pR
                            if j < 5:
                                pS = bank()
                                for par in range(2):
                                    ps_ = slice(par * 64, (par + 1) * 64)
                                    for hi in range(4):
                                        X_ = xy.ap[ps_, hi * 128:hi * 128 + 64]
                                        Y_ = xy.ap[ps_, hi * 128 + 64:hi * 128 + 128]
                                        s.pe(MM(pS.ap[ps_, hi * 128:hi * 128 + 64], Y_, X_), [xy.r(0), xy.r(1)], [pS])
                                        s.pe(MM(pS.ap[ps_, hi * 128 + 64:hi * 128 + 128], X_, Y_), [xy.r(0), xy.r(1)], [pS])
                                G[g]["pS"] = pS
                        for g in range(2):
                            rr = RR[g][cur]
                            rrn = RR[g][1 - cur]
                            xyn = XY[g][1 - cur]
                            if j < 5:
                                s.act(ACP(xyn.ap, G[g]["pS"].ap[:, 0:512]), [G[g]["pS"]], [xyn.r(0), xyn.r(1)])
                            s.dve(TT(rrn.ap, G[g]["pR"].ap[:, 0:512], rr.ap, ALU.add), [G[g]["pR"], rr], [rrn])
                        cur = 1 - cur
                    for g in range(2):
                        pU = bank()
                        for par in range(2):
                            ps_ = slice(par * 64, (par + 1) * 64)
                            for hi in range(4):
                                h = g * 4 + hi
                                s.pe(MM(pU.ap[ps_, hi * 128:hi * 128 + 64], ch(UG16, h, par), ids(h % 2)), [UG16, id16], [pU])
                                s.pe(MM(pU.ap[ps_, hi * 128 + 64:hi * 128 + 128], ch(KG16, h, par), ids(h % 2)), [KG16, id16], [pU])
                        G[g]["pU"] = pU
                    for g in range(2):
                        s.act(ACP(UKG[g].ap, G[g]["pU"].ap[:, 0:512]), [G[g]["pU"]], [UKG[g]])
                    for g in range(2):
                        A = AXY[g]
                        wz = RR[g][cur]
                        ukg = UKG[g]
                        pQs = [bank(), bank()]
                        pNs = [bank(), bank()]
                        pYL = bank()
                        for par in range(2):
                            ps_ = slice(par * 64, (par + 1) * 64)
                            pQ = pQs[par]
                            pN = pNs[par]
                            for hi in range(4):
                                h = g * 4 + hi
                                kcl, hh = hi // 2, hi % 2
                                hb = slice(hh * 64, (hh + 1) * 64)
                                Wm = wz.ap[ps_, hi * 128:hi * 128 + 64]
                                Zm = wz.ap[ps_, hi * 128 + 64:hi * 128 + 128]
                                AruT = A.ap[ps_, hi * 128 + 64:hi * 128 + 128]
                                ArkT = A.ap[ps_, 512 + hi * 128 + 64:512 + hi * 128 + 128]
                                Ug = ukg.ap[ps_, hi * 128:hi * 128 + 64]
                                Kg = ukg.ap[ps_, hi * 128 + 64:hi * 128 + 128]
                                Vm = vtm16.ap[ps_, h * 64:(h + 1) * 64]
                                s.pe(MM(pQ.ap[hb, kcl * 64:(kcl + 1) * 64], Wm, AruT), [wz, A.r(0)], [pQ])
                                s.pe(MM(pQ.ap[hb, 128 + kcl * 64:128 + (kcl + 1) * 64], Wm, Ug), [wz, ukg], [pQ])
                                s.pe(MM(pN.ap[hb, kcl * 64:(kcl + 1) * 64], Ug, Zm, True, False), [ukg, wz], [pN])
                                s.pe(MM(pN.ap[hb, kcl * 64:(kcl + 1) * 64], Kg, Vm, False, True), [ukg, vtm16], [pN])
                                s.pe(MM(pYL.ap[ps_, hi * 64:(hi + 1) * 64], AruT, Zm, True, False), [A.r(0), wz], [pYL])
                                s.pe(MM(pYL.ap[ps_, hi * 64:(hi + 1) * 64], ArkT, Vm, False, True), [A.r(1), vtm16], [pYL])
                        gs = slice(g * 256, (g + 1) * 256)
                        for par in range(2):
                            qo = QT16.ap[:, gs].rearrange("p (k a t) -> p k a t", k=2, a=2)[:, :, par, :]
                            rt = RT32.v3(4, NB)[:, 2 * g:2 * g + 2, tt * 128 + par * 64:tt * 128 + (par + 1) * 64]
                            s.dve(TT(qo, pQs[par].ap[:, 0:128].rearrange("p (k t) -> p k t", k=2), rt, ALU.add),
                                  [pQs[par], RT32], [QT16.r(g)])
                            mo = MT16.ap[:, g * 256 + par * 128:g * 256 + (par + 1) * 128]
                            s.act(ACP(mo, pQs[par].ap[:, 128:256]), [pQs[par]], [MT16.r(g)])
                            no = N32.ap[:, g * 256 + par * 128:g * 256 + (par + 1) * 128]
                            s.act(ACP(no, pNs[par].ap[:, 0:128]), [pNs[par]], [N32.r(g)])
                        s.dve(CP(YL32.ap[:, gs], pYL.ap[:, 0:256]), [pYL], [YL32.r(g)])
                    pYh = [bank(), bank()]
                    Hv = H32.ap[:, 0:256]
                    H16v = H16.ap[:, 0:256]
                    for par in range(2):
                        ps_ = slice(par * 64, (par + 1) * 64)
                        pH = bank()
                        for h in range(8):
                            kc, hh = h // 2, h % 2
                            g, kcl = kc // 2, kc % 2
                            hb = slice(hh * 64, (hh + 1) * 64)
                            qc = g * 256 + kcl * 128 + par * 64
                            mc = g * 256 + par * 128 + kcl * 64
                            Hh = H16v[hb, kc * 64:(kc + 1) * 64]
                            s.pe(MM(pYh[hh].ap[ps_, kc * 64:(kc + 1) * 64], QT16.ap[hb, qc:qc + 64], Hh), [QT16.r(g), H16], [pYh[hh]])
                            s.pe(MM(pH.ap[hb, kc * 64:(kc + 1) * 64], MT16.ap[hb, mc:mc + 64], Hh), [MT16.r(g), H16], [pH])
                        clast = tt * 128 + par * 64 + 63
                        gcb = eb.v3(4, NB)[:, :, clast:clast + 1].to_broadcast([128, 4, 64])
                        s.pool(TT(Ht.v3(4, 64), H32.ap[:, 0:256].rearrange("p (k v) -> p k v", k=4), gcb, ALU.mult), [H32, eb], [Ht])
                        n4 = N32.ap.rearrange("p (g a k v) -> p g a k v", g=2, a=2, k=2)[:, :, par]
                        s.pool(TT(Ht.ap.rearrange("p (g k v) -> p g k v", g=2, k=2), Ht.ap.rearrange("p (g k v) -> p g k v", g=2, k=2),
                                  n4, ALU.add), [Ht, N32.r(0), N32.r(1)], [Ht])
                        s.dve(TT(Hv, pH.ap[:, 0:256], Ht.ap, ALU.add), [pH, Ht], [H32])
                        s.act(ACP(H16v, Hv), [H32], [H16])
                    if CUT == 13:
                        s.barrier(); return
                    for hh in range(2):
                        yo = ysb.ap.rearrange("p (k a v) -> p k a v", k=4, a=2)[:, :, hh, :]
                        yl = YL32.ap.rearrange("p (k a v) -> p k a v", k=4, a=2)[:, :, hh, :]
                        s.dve(TT(yo, pYh[hh].ap[:, 0:256].rearrange("p (k v) -> p k v", k=4), yl, ALU.add),
                              [pYh[hh], YL32.r(0), YL32.r(1)], [ysb])
                    s.dve(RED(st8.ap[:, 0:8], ysb.v3(8, 64)), [ysb], [st8])
                    s.dve(TS(st8.ap[:, 0:8], st8.ap[:, 0:8], -1.0 / 64.0, None, ALU.mult), [st8], [st8])
                    s.dve(TT(yc.v3(8, 64), ysb.v3(8, 64), st8.ap[:, 0:8].unsqueeze(2).to_broadcast([128, 8, 64]), ALU.add),
                          [ysb, st8], [yc])
                    s.pool(TT(ysq.ap, yc.ap, yc.ap, ALU.mult), [yc], [ysq])
                    s.dve(RED(st8.ap[:, 8:16], ysq.v3(8, 64)), [ysq], [st8])
                    s.dve(TS(st8.ap[:, 8:16], st8.ap[:, 8:16], 1.0 / 64.0, GN_EPS), [st8], [st8])
                    s.act(ACTF(st8.ap[:, 8:16], st8.ap[:, 8:16], AF.Sqrt), [st8], [st8])
                    s.dve(RCP(st8.ap[:, 16:24], st8.ap[:, 8:16]), [st8], [st8])
                    s.dve(TT(yc.v3(8, 64), yc.v3(8, 64), st8.ap[:, 16:24].unsqueeze(2).to_broadcast([128, 8, 64]), ALU.mult),
                          [yc, st8], [yc])
                    s.pool(TT(yc.ap, yc.ap, gng_b.ap, ALU.mult), [yc, gng_b], [yc])
                    s.pool(TT(yc.ap, yc.ap, gnb_b.ap, ALU.add), [yc, gnb_b], [yc])
                    s.pool(TT(yc.ap, yc.ap, bon.ap, ALU.add), [yc, bon], [yc])
                    yf = yfin[tt]
                    s.dve(TT(yf.ap, yc.ap, zsl.ap, ALU.mult), [yc, zsl], [yf])
                    s.dma("pool", OM[t0 + tt * 128:t0 + (tt + 1) * 128, 512:1024], yf.ap, reads=[yf])
            s.barrier()

        def pass_C(l, xin, xout):
            ar.reset()
            wA = ar.bf("wA", 8 * 1024)
            wO = ar.bf("wO", 8 * 1024)
            wst = [ar.f32(f"cwst{i}", 1024) for i in range(3)]
            n = 0
            for kc in range(8):
                for which in range(2):
                    st = wst[n % 3]
                    if which == 0:
                        src = (W["w_branch_hg"][l] if kc < 4 else W["w_branch_rw"][l])[(kc % 4) * 128:(kc % 4 + 1) * 128, :]
                        dst = wA.v3(8, 1024)[:, kc, :]
                        dr = wA.r(kc)
                    else:
                        src = W["w_out"][l][kc * 128:(kc + 1) * 128, :]
                        dst = wO.v3(8, 1024)[:, kc, :]
                        dr = wO.r(kc)
                    s.dma(dmaq(), st.ap, src, writes=[st])
                    if n % 2:
                        s.dve(CP(dst, st.ap), [st], [dr])
                    else:
                        s.act(ACP(dst, st.ap), [st], [dr])
                    n += 1
            om = [ar.f32(f"om{i}", 1024) for i in range(2)]
            xt = [ar.f32(f"cx{i}", 1024) for i in range(2)]
            gt = [ar.f32(f"gt{i}", 2048) for i in range(2)]
            omT_ = [ar.bf(f"omT{i}", 1024) for i in range(2)]
            mg_ = [ar.f32(f"mg{i}", 1024) for i in range(2)]
            m2_ = [ar.f32(f"m2{i}", 1024) for i in range(2)]
            mT_ = [ar.bf(f"mT{i}", 1024) for i in range(2)]
            zr_ = [ar.f32(f"zr{i}", 1024) for i in range(2)]
            bst_ = [ar.f32(f"bst{i}", 16) for i in range(2)]
            xo = [ar.f32(f"xo{i}", 1024) for i in range(2)]
            for it in range(T // 128):
                r0 = it * 128
                omT, mg, m2, mT, zr, bst = omT_[it % 2], mg_[it % 2], m2_[it % 2], mT_[it % 2], zr_[it % 2], bst_[it % 2]
                o_ = om[it % 2]
                x_ = xt[it % 2]
                g_ = gt[it % 2]
                s.dma("sp", o_.ap, OM[r0:r0 + 128, :], writes=[o_])
                s.dma("sp", x_.ap, xin[r0:r0 + 128, :], writes=[x_])
                s.dma("sp", g_.ap, PT[r0:r0 + 128, 1024:3072], writes=[g_])
                for half in range(2):
                    pb = bank()
                    for j in range(4):
                        kc = half * 4 + j
                        s.pe(TR(pb.ap[:, j * 128:(j + 1) * 128], o_.ap[:, kc * 128:(kc + 1) * 128], ident), [o_, cst], [pb])
                    evac(omT.ap[:, half * 512:(half + 1) * 512], pb.ap[:, 0:512], [pb], [omT.r(half)])
                s.act(ACTF(g_.ap, g_.ap, AF.Sigmoid), [g_], [g_])
                omv = omT.v3(8, 128)
                for hf in range(2):
                    cs_ = slice(hf * 512, (hf + 1) * 512)
                    pa_ = bank()
                    pb_ = bank()
                    for kc in range(4):
                        s.pe(MM(pa_.ap[:, 0:512], omv[:, kc, :], wA.v3(8, 1024)[:, kc, cs_], kc == 0, kc == 3),
                             [omT.r(0), wA.r(kc)], [pa_])
                    for kc in range(4, 8):
                        s.pe(MM(pb_.ap[:, 0:512], omv[:, kc, :], wA.v3(8, 1024)[:, kc, cs_], kc == 4, kc == 7),
                             [omT.r(1), wA.r(kc)], [pb_])
                    s.dve(TT(mg.ap[:, cs_], pa_.ap[:, 0:512], g_.ap[:, hf * 512:(hf + 1) * 512], ALU.mult), [pa_, g_], [mg.r(hf)])
                    s.dve(TT(m2.ap[:, cs_], pb_.ap[:, 0:512], g_.ap[:, 1024 + hf * 512:1024 + (hf + 1) * 512], ALU.mult),
                          [pb_, g_], [m2.r(hf)])
                    s.dve(TT(mg.ap[:, cs_], mg.ap[:, cs_], m2.ap[:, cs_], ALU.add), [mg.r(hf), m2.r(hf)], [mg.r(hf)])
                for half in range(2):
                    pb = bank()
                    for j in range(4):
                        kc = half * 4 + j
                        s.pe(TR(pb.ap[:, j * 128:(j + 1) * 128], mg.ap[:, kc * 128:(kc + 1) * 128], ident), [mg.r(half), cst], [pb])
                    evac(mT.ap[:, half * 512:(half + 1) * 512], pb.ap[:, 0:512], [pb], [mT.r(half)])
                mv = mT.v3(8, 128)
                for hf in range(2):
                    cs_ = slice(hf * 512, (hf + 1) * 512)
                    po_ = bank()
                    for kc in range(8):
                        s.pe(MM(po_.ap[:, 0:512], mv[:, kc, :], wO.v3(8, 1024)[:, kc, cs_], kc == 0, kc == 7),
                             [mT.r(kc // 4), wO.r(kc)], [po_])
                    s.dve(TT(zr.ap[:, cs_], po_.ap[:, 0:512], gate_b.ap[:, cs_], ALU.mult), [po_, gate_b], [zr.r(hf)])
                    s.dve(STT(zr.ap[:, cs_], x_.ap[:, cs_], DN_ALPHA, zr.ap[:, cs_], ALU.mult, ALU.add), [x_, zr.r(hf)], [zr.r(hf)])
                    s.dve(BNS(bst.ap[:, hf * 6:(hf + 1) * 6], zr.ap[:, cs_]), [zr.r(hf)], [bst])
                s.dve(BNA(bst.ap[:, 12:14], bst.ap[:, 0:12]), [bst], [bst])
                s.dve(TS(bst.ap[:, 14:15], bst.ap[:, 13:14], LN_EPS, None, ALU.add), [bst], [bst])
                s.act(ACTF(bst.ap[:, 14:15], bst.ap[:, 14:15], AF.Sqrt), [bst], [bst])
                s.dve(RCP(bst.ap[:, 15:16], bst.ap[:, 14:15]), [bst], [bst])
                xo_ = xo[it % 2]
                s.dve(TS(xo_.ap, zr.ap, bst.ap[:, 12:13], bst.ap[:, 15:16], ALU.subtract, ALU.mult),
                      [zr.r(0), zr.r(1), bst], [xo_])
                s.dve(TT(xo_.ap, xo_.ap, lng_b.ap, ALU.mult), [xo_, lng_b], [xo_])
                s.dve(TT(xo_.ap, xo_.ap, lnb_b.ap, ALU.add), [xo_, lnb_b], [xo_])
                s.dma("pool", xout[r0:r0 + 128, :], xo_.ap, reads=[xo_])
            s.barrier()

        s.barrier()
        stages = []
        for l in range(NL):
            xin = x_in if l == 0 else X1
            xout = out if l == NL - 1 else X1
            stages += [("setup", l), ("A", l), ("B1", l), ("B2", l), ("C", l)]
        for name, l in stages:
            xin = x_in if l == 0 else X1
            xout = out if l == NL - 1 else X1
            if name == "setup":
                layer_setup(l)
            elif name == "A":
                pass_A(l, xin)
            elif name == "B1":
                pass_B1(l)
            elif name == "B2":
                pass_B2(l)
            else:
                pass_C(l, xin, xout)
            if stop_after == (name, l):
                break
        s.barrier()
        s.emit()
    return nc, s


_CACHE = {}


def _prep_weights(inputs):
    m = {}
    for n in WEIGHT_NAMES:
        a = np.ascontiguousarray(np.asarray(inputs[n], dtype=np.float32))
        m[n] = a.reshape(WEIGHT_SHAPES[n])
    m["consts"] = make_consts()
    return m


def kernel(**inputs):
    x = np.asarray(inputs["x"], dtype=np.float32)
    c = np.asarray(inputs["c"], dtype=np.float32)
    B, T, _ = x.shape
    key = (T,)
    if key not in _CACHE:
        _CACHE[key] = build(T)[0]
    nc = _CACHE[key]
    wm = _prep_weights(inputs)
    in_maps = []
    for core in range(8):
        b = core % B
        mm = dict(wm)
        mm["x"] = np.ascontiguousarray(x[b])
        mm["c"] = np.ascontiguousarray(c[b:b + 1])
        in_maps.append(mm)
    res = run_bass_kernel_spmd(nc, in_maps, core_ids=list(range(8)))
    return np.stack([res.results[b]["out"] for b in range(B)], axis=0).astype(np.float32)
```

```python
import contextlib
import math
import numpy as np
import concourse.bass as bass
import concourse.mybir as mybir
from concourse.bass_utils import run_bass_kernel_spmd

F32 = mybir.dt.float32
BF16 = mybir.dt.bfloat16
AF = mybir.ActivationFunctionType
ALU = mybir.AluOpType
AX = mybir.AxisListType

ENGS = ("pe", "dve", "act", "pool", "sp")
EPOCH = 7000
NDMASEM = 8

D = 1024
NIN = 6272
DN_ALPHA = 4.0 ** 0.25
LN_EPS = 1e-5
RMS_EPS = 1e-6
GN_EPS = 64e-5
NB = 256
import os
CUT = int(os.environ.get('KCUT', '0'))
SKIP = int(os.environ.get('KSKIP', '0'))
SKIPM = int(os.environ.get('KSKIPM', '0'))
KV = int(os.environ.get('KV', '0'))
MERGE_PR = int(os.environ.get('KMERGE', '1'))
CH = 64


class Res:
    __slots__ = ("last_w", "readers", "name", "excl")

    def __init__(self, name="", excl=False):
        self.last_w = None
        self.readers = []
        self.name = name
        self.excl = excl


class Tl:
    def __init__(self, ap, name):
        self.ap = ap
        self.res = Res(name)
        self.name = name
        self.sub = {}

    def v3(self, a, b):
        return self.ap.rearrange("p (a b) -> p a b", a=a, b=b)

    def v4(self, a, b, c):
        return self.ap.rearrange("p (a b c) -> p a b c", a=a, b=b, c=c)

    def r(self, key):
        x = self.sub.get(key)
        if x is None:
            x = Res(f"{self.name}:{key}")
            self.sub[key] = x
        return x


class Sched:
    def __init__(self, nc, stack):
        self.nc = nc
        self.stack = stack
        self.q = {e: [] for e in ENGS}
        self.cnt = {e: 0 for e in ENGS}
        self.seen = {e: {} for e in ENGS}
        self.esem = {}
        self.dsem = {}
        self.dcnt = {}
        self.dnext = {e: 0 for e in ENGS}
        self.same_engine_sync = {"pe": False, "dve": True, "act": True, "pool": True, "sp": True}
        self.nops = 0

    def sem(self, name):
        return self.stack.enter_context(self.nc.semaphore(name))

    def R(self, obj):
        if isinstance(obj, Res):
            return obj
        return obj.res

    def _esem(self, eng, count):
        ep = (count - 1) // EPOCH
        k = (eng, ep)
        if k not in self.esem:
            self.esem[k] = self.sem(f"e_{eng}_{ep}")
        return self.esem[k], count - ep * EPOCH

    def _add_wait(self, waits, eng, dep):
        key, val = dep
        if key[0] == "eng" and key[1] == eng and not self.same_engine_sync[eng]:
            return
        if self.seen[eng].get(key, 0) >= val:
            return
        if val > waits.get(key, 0):
            waits[key] = val

    def _lower_waits(self, eng, waits):
        wl = []
        for key, val in waits.items():
            self.seen[eng][key] = val
            if key[0] == "eng":
                wl.append(self._esem(key[1], val))
            else:
                wl.append((self.dsem[(key[1], key[2])], val))
        return wl

    def op(self, eng, fn, reads=(), writes=(), dma=False):
        reads = [self.R(r) for r in reads]
        writes = [self.R(w) for w in writes]
        ex = [r for r in reads if r.excl]
        if ex:
            reads = [r for r in reads if not r.excl]
            writes = writes + [r for r in ex if r not in writes]
        waits = {}
        for r in reads:
            if r.last_w is not None:
                self._add_wait(waits, eng, r.last_w)
        for w in writes:
            if w.last_w is not None:
                self._add_wait(waits, eng, w.last_w)
            for rd in w.readers:
                self._add_wait(waits, eng, rd)
        if dma:
            idx = self.dnext[eng]
            self.dnext[eng] = (idx + 1) % NDMASEM
            dk = (eng, idx)
            if dk not in self.dsem:
                self.dsem[dk] = self.sem(f"d_{eng}_{idx}")
                self.dcnt[dk] = 0
            key = ("dma", eng, idx)
            if self.dcnt[dk] > 0:
                self._add_wait(waits, eng, (key, self.dcnt[dk] * 16))
            self.dcnt[dk] += 1
            done = (key, self.dcnt[dk] * 16)
            inc = (self.dsem[dk], 16)
        else:
            self.cnt[eng] += 1
            c = self.cnt[eng]
            done = (("eng", eng), c)
            inc = (self._esem(eng, c)[0], 1)
        wl = self._lower_waits(eng, waits)
        self.q[eng].append((fn, wl, inc))
        self.nops += 1
        for r in reads:
            r.readers.append(done)
            if len(r.readers) > 64:
                best = {}
                for k, v in r.readers:
                    if v > best.get(k, 0):
                        best[k] = v
                r.readers = list(best.items())
        for w in writes:
            w.last_w = done
            w.readers = []
        return done

    def pe(self, fn, reads=(), writes=()):
        return self.op("pe", fn, reads, writes)

    def dve(self, fn, reads=(), writes=()):
        return self.op("dve", fn, reads, writes)

    def act(self, fn, reads=(), writes=()):
        return self.op("act", fn, reads, writes)

    def pool(self, fn, reads=(), writes=()):
        return self.op("pool", fn, reads, writes)

    def dma(self, eng, out, in_, reads=(), writes=(), **kw):
        return self.op(eng, lambda e: e.dma_start(out=out, in_=in_, **kw), reads, writes, dma=True)

    def wait_all(self, eng, deps):
        waits = {}
        for d in deps:
            self._add_wait(waits, eng, d)
        wl = self._lower_waits(eng, waits)
        if wl:
            self.q[eng].append((None, wl, None))

    def barrier(self):
        deps = [(("eng", e), self.cnt[e]) for e in ENGS if self.cnt[e] > 0]
        deps += [(("dma", k[0], k[1]), v * 16) for k, v in self.dcnt.items()]
        for e in ENGS:
            self.wait_all(e, deps)

    def emit(self):
        nc = self.nc
        with nc.Block() as block:
            def run(eng_name):
                def body(eng):
                    for fn, wl, inc in self.q[eng_name]:
                        for sm, v in wl:
                            eng.wait_ge(sm, v)
                        if fn is not None:
                            fn(eng).then_inc(inc[0], inc[1])
                return body
            block.sync(run("sp"))
            block.tensor(run("pe"))
            block.vector(run("dve"))
            block.scalar(run("act"))
            block.gpsimd(run("pool"))


def MM(out, lhsT, rhs, st=True, sp=True):
    return lambda e: e.matmul(out, lhsT, rhs, start=st, stop=sp)


def TR(out, in_, ident):
    return lambda e: e.transpose(out, in_, ident)


def ACTF(out, in_, func, bias=0.0, scale=1.0):
    return lambda e: e.activation(out=out, in_=in_, func=func, bias=bias, scale=scale)


def CP(out, in_):
    return lambda e: e.tensor_copy(out=out, in_=in_)


def ACP(out, in_):
    return lambda e: e.copy(out=out, in_=in_)


def TT(out, a, b, op):
    return lambda e: e.tensor_tensor(out=out, in0=a, in1=b, op=op)


def TS(out, in0, s1, s2=None, op0=ALU.mult, op1=ALU.add):
    if s2 is None:
        return lambda e: e.tensor_scalar(out=out, in0=in0, scalar1=s1, scalar2=None, op0=op0)
    return lambda e: e.tensor_scalar(out=out, in0=in0, scalar1=s1, scalar2=s2, op0=op0, op1=op1)


def STT(out, in0, scalar, in1, op0, op1):
    return lambda e: e.scalar_tensor_tensor(out=out, in0=in0, scalar=scalar, in1=in1, op0=op0, op1=op1)


def MS(ap, v):
    return lambda e: e.memset(ap, v)


def RED(out, in_, op=ALU.add):
    return lambda e: e.tensor_reduce(out=out, in_=in_, axis=AX.X, op=op)


def RCP(out, in_):
    return lambda e: e.reciprocal(out=out, in_=in_)


def SCAN(out, d0, d1):
    return lambda e: e.tensor_tensor_scan(out=out, data0=d0, data1=d1, initial=0.0, op0=ALU.mult, op1=ALU.add)


def BNS(out, in_):
    return lambda e: e.bn_stats(out=out, in_=in_)


def BNA(out, in_):
    return lambda e: e.bn_aggr(out=out, in_=in_)


C_ID = 0
C_BLK = 128
C_HSEL = 256
C_LE = 258
C_LT = 322
C_GT = 386
C_SM = 450
C_N = 450 + 256


def make_consts():
    cst = np.zeros((128, C_N), np.float32)
    p = np.arange(128)[:, None]
    j = np.arange(128)[None, :]
    cst[:, C_ID:C_ID + 128] = (p == j)
    cst[:, C_BLK:C_BLK + 128] = (p // 64 == j // 64)
    cst[:, C_HSEL:C_HSEL + 2] = (p // 64 == np.arange(2)[None, :])
    j64 = np.arange(64)[None, :]
    cst[:, C_LE:C_LE + 64] = ((p % 64) <= j64)
    cst[:, C_LT:C_LT + 64] = ((p % 64) < j64)
    cst[:, C_GT:C_GT + 64] = ((p % 64) > j64)
    cst[:, C_SM:C_SM + 256] = ((np.arange(256)[None, :] % 64) != 0)
    return cst


class Arena:
    def __init__(self, tensor, words):
        self.t = tensor
        self.words = words
        self.off = 0
        self.gen = 0

    def reset(self):
        self.off = 0
        self.gen += 1

    def f32(self, name, cols, parts=128):
        assert self.off + cols <= self.words, (name, self.off, cols, self.words)
        ap = self.t[0:parts, self.off:self.off + cols]
        self.off += (cols + 15) // 16 * 16
        return Tl(ap, f"{name}_{self.gen}")

    def bf(self, name, cols, parts=128):
        w = (cols + 1) // 2
        assert self.off + w <= self.words, (name, self.off, w, self.words)
        ap = self.t[0:parts, self.off:self.off + w].bitcast(BF16)[:, 0:cols]
        self.off += (w + 15) // 16 * 16
        return Tl(ap, f"{name}_{self.gen}")


WEIGHT_NAMES = ["w_ada", "b_ada", "w_in", "hg_lower_bounds", "hg_norm_g", "rw_mu", "rw_w0", "rw_w_up",
                "rw_a0", "rw_a_up", "rw_k_k", "rw_k_a", "rw_r_k", "rw_v0", "rw_v_down", "rw_v_up",
                "rw_gn_g", "rw_gn_b", "w_branch_hg", "w_branch_rw", "w_out", "ln_g", "ln_b"]
WEIGHT_SHAPES = {
    "w_ada": [2, 1024, 3072], "b_ada": [2, 3072], "w_in": [2, 1024, NIN], "hg_lower_bounds": [2, 512],
    "hg_norm_g": [2, 512], "rw_mu": [2, 2176], "rw_w0": [2, 512], "rw_w_up": [2, 64, 512], "rw_a0": [2, 512],
    "rw_a_up": [2, 64, 512], "rw_k_k": [2, 512], "rw_k_a": [2, 512], "rw_r_k": [2, 512], "rw_v0": [1, 512],
    "rw_v_down": [1, 1024, 32], "rw_v_up": [1, 32, 512], "rw_gn_g": [2, 512], "rw_gn_b": [2, 512],
    "w_branch_hg": [2, 512, 1024], "w_branch_rw": [2, 512, 1024], "w_out": [2, 1024, 1024],
    "ln_g": [2, 1024], "ln_b": [2, 1024],
}

ARENA_WORDS = 45500


def build(T, NL=2, dbg=False, stop_after=None):
    assert T % NB == 0
    NBLK = T // NB
    nc = bass.Bass("TRN2", target_bir_lowering=False)
    x_in = nc.dram_tensor("x", [T, D], F32, kind="ExternalInput").ap()
    c_in = nc.dram_tensor("c", [1, D], F32, kind="ExternalInput").ap()
    cst_in = nc.dram_tensor("consts", [128, C_N], F32, kind="ExternalInput").ap()
    W = {n: nc.dram_tensor(n, WEIGHT_SHAPES[n], F32, kind="ExternalInput").ap() for n in WEIGHT_NAMES}
    out = nc.dram_tensor("out", [T, D], F32, kind="ExternalOutput").ap()
    sk = "ExternalOutput" if dbg else "Internal"
    PF = nc.dram_tensor("PF", [26, 128, T], F32, kind=sk).ap()
    PT = nc.dram_tensor("PT", [T, 3072], F32, kind=sk).ap()
    OM = nc.dram_tensor("OM", [T, 1024], F32, kind=sk).ap()
    VF = nc.dram_tensor("VF", [4, 128, T], F32, kind=sk).ap()
    X1 = nc.dram_tensor("X1", [T, D], F32, kind=sk).ap()

    with contextlib.ExitStack() as stack:
        s = Sched(nc, stack)

        def sbt(name, cols, dtype=F32, parts=128):
            t = stack.enter_context(nc.sbuf_tensor(name, [128, cols], dtype))
            return Tl(t[0:parts, :], name)

        arena_t = stack.enter_context(nc.sbuf_tensor("arena", [128, ARENA_WORDS], F32))
        ar = Arena(arena_t, ARENA_WORDS)
        banks = []
        for i in range(8):
            t = stack.enter_context(nc.psum_tensor(f"bank{i}", [128, 512], F32))
            banks.append(Tl(t[:, :], f"bank{i}"))
            banks[-1].res.excl = True
        bank_i = [0]

        def bank():
            b = banks[bank_i[0] % 8]
            bank_i[0] += 1
            return b

        def drain(gens):
            gens = list(gens)
            while gens:
                nxt = []
                for g_ in gens:
                    try:
                        next(g_)
                        nxt.append(g_)
                    except StopIteration:
                        pass
                gens = nxt

        dq = [0]

        def dmaq():
            dq[0] += 1
            return "sp" if dq[0] % 2 else "pool"

        ev = [0]

        def evac(out_ap, in_ap, reads, writes):
            ev[0] += 1
            if ev[0] % 2:
                return s.dve(CP(out_ap, in_ap), reads, writes)
            return s.act(ACP(out_ap, in_ap), reads, writes)

        cst = sbt("cst", C_N)
        s.dma("sp", cst.ap, cst_in, writes=[cst])
        ident = cst.ap[:, C_ID:C_ID + 128]
        id16 = sbt("id16", 128, BF16)
        s.dve(CP(id16.ap, ident), [cst], [id16])
        hsel16 = sbt("hsel16", 2, BF16)
        s.dve(CP(hsel16.ap, cst.ap[:, C_HSEL:C_HSEL + 2]), [cst], [hsel16])
        blk32 = cst.ap[:, C_BLK:C_BLK + 128]
        blk16 = sbt("blk16", 128, BF16)
        s.dve(CP(blk16.ap, cst.ap[:, C_BLK:C_BLK + 128]), [cst], [blk16])
        scanmask = cst.ap[:, C_SM:C_SM + 256]
        hgmask = sbt("hgmask", 256)
        rwmXY = sbt("rwmXY", 512)
        rwmZ = sbt("rwmZ", 256)
        for h in range(4):
            s.pool(CP(hgmask.ap[:, h * 64:(h + 1) * 64], cst.ap[:, C_LE:C_LE + 64]), [cst], [hgmask])
            s.pool(CP(rwmXY.ap[:, h * 128:h * 128 + 64], cst.ap[:, C_LT:C_LT + 64]), [cst], [rwmXY])
            s.pool(CP(rwmXY.ap[:, h * 128 + 64:h * 128 + 128], cst.ap[:, C_LE:C_LE + 64]), [cst], [rwmXY])
            s.pool(CP(rwmZ.ap[:, h * 64:(h + 1) * 64], cst.ap[:, C_GT:C_GT + 64]), [cst], [rwmZ])

        cols = sbt("cols", 96)
        K_SH, K_SC = 0, 8
        K_LB, K_OMLB, K_NOMLB = 16, 20, 24
        K_W0, K_A0, K_KK, K_KA, K_OMKA, K_RK, K_V0 = 28, 32, 36, 40, 44, 48, 52
        K_MU, K_OMU = 56, 73
        tmp_cols = sbt("tmp_cols", 64)
        gate_b = sbt("gate_b", 1024)
        hgg_b = sbt("hgg_b", 512)
        gng_b = sbt("gng_b", 512)
        gnb_b = sbt("gnb_b", 512)
        lng_b = sbt("lng_b", 1024)
        lnb_b = sbt("lnb_b", 1024)
        lora16 = sbt("lora16", 512, BF16)
        vup16 = sbt("vup16", 512, BF16)

        def colload(dst_c, src_ap_1d, n):
            s.dma(dmaq(), cols.ap[:, dst_c:dst_c + n], src_ap_1d.rearrange("(k p) -> p k", p=128),
                  writes=[cols], allow_slow_non_contiguous=True)

        def layer_setup(l):
            ar.reset()
            colload(K_W0, W["rw_w0"][l], 4)
            colload(K_A0, W["rw_a0"][l], 4)
            colload(K_KK, W["rw_k_k"][l], 4)
            colload(K_KA, W["rw_k_a"][l], 4)
            colload(K_RK, W["rw_r_k"][l], 4)
            colload(K_MU, W["rw_mu"][l], 17)
            if l >= 1:
                colload(K_V0, W["rw_v0"][l - 1], 4)
            s.dve(TS(cols.ap[:, K_OMU:K_OMU + 17], cols.ap[:, K_MU:K_MU + 17], -1.0, 1.0), [cols], [cols])
            s.dve(TS(cols.ap[:, K_OMKA:K_OMKA + 4], cols.ap[:, K_KA:K_KA + 4], -1.0, 1.0), [cols], [cols])
            if l == 0:
                s.dve(MS(cols.ap[:, K_LB:K_LB + 4], 0.0), [], [cols])
            else:
                s.dma(dmaq(), tmp_cols.ap[:, 0:4], W["hg_lower_bounds"][0].rearrange("(k p) -> p k", p=128),
                      writes=[tmp_cols], allow_slow_non_contiguous=True)
                s.dma(dmaq(), tmp_cols.ap[:, 4:8], W["hg_lower_bounds"][1].rearrange("(k p) -> p k", p=128),
                      writes=[tmp_cols], allow_slow_non_contiguous=True)
                s.dve(TT(tmp_cols.ap[:, 8:12], tmp_cols.ap[:, 4:8], tmp_cols.ap[:, 0:4], ALU.subtract),
                      [tmp_cols], [tmp_cols])
                s.act(ACTF(cols.ap[:, K_LB:K_LB + 4], tmp_cols.ap[:, 8:12], AF.Sigmoid), [tmp_cols], [cols])
            s.dve(TS(cols.ap[:, K_OMLB:K_OMLB + 4], cols.ap[:, K_LB:K_LB + 4], -1.0, 1.0), [cols], [cols])
            s.dve(TS(cols.ap[:, K_NOMLB:K_NOMLB + 4], cols.ap[:, K_OMLB:K_OMLB + 4], -1.0), [cols], [cols])
            if CUT == 1:
                s.barrier(); return
            s.dma(dmaq(), hgg_b.ap, W["hg_norm_g"][l].partition_broadcast(128), writes=[hgg_b])
            s.dma(dmaq(), gng_b.ap, W["rw_gn_g"][l].partition_broadcast(128), writes=[gng_b])
            s.dma(dmaq(), gnb_b.ap, W["rw_gn_b"][l].partition_broadcast(128), writes=[gnb_b])
            s.dma(dmaq(), lng_b.ap, W["ln_g"][l].partition_broadcast(128), writes=[lng_b])
            s.dma(dmaq(), lnb_b.ap, W["ln_b"][l].partition_broadcast(128), writes=[lnb_b])
            if CUT == 2:
                s.barrier(); return
            st = ar.f32("lst", 512)
            s.dma(dmaq(), st.ap[0:64, :], W["rw_w_up"][l], writes=[st])
            s.dma(dmaq(), st.ap[64:128, :], W["rw_a_up"][l], writes=[st])
            s.dve(CP(lora16.ap, st.ap), [st], [lora16])
            if l >= 1:
                st2 = ar.f32("lst2", 512)
                s.dma(dmaq(), st2.ap[0:32, :], W["rw_v_up"][l - 1], writes=[st2])
                s.dve(CP(vup16.ap[0:32, :], st2.ap[0:32, :]), [st2], [vup16])
            if CUT == 3:
                s.barrier(); return
            ccol = ar.f32("ccol", 8)
            s.dma(dmaq(), ccol.ap, c_in[0].rearrange("(k p) -> p k", p=128), writes=[ccol],
                  allow_slow_non_contiguous=True)
            scol = ar.f32("scol", 8)
            s.act(ACTF(scol.ap, ccol.ap, AF.Silu), [ccol], [scol])
            srep = ar.f32("srep", 8 * 128)
            s.dve(CP(srep.v3(8, 128), scol.ap.unsqueeze(2).to_broadcast([128, 8, 128])), [scol], [srep])
            bcol = ar.f32("bcol", 24)
            s.dma(dmaq(), bcol.ap, W["b_ada"][l].rearrange("(k p) -> p k", p=128), writes=[bcol],
                  allow_slow_non_contiguous=True)
            brow = ar.f32("brow", 3072)
            s.dma(dmaq(), brow.ap, W["b_ada"][l].partition_broadcast(128), writes=[brow])
            crow = ar.f32("crow", 2048)
            ctmp = ar.f32("ctmp", 1024)
            if CUT == 4:
                s.barrier(); return
            wst = [ar.f32(f"adaw{i}", 8 * 512) for i in range(2)]
            for cb in range(6):
                wt = wst[cb % 2]
                s.dma(dmaq(), wt.v3(8, 512),
                      W["w_ada"][l][:, cb * 512:(cb + 1) * 512].rearrange("(k p) c -> p k c", p=128), writes=[wt])
                pb = bank()
                for kc in range(8):
                    s.pe(MM(pb.ap[:, 0:512], srep.v3(8, 128)[:, kc, :], wt.v3(8, 512)[:, kc, :],
                            kc == 0, kc == 7), [wt, srep], [pb])
                if cb < 4:
                    s.dve(TT(crow.ap[:, cb * 512:(cb + 1) * 512], pb.ap[:, 0:512], brow.ap[:, cb * 512:(cb + 1) * 512],
                             ALU.add), [pb, brow], [crow])
                else:
                    g0 = (cb - 4) * 512
                    s.dve(TT(gate_b.ap[:, g0:g0 + 512], pb.ap[:, 0:512], brow.ap[:, cb * 512:(cb + 1) * 512], ALU.add),
                          [pb, brow], [gate_b])
            for which in range(2):
                s.dve(TT(ctmp.v3(8, 128), crow.ap[:, which * 1024:(which + 1) * 1024].rearrange("p (k c) -> p k c", k=8),
                         ident.unsqueeze(1).to_broadcast([128, 8, 128]), ALU.mult), [crow, cst], [ctmp])
                dstc = K_SH if which == 0 else K_SC
                s.dve(RED(cols.ap[:, dstc:dstc + 8], ctmp.v3(8, 128)), [ctmp], [cols])
            s.dve(TS(cols.ap[:, K_SC:K_SC + 8], cols.ap[:, K_SC:K_SC + 8], 1.0, None, ALU.add), [cols], [cols])
            s.dve(TS(gate_b.ap, gate_b.ap, 1.0, None, ALU.add), [gate_b], [gate_b])
            s.barrier()

        def fmcol(ci):
            if ci < 4:
                return ci * 128
            if ci < 8:
                return 512 + (ci - 4) * 128
            return 2048 + (ci - 8) * 128

        def tmcol(g):
            return (1024, 1536, 4224, 4736, 5248, 5760)[g]

        def pass_A(l, xin):
            ar.reset()
            w16 = ar.bf("w16", 8 * NIN)
            w16v = w16.v3(8, NIN)
            hT = [ar.bf(f"hT{i}", 8 * NB) for i in range(2)]
            xt = [ar.f32(f"xt{i}", 1024) for i in range(2)]
            wst = [ar.f32(f"wst{i}", 1568) for i in range(3)]
            fst = [ar.f32(f"fst{i}", 512) for i in range(3)]
            tst = [ar.f32(f"tst{i}", 3072) for i in range(2)]
            vd16 = ar.bf("vd16", 8 * 32) if l >= 1 else None
            n = 0
            for kc in range(8):
                for qd in range(4):
                    st = wst[n % 3]
                    s.dma(dmaq(), st.ap, W["w_in"][l][kc * 128:(kc + 1) * 128, qd * 1568:(qd + 1) * 1568],
                          writes=[st])
                    dst = w16v[:, kc, qd * 1568:(qd + 1) * 1568]
                    eng = ("dve", "act", "pool")[n % 3]
                    if eng == "act":
                        s.act(ACP(dst, st.ap), [st], [w16.r((kc, qd))])
                    else:
                        s.op(eng, CP(dst, st.ap), [st], [w16.r((kc, qd))])
                    n += 1
            if l >= 1:
                vst = ar.f32("vst", 8 * 32)
                s.dma(dmaq(), vst.v3(8, 32), W["rw_v_down"][l - 1].rearrange("(k p) c -> p k c", p=128),
                      writes=[vst])
                s.dve(CP(vd16.ap, vst.ap), [vst], [vd16])

            def wres(kc, c0, c1):
                return [w16.r((kc, q)) for q in range(c0 // 1568, (c1 - 1) // 1568 + 1)]

            nfm = 25
            fi = 0
            for blk in range(NBLK):
                t0 = blk * NB
                h = hT[blk % 2]
                hv = h.v3(8, NB)
                for tt in range(2):
                    xtile = xt[tt]
                    s.dma("sp", xtile.ap, xin[t0 + tt * 128:t0 + (tt + 1) * 128, :], writes=[xtile])
                    for half in range(2):
                        pb = bank()
                        for j in range(4):
                            kc = half * 4 + j
                            s.pe(TR(pb.ap[:, j * 128:(j + 1) * 128], xtile.ap[:, kc * 128:(kc + 1) * 128], ident),
                                 [xtile, cst], [pb])
                        for j in range(4):
                            kc = half * 4 + j
                            s.act(ACTF(hv[:, kc, tt * 128:(tt + 1) * 128], pb.ap[:, j * 128:(j + 1) * 128],
                                       AF.Identity, bias=cols.ap[:, K_SH + kc:K_SH + kc + 1],
                                       scale=cols.ap[:, K_SC + kc:K_SC + kc + 1]), [pb, cols], [h])
                for pr in range((nfm + 1) // 2):
                    cis = [ci for ci in (2 * pr, 2 * pr + 1) if ci < nfm]
                    pb = bank()
                    for j, ci in enumerate(cis):
                        c0 = fmcol(ci)
                        for kc in range(8):
                            s.pe(MM(pb.ap[:, j * NB:(j + 1) * NB], w16v[:, kc, c0:c0 + 128], hv[:, kc, :],
                                    kc == 0, kc == 7), wres(kc, c0, c0 + 128) + [h], [pb])
                    st = fst[fi % 3]
                    fi += 1
                    nn = len(cis) * NB
                    evac(st.ap[:, 0:nn], pb.ap[:, 0:nn], [pb], [st])
                    s.dma("pool", PF[cis[0]:cis[0] + len(cis), :, t0:t0 + NB].rearrange("c p t -> p c t"),
                          st.ap[:, 0:nn].rearrange("p (c t) -> p c t", c=len(cis)), reads=[st])
                if l >= 1:
                    pb = bank()
                    for kc in range(8):
                        s.pe(MM(pb.ap[0:32, 0:NB], vd16.v3(8, 32)[:, kc, :], hv[:, kc, :], kc == 0, kc == 7),
                             [vd16, h], [pb])
                    st = fst[fi % 3]
                    fi += 1
                    evac(st.ap[0:32, 0:NB], pb.ap[0:32, 0:NB], [pb], [st])
                    s.dma("pool", PF[25, 0:32, t0:t0 + NB], st.ap[0:32, 0:NB], reads=[st])
                for tt in range(2):
                    ts_ = tst[tt]
                    for g in range(6):
                        pb = bank()
                        c0 = tmcol(g)
                        for kc in range(8):
                            s.pe(MM(pb.ap[:, 0:512], hv[:, kc, tt * 128:(tt + 1) * 128], w16v[:, kc, c0:c0 + 512],
                                    kc == 0, kc == 7), wres(kc, c0, c0 + 512) + [h], [pb])
                        evac(ts_.ap[:, g * 512:(g + 1) * 512], pb.ap[:, 0:512], [pb], [ts_.r(g)])
                    s.dma("pool", PT[t0 + tt * 128:t0 + (tt + 1) * 128, :], ts_.ap,
                          reads=[ts_.r(g) for g in range(6)], writes=[ts_.r(g) for g in range(0)])
            s.barrier()

        def pass_B1(l):
            ar.reset()
            NCH = NB // CH
            S32 = [ar.f32(f"S32_{h}", 128) for h in range(4)]
            S16 = [ar.bf(f"S16_{h}", 128) for h in range(4)]
            for h in range(4):
                s.dve(MS(S32[h].ap, 0.0), [], [S32[h]])
                s.dve(MS(S16[h].ap, 0.0), [], [S16[h]])
            qf = [ar.f32(f"qf{i}", 8 * NB) for i in range(2)]
            tmv = [[ar.f32(f"tmv{i}_{tt}", 1024) for tt in range(2)] for i in range(2)]
            sig_ = [ar.f32(f"sig{i}", 4 * NB) for i in range(2)]
            fv_ = [ar.f32(f"fv{i}", 4 * NB) for i in range(2)]
            lgf_ = [ar.f32(f"lgf{i}", 4 * NB) for i in range(2)]
            khg_ = [ar.f32(f"khg{i}", 4 * NB) for i in range(2)]
            bcs_ = [ar.f32(f"bcs{i}", 4 * NB) for i in range(2)]
            eb_ = [ar.f32(f"eb{i}", 4 * NB) for i in range(2)]
            enb_ = [ar.f32(f"enb{i}", 4 * NB) for i in range(2)]
            kt32_ = [ar.f32(f"kt32{i}", 4 * NB) for i in range(2)]
            q16_ = [ar.bf(f"q16{i}", 4 * NB) for i in range(2)]
            k16_ = [ar.bf(f"k16{i}", 4 * NB) for i in range(2)]
            kh16_ = [ar.bf(f"kh16{i}", 4 * NB) for i in range(2)]
            v16 = [ar.bf(f"v16_{tt}", 512) for tt in range(2)]
            sT16 = [ar.bf(f"sT16_{tt}", 256) for tt in range(2)]
            khtm = [ar.bf(f"khtm_{tt}", 512) for tt in range(2)]
            osb_ = [ar.f32(f"osb{i}", 512) for i in range(2)]
            osq_ = [ar.f32(f"osq{i}", 512) for i in range(2)]
            ss_ = [ar.f32(f"ss{i}", 8) for i in range(2)]
            zs_ = [ar.f32(f"zs{i}", 512) for i in range(2)]
            ofin = [ar.f32(f"ofin{i}", 512) for i in range(2)]
            for blk in range(NBLK):
                t0 = blk * NB
                posts = []
                sig = sig_[blk % 2]
                fv = fv_[blk % 2]
                lgf = lgf_[blk % 2]
                khg = khg_[blk % 2]
                bcs = bcs_[blk % 2]
                eb = eb_[blk % 2]
                enb = enb_[blk % 2]
                kt32 = kt32_[blk % 2]
                q16 = q16_[blk % 2]
                k16 = k16_[blk % 2]
                kh16 = kh16_[blk % 2]
                qt = qf[blk % 2]
                qv = qt.v3(8, NB)
                s.dma("sp", qv, PF[0:8, :, t0:t0 + NB].rearrange("c p t -> p c t"), writes=[qt])
                tm = tmv[blk % 2]
                for tt in range(2):
                    s.dma("sp", tm[tt].ap, PT[t0 + tt * 128:t0 + (tt + 1) * 128, 0:1024], writes=[tm[tt]])
                s.act(ACTF(sig.ap, qt.ap[:, 4 * NB:8 * NB], AF.Sigmoid), [qt], [sig])
                for h in range(4):
                    sl = slice(h * NB, (h + 1) * NB)
                    s.dve(TS(fv.ap[:, sl], sig.ap[:, sl], cols.ap[:, K_OMLB + h:K_OMLB + h + 1],
                             cols.ap[:, K_LB + h:K_LB + h + 1]), [sig, cols], [fv])
                    s.pool(TS(khg.ap[:, sl], sig.ap[:, sl], cols.ap[:, K_NOMLB + h:K_NOMLB + h + 1],
                              cols.ap[:, K_OMLB + h:K_OMLB + h + 1]), [sig, cols], [khg])
                s.act(ACTF(lgf.ap, fv.ap, AF.Ln), [fv], [lgf])
                for h in range(4):
                    sl = slice(h * NB, (h + 1) * NB)
                    s.dve(SCAN(bcs.ap[:, sl], scanmask, lgf.ap[:, sl]), [lgf, cst], [bcs])
                s.act(ACTF(eb.ap, bcs.ap, AF.Exp), [bcs], [eb])
                s.act(ACTF(enb.ap, bcs.ap, AF.Exp, scale=-1.0), [bcs], [enb])
                s.dve(TT(q16.ap, qt.ap[:, 0:4 * NB], eb.ap, ALU.mult), [qt, eb], [q16])
                s.dve(TT(kt32.ap, khg.ap, enb.ap, ALU.mult), [khg, enb], [kt32])
                s.act(ACP(k16.ap, kt32.ap), [kt32], [k16])
                e4 = eb.v4(4, NCH, CH)
                s.dve(TT(kh16.v4(4, NCH, CH), kt32.v4(4, NCH, CH), e4[:, :, :, CH - 1:CH].to_broadcast([128, 4, NCH, CH]),
                         ALU.mult), [kt32, eb], [kh16])
                for tt in range(2):
                    osb, osq, ss, zs = osb_[tt], osq_[tt], ss_[tt], zs_[tt]
                    vt = tm[tt]
                    s.act(ACP(v16[tt].ap, vt.ap[:, 0:512]), [vt], [v16[tt]])
                    pbs = bank()
                    pbk = bank()
                    pbk16 = pbk.ap.bitcast(BF16)
                    for par in range(2):
                        cc = tt * 2 + par
                        ps_ = slice(par * 64, (par + 1) * 64)
                        for h in range(4):
                            c0 = h * NB + cc * CH
                            s.pe(MM(pbs.ap[ps_, h * 64:(h + 1) * 64], k16.ap[:, c0:c0 + CH], q16.ap[:, c0:c0 + CH]),
                                 [k16, q16], [pbs])
                            s.pe(TR(pbk16[ps_, h * 128:(h + 1) * 128], kh16.ap[:, c0:c0 + CH], id16.ap),
                                 [kh16, id16], [pbk])
                    s.dve(TT(sT16[tt].ap, pbs.ap[:, 0:256], hgmask.ap, ALU.mult), [pbs, hgmask], [sT16[tt]])
                    s.act(ACP(khtm[tt].ap, pbk16[:, 0:512]), [pbk], [khtm[tt]])
                    pbo = bank()
                    for par in range(2):
                        cc = tt * 2 + par
                        ps_ = slice(par * 64, (par + 1) * 64)
                        for h in range(4):
                            c0 = h * NB + cc * CH
                            hs = slice(h * 128, (h + 1) * 128)
                            s.pe(MM(pbo.ap[ps_, hs], sT16[tt].ap[ps_, h * 64:(h + 1) * 64], v16[tt].ap[ps_, hs],
                                    True, False), [sT16[tt], v16[tt]], [pbo])
                            s.pe(MM(pbo.ap[ps_, hs], q16.ap[:, c0:c0 + CH], S16[h].ap, False, True),
                                 [q16, S16[h]], [pbo])
                        pkv = bank()
                        for h in range(4):
                            hs = slice(h * 128, (h + 1) * 128)
                            s.pe(MM(pkv.ap[:, hs], khtm[tt].ap[ps_, hs], v16[tt].ap[ps_, hs]),
                                 [khtm[tt], v16[tt]], [pkv])
                        for h in range(4):
                            c0 = h * NB + cc * CH
                            hs = slice(h * 128, (h + 1) * 128)
                            s.dve(STT(S32[h].ap, S32[h].ap, eb.ap[:, c0 + CH - 1:c0 + CH], pkv.ap[:, hs],
                                      ALU.mult, ALU.add), [S32[h], eb, pkv], [S32[h]])
                            s.act(ACP(S16[h].ap, S32[h].ap), [S32[h]], [S16[h]])
                    s.act(ACP(osb.ap, pbo.ap[:, 0:512]), [pbo], [osb])

                    def post_b1(osb, osq, ss, zs, vt, of, row0):
                        s.pool(TT(osq.ap, osb.ap, osb.ap, ALU.mult), [osb], [osq]); yield
                        s.dve(RED(ss.ap[:, 0:4], osq.v3(4, 128)), [osq], [ss]); yield
                        s.dve(TS(ss.ap[:, 0:4], ss.ap[:, 0:4], 1.0 / 128.0, RMS_EPS), [ss], [ss]); yield
                        s.act(ACTF(ss.ap[:, 0:4], ss.ap[:, 0:4], AF.Sqrt), [ss], [ss]); yield
                        s.dve(RCP(ss.ap[:, 4:8], ss.ap[:, 0:4]), [ss], [ss]); yield
                        s.act(ACTF(zs.ap, vt.ap[:, 512:1024], AF.Silu), [vt], [zs]); yield
                        s.pool(TT(zs.ap, zs.ap, hgg_b.ap, ALU.mult), [zs, hgg_b], [zs]); yield
                        s.dve(TT(of.v3(4, 128), osb.v3(4, 128), ss.ap[:, 4:8].unsqueeze(2).to_broadcast([128, 4, 128]),
                                 ALU.mult), [osb, ss], [of]); yield
                        s.dve(TT(of.ap, of.ap, zs.ap, ALU.mult), [of, zs], [of]); yield
                        s.dma("pool", OM[row0:row0 + 128, 0:512], of.ap, reads=[of]); yield

                    posts.append(post_b1(osb, osq, ss, zs, vt, ofin[tt], t0 + tt * 128))
                drain(posts)
            s.barrier()

        def pass_B2(l):
            ar.reset()
            NCH = NB // CH
            W4 = 4 * NB
            H32 = ar.f32("H32", 512)
            H16 = ar.bf("H16", 512)
            s.dve(MS(H32.ap, 0.0), [], [H32])
            s.dve(MS(H16.ap, 0.0), [], [H16])
            rwl = [ar.f32(f"rwl{i}", 17 * (NB + 1)) for i in range(1)]
            sh = ar.f32("sh", 17 * NB)
            shv = sh.v3(17, NB)
            lw = ar.f32("lw", W4)
            av = ar.f32("av", W4)
            bcs = ar.f32("bcs", W4)
            t1 = ar.f32("t1", W4)
            t2 = ar.f32("t2", W4)
            t3 = ar.f32("t3", W4)
            eb = ar.f32("eb", W4)
            enb = ar.f32("enb", W4)
            ebx = ar.f32("ebx", W4)
            egb = ar.f32("egb", W4)
            kkn = ar.f32("kkn", W4)
            kmod = ar.f32("kmod", W4)
            uu = ar.f32("uu", W4)
            PR16 = ar.bf("PR16", 2 * W4)
            PT16 = Tl(PR16.ap[:, 0:W4], "PT16v")
            RT16 = Tl(PR16.ap[:, W4:2 * W4], "RT16v")
            UT16 = ar.bf("UT16", W4)
            KT16 = ar.bf("KT16", W4)
            UTz = [ar.bf(f"UTz{i}", W4) for i in range(2)]
            KTz = [ar.bf(f"KTz{i}", W4) for i in range(2)]
            PTz = [ar.bf(f"PTz{i}", W4) for i in range(2)]
            for zt in UTz + KTz + PTz:
                s.pool(MS(zt.ap, 0.0), [], [zt])
            UG16 = ar.bf("UG16", W4)
            KG16 = ar.bf("KG16", W4)
            rk16 = ar.bf("rk16", W4)
            lo16 = ar.bf("lo16", NB)
            vr16 = ar.bf("vr16", NB) if l >= 1 else None
            vrs = ar.f32("vrs", NB) if l >= 1 else None
            vfl = ar.f32("vfl", W4) if l >= 1 else None
            vtm32 = ar.f32("vtm32", 512)
            vtm16 = ar.bf("vtm16", 512)
            zsl_ = [ar.f32(f"zsl{i}", 512) for i in range(2)]
            bon_ = [ar.f32(f"bon{i}", 512) for i in range(2)]
            AXY = [ar.bf(f"AXY{g}", 1024) for g in range(2)]
            RT32 = ar.f32("RT32", W4)
            XY = [[ar.bf(f"XY{g}_{i}", 512) for i in range(2)] for g in range(2)]
            RR = [[ar.bf(f"RR{g}_{i}", 512) for i in range(2)] for g in range(2)]
            QT16_ = [ar.bf(f"QT16{i}", 512) for i in range(2)]
            MT16_ = [ar.bf(f"MT16{i}", 512) for i in range(2)]
            UKG = [ar.bf(f"UKG{g}", 512) for g in range(2)]
            YL32_ = [ar.f32(f"YL32{i}", 512) for i in range(2)]
            N32_ = [ar.f32(f"N32{i}", 512) for i in range(2)]
            Ht = ar.f32("Ht", 256)
            ysb_ = [ar.f32(f"ysb{i}", 512) for i in range(2)]
            yc_ = [ar.f32(f"yc{i}", 512) for i in range(2)]
            ysq_ = [ar.f32("ysq", 512)] * 2
            st8_ = [ar.f32(f"st8{i}", 32) for i in range(2)]
            yfin = [ar.f32(f"yfin{i}", 512) for i in range(2)]
            cvf = cols.ap
            for blk in range(NBLK):
                t0 = blk * NB
                posts = []
                rl = rwl[0]
                rlv = rl.v3(17, NB + 1)
                if t0 == 0:
                    s.dve(MS(rlv[:, :, 0:1], 0.0), [], [rl])
                    s.dma("sp", rlv[:, :, 1:NB + 1], PF[8:25, :, 0:NB].rearrange("c p t -> p c t"), writes=[rl])
                else:
                    s.dma("sp", rlv, PF[8:25, :, t0 - 1:t0 + NB].rearrange("c p t -> p c t"), writes=[rl])
                for kc in range(17):
                    s.act(ACTF(shv[:, kc, :], rlv[:, kc, 1:NB + 1], AF.Identity,
                               scale=cvf[:, K_OMU + kc:K_OMU + kc + 1]), [rl, cols], [sh.r(kc)])
                    s.dve(STT(shv[:, kc, :], rlv[:, kc, 0:NB], cvf[:, K_MU + kc:K_MU + kc + 1], shv[:, kc, :],
                              ALU.mult, ALU.add), [rl, cols, sh.r(kc)], [sh.r(kc)])
                rsl = lambda kc: shv[:, kc, :]
                R_ = sh.ap[:, 0:W4]
                K_ = sh.ap[:, W4:2 * W4]
                V_ = sh.ap[:, 2 * W4:3 * W4]
                Z_ = sh.ap[:, 3 * W4:4 * W4]
                rR = [sh.r(kc) for kc in range(0, 4)]
                rK = [sh.r(kc) for kc in range(4, 8)]
                rV = [sh.r(kc) for kc in range(8, 12)]
                rZ = [sh.r(kc) for kc in range(12, 16)]
                s.act(ACTF(lo16.ap[0:64, :], shv[0:64, 16, :], AF.Tanh), [sh.r(16)], [lo16])
                s.dve(CP(lo16.ap[64:128, :], shv[64:128, 16, :]), [sh.r(16)], [lo16])
                pw = [bank(), bank()]
                pa = [bank(), bank()]
                for kc in range(4):
                    s.pe(MM(pw[kc // 2].ap[:, (kc % 2) * NB:(kc % 2 + 1) * NB], lora16.ap[0:64, kc * 128:(kc + 1) * 128],
                            lo16.ap[0:64, :]), [lora16, lo16], [pw[kc // 2]])
                    s.pe(MM(pa[kc // 2].ap[:, (kc % 2) * NB:(kc % 2 + 1) * NB], lora16.ap[64:128, kc * 128:(kc + 1) * 128],
                            lo16.ap[64:128, :]), [lora16, lo16], [pa[kc // 2]])
                for kc in range(4):
                    sl = slice(kc * NB, (kc + 1) * NB)
                    psl = slice((kc % 2) * NB, (kc % 2 + 1) * NB)
                    s.act(ACTF(lw.ap[:, sl], pw[kc // 2].ap[:, psl], AF.Sigmoid, bias=cvf[:, K_W0 + kc:K_W0 + kc + 1]),
                          [pw[kc // 2], cols], [lw])
                    s.act(ACTF(av.ap[:, sl], pa[kc // 2].ap[:, psl], AF.Sigmoid, bias=cvf[:, K_A0 + kc:K_A0 + kc + 1]),
                          [pa[kc // 2], cols], [av])
                s.dve(TS(lw.ap, lw.ap, -math.exp(-0.5), None, ALU.mult), [lw], [lw])
                if l == 0:
                    s.dma("pool", VF[:, :, t0:t0 + NB].rearrange("c p t -> p c t"), V_.rearrange("p (c t) -> p c t", c=4),
                          reads=rV)
                else:
                    s.dma("sp", vrs.ap[0:32, :], PF[25, 0:32, t0:t0 + NB], writes=[vrs])
                    s.dma("sp", vfl.v3(4, NB), VF[:, :, t0:t0 + NB].rearrange("c p t -> p c t"), writes=[vfl])
                    s.dve(CP(vr16.ap[0:32, :], vrs.ap[0:32, :]), [vrs], [vr16])
                    pv = [bank(), bank()]
                    for kc in range(4):
                        s.pe(MM(pv[kc // 2].ap[:, (kc % 2) * NB:(kc % 2 + 1) * NB], vup16.ap[0:32, kc * 128:(kc + 1) * 128],
                                vr16.ap[0:32, :]), [vup16, vr16], [pv[kc // 2]])
                    for kc in range(4):
                        sl = slice(kc * NB, (kc + 1) * NB)
                        psl = slice((kc % 2) * NB, (kc % 2 + 1) * NB)
                        s.act(ACTF(t1.ap[:, sl], pv[kc // 2].ap[:, psl], AF.Sigmoid,
                                   bias=cvf[:, K_V0 + kc:K_V0 + kc + 1]), [pv[kc // 2], cols], [t1])
                    s.pool(TT(vfl.ap, vfl.ap, V_, ALU.subtract), [vfl] + rV, [vfl])
                    s.dve(TT(vfl.ap, vfl.ap, t1.ap, ALU.mult), [vfl, t1], [vfl])
                    s.pool(TT(V_, V_, vfl.ap, ALU.add), [vfl] + rV, rV)
                for kc in range(4):
                    sl = slice(kc * NB, (kc + 1) * NB)
                    s.dve(SCAN(bcs.ap[:, sl], scanmask, lw.ap[:, sl]), [lw, cst], [bcs])
                s.dve(TT(t2.ap, bcs.ap, lw.ap, ALU.subtract), [bcs, lw], [t2])
                b4 = bcs.v4(4, NCH, CH)
                s.dve(TT(t3.v4(4, NCH, CH), b4[:, :, :, CH - 1:CH].to_broadcast([128, 4, NCH, CH]), b4, ALU.subtract),
                      [bcs], [t3])
                s.act(ACTF(eb.ap, bcs.ap, AF.Exp), [bcs], [eb])
                s.act(ACTF(enb.ap, bcs.ap, AF.Exp, scale=-1.0), [bcs], [enb])
                s.act(ACTF(ebx.ap, t2.ap, AF.Exp), [t2], [ebx])
                s.act(ACTF(egb.ap, t3.ap, AF.Exp), [t3], [egb])
                for kc in range(4):
                    sl = slice(kc * NB, (kc + 1) * NB)
                    s.act(ACTF(t1.ap[:, sl], K_[:, sl], AF.Square, scale=cvf[:, K_KK + kc:K_KK + kc + 1]),
                          [rK[kc], cols], [t1])
                pq = [bank(), bank()]
                for kc in range(4):
                    s.pe(MM(pq[kc // 2].ap[:, (kc % 2) * NB:(kc % 2 + 1) * NB], blk32, t1.ap[:, kc * NB:(kc + 1) * NB]),
                         [cst, t1], [pq[kc // 2]])
                for hf in range(2):
                    sl = slice(hf * 2 * NB, (hf + 1) * 2 * NB)
                    s.dve(TS(t2.ap[:, sl], pq[hf].ap[:, 0:2 * NB], 1e-24, None, ALU.max), [pq[hf]], [t2])
                s.act(ACTF(t2.ap, t2.ap, AF.Ln), [t2], [t2])
                s.act(ACTF(t3.ap, t2.ap, AF.Exp, scale=-0.5), [t2], [t3])
                for kc in range(4):
                    sl = slice(kc * NB, (kc + 1) * NB)
                    s.dve(STT(kkn.ap[:, sl], K_[:, sl], cvf[:, K_KK + kc:K_KK + kc + 1], t3.ap[:, sl], ALU.mult, ALU.mult),
                          [rK[kc], cols, t3], [kkn])
                    s.pool(TS(t1.ap[:, sl], av.ap[:, sl], cvf[:, K_KA + kc:K_KA + kc + 1],
                              cvf[:, K_OMKA + kc:K_OMKA + kc + 1]), [av, cols], [t1])
                s.dve(TT(kmod.ap, K_, t1.ap, ALU.mult), rK + [t1], [kmod])
                s.dve(TT(uu.ap, av.ap, kkn.ap, ALU.mult), [av, kkn], [uu])
                s.dve(STT(PT16.ap, kkn.ap, -1.0, ebx.ap, ALU.mult, ALU.mult), [kkn, ebx], [PT16])
                for hh_ in range(2):
                    hp_ = slice(hh_ * 64, (hh_ + 1) * 64)
                    s.dve(TT(UTz[hh_].ap[hp_, :], uu.ap[hp_, :], enb.ap[hp_, :], ALU.mult), [uu, enb], [UTz[hh_]])
                    s.dve(TT(KTz[hh_].ap[hp_, :], kmod.ap[hp_, :], enb.ap[hp_, :], ALU.mult), [kmod, enb], [KTz[hh_]])
                    s.act(ACP(PTz[hh_].ap[hp_, :], PT16.ap[hp_, :]), [PT16], [PTz[hh_]])
                s.pool(TT(RT32.ap, R_, eb.ap, ALU.mult), rR + [eb], [RT32])
                s.act(ACP(RT16.ap, RT32.ap), [RT32], [RT16])
                s.dve(TT(UG16.ap, uu.ap, egb.ap, ALU.mult), [uu, egb], [UG16])
                s.dve(TT(KG16.ap, kmod.ap, egb.ap, ALU.mult), [kmod, egb], [KG16])
                s.dve(TT(t2.ap, R_, kmod.ap, ALU.mult), rR + [kmod], [t2])
                for kc in range(4):
                    sl = slice(kc * NB, (kc + 1) * NB)
                    s.act(ACTF(rk16.ap[:, sl], t2.ap[:, sl], AF.Identity, scale=cvf[:, K_RK + kc:K_RK + kc + 1]),
                          [t2, cols], [rk16])

                if CUT == 10:
                    s.barrier(); return
                for tt in range(2):
                    zsl, bon, QT16, MT16, YL32 = zsl_[tt], bon_[tt], QT16_[tt], MT16_[tt], YL32_[tt]
                    N32, ysb, yc, ysq, st8 = N32_[tt], ysb_[tt], yc_[tt], ysq_[tt], st8_[tt]
                    tsl = lambda kc: slice(kc * NB + tt * 128, kc * NB + (tt + 1) * 128)
                    pvt = bank()
                    pzt = bank()
                    for kc in range(4):
                        s.pe(TR(pvt.ap[:, kc * 128:(kc + 1) * 128], V_[:, tsl(kc)], ident), rV + [cst], [pvt])
                    for kc in range(4):
                        s.pe(TR(pzt.ap[:, kc * 128:(kc + 1) * 128], Z_[:, tsl(kc)], ident), rZ + [cst], [pzt])
                    if CUT == 111:
                        s.barrier(); return
                    s.dve(CP(vtm32.ap, pvt.ap[:, 0:512]), [pvt], [vtm32])
                    s.act(ACP(vtm16.ap, pvt.ap[:, 0:512]), [pvt], [vtm16])
                    s.act(ACTF(zsl.ap, pzt.ap[:, 0:512], AF.Silu), [pzt], [zsl])
                    if CUT == 112:
                        s.barrier(); return
                    pst = bank()
                    for kc in range(4):
                        s.pe(MM(pst.ap[:, kc * 128:(kc + 1) * 128], rk16.ap[:, tsl(kc)], blk16.ap), [rk16, blk16], [pst])
                    s.dve(TT(bon.ap, pst.ap[:, 0:512], vtm32.ap, ALU.mult), [pst, vtm32], [bon])

                    if CUT == 11:
                        s.barrier(); return
                    def fm(tile16, h, par):
                        kc, hh = h // 2, h % 2
                        c0 = kc * NB + tt * 128 + par * 64
                        return tile16.ap[hh * 64:(hh + 1) * 64, c0:c0 + 64]

                    def ch(tile16, h, par):
                        kc = h // 2
                        c0 = kc * NB + tt * 128 + par * 64
                        return tile16.ap[:, c0:c0 + 64]

                    ids = lambda hh: id16.ap[:, hh * 64:(hh + 1) * 64]
                    PRv = PR16.ap.rearrange("p (two w) -> p two w", two=2)
                    G = [dict(), dict()]
                    rd = [PT16, RT16] + UTz + KTz + PTz
                    for g in range(2):
                        pX, pY, pZ = bank(), bank(), bank()
                        for par in range(2):
                            ps_ = slice(par * 64, (par + 1) * 64)
                            for hi in range(4):
                                h = g * 4 + hi
                                hh = h % 2
                                c0 = (h // 2) * NB + tt * 128 + par * 64
                                pr = PRv[:, :, c0:c0 + 64]
                                ox = pX.ap[ps_, hi * 128:(hi + 1) * 128].rearrange("p (a b) -> p a b", a=2)
                                oy = pY.ap[ps_, hi * 128:(hi + 1) * 128].rearrange("p (a b) -> p a b", a=2)
                                if MERGE_PR:
                                    s.pe(MM(ox, ch(UTz[hh], h, par), pr), rd, [pX])
                                    s.pe(MM(oy, ch(KTz[hh], h, par), pr), rd, [pY])
                                else:
                                    s.pe(MM(pX.ap[ps_, hi * 128:hi * 128 + 64], ch(UTz[hh], h, par), ch(PT16, h, par)), rd, [pX])
                                    s.pe(MM(pX.ap[ps_, hi * 128 + 64:hi * 128 + 128], ch(UTz[hh], h, par), ch(RT16, h, par)), rd, [pX])
                                    s.pe(MM(pY.ap[ps_, hi * 128:hi * 128 + 64], ch(KTz[hh], h, par), ch(PT16, h, par)), rd, [pY])
                                    s.pe(MM(pY.ap[ps_, hi * 128 + 64:hi * 128 + 128], ch(KTz[hh], h, par), ch(RT16, h, par)), rd, [pY])
                                s.pe(MM(pZ.ap[ps_, hi * 64:(hi + 1) * 64], ch(PTz[hh], h, par), ch(UTz[hh], h, par)), rd, [pZ])
                        G[g]["pA"] = (pX, pY, pZ)
                    for g in range(2):
                        pX, pY, pZ = G[g]["pA"]
                        A = AXY[g]
                        s.dve(TT(A.ap[:, 0:512], pX.ap[:, 0:512], rwmXY.ap, ALU.mult), [pX, rwmXY], [A.r(0)])
                        s.dve(TT(A.ap[:, 512:1024], pY.ap[:, 0:512], rwmXY.ap, ALU.mult), [pY, rwmXY], [A.r(1)])
                        xy = XY[g][0]
                        s.dve(TT(xy.v3(4, 128)[:, :, 0:64], pZ.ap[:, 0:256].rearrange("p (h c) -> p h c", h=4),
                                 rwmZ.v3(4, 64), ALU.mult), [pZ, rwmZ], [xy.r(0)])
                        s.act(ACP(xy.v3(4, 128)[:, :, 64:128], A.ap[:, 0:512].rearrange("p (h c) -> p h c", h=4)[:, :, 0:64]),
                              [A.r(0)], [xy.r(1)])
                    for g in range(2):
                        A = AXY[g]
                        pR = bank()
                        for par in range(2):
                            ps_ = slice(par * 64, (par + 1) * 64)
                            for hi in range(4):
                                h = g * 4 + hi
                                s.pe(MM(pR.ap[ps_, hi * 128:hi * 128 + 64], ch(PT16, h, par), ids(h % 2)), [PT16, id16], [pR])
                                s.pe(MM(pR.ap[ps_, hi * 128 + 64:hi * 128 + 128],
                                        A.ap[ps_, 512 + hi * 128:512 + hi * 128 + 64],
                                        vtm16.ap[ps_, h * 64:(h + 1) * 64]), [A.r(1), vtm16], [pR])
                        G[g]["pR"] = pR
                    for g in range(2):
                        evac(RR[g][0].ap, G[g]["pR"].ap[:, 0:512], [G[g]["pR"]], [RR[g][0]])
                    cur = 0
                    for j in range(6):
                        for g in range(2):
                            xy = XY[g][cur]
                            rr = RR[g][cur]
                            pR = bank()
                            for par in range(2):
                                ps_ = slice(par * 64, (par + 1) * 64)
                                for hi in range(4):
                                    s.pe(MM(pR.ap[ps_, hi * 128:(hi + 1) * 128], xy.ap[ps_, hi * 128 + 64:hi * 128 + 128],
                                            rr.ap[ps_, hi * 128:(hi + 1) * 128]), [xy.r(1), rr], [pR])
                            G[g]["pR"] = pR
                            if j < 5:
                                pS = bank()
                                for par in range(2):
                                    ps_ = slice(par * 64, (par + 1) * 64)
                                    for hi in range(4):
                                        X_ = xy.ap[ps_, hi * 128:hi * 128 + 64]
                                        Y_ = xy.ap[ps_, hi * 128 + 64:hi * 128 + 128]
                                        s.pe(MM(pS.ap[ps_, hi * 128:hi * 128 + 64], Y_, X_), [xy.r(0), xy.r(1)], [pS])
                                        s.pe(MM(pS.ap[ps_, hi * 128 + 64:hi * 128 + 128], X_, Y_), [xy.r(0), xy.r(1)], [pS])
                                G[g]["pS"] = pS
                        for g in range(2):
                            rr = RR[g][cur]
                            rrn = RR[g][1 - cur]
                            xyn = XY[g][1 - cur]
                            if j < 5:
                                s.act(ACP(xyn.ap, G[g]["pS"].ap[:, 0:512]), [G[g]["pS"]], [xyn.r(0), xyn.r(1)])
                            s.dve(TT(rrn.ap, G[g]["pR"].ap[:, 0:512], rr.ap, ALU.add), [G[g]["pR"], rr], [rrn])
                        cur = 1 - cur
                    for g in range(2):
                        pU = bank()
                        for par in range(2):
                            ps_ = slice(par * 64, (par + 1) * 64)
                            for hi in range(4):
                                h = g * 4 + hi
                                s.pe(MM(pU.ap[ps_, hi * 128:hi * 128 + 64], ch(UG16, h, par), ids(h % 2)), [UG16, id16], [pU])
                                s.pe(MM(pU.ap[ps_, hi * 128 + 64:hi * 128 + 128], ch(KG16, h, par), ids(h % 2)), [KG16, id16], [pU])
                        G[g]["pU"] = pU
                    for g in range(2):
                        s.act(ACP(UKG[g].ap, G[g]["pU"].ap[:, 0:512]), [G[g]["pU"]], [UKG[g]])
                    for g in range(2):
                        A = AXY[g]
                        wz = RR[g][cur]
                        ukg = UKG[g]
                        pQs = [bank(), bank()]
                        pNs = [bank(), bank()]
                        pYL = bank()
                        for par in range(2):
                            ps_ = slice(par * 64, (par + 1) * 64)
                            pQ = pQs[par]
                            pN = pNs[par]
                            for hi in range(4):
                                h = g * 4 + hi
                                kcl, hh = hi // 2, hi % 2
                                hb = slice(hh * 64, (hh + 1) * 64)
                                Wm = wz.ap[ps_, hi * 128:hi * 128 + 64]
                                Zm = wz.ap[ps_, hi * 128 + 64:hi * 128 + 128]
                                AruT = A.ap[ps_, hi * 128 + 64:hi * 128 + 128]
                                ArkT = A.ap[ps_, 512 + hi * 128 + 64:512 + hi * 128 + 128]
                                Ug = ukg.ap[ps_, hi * 128:hi * 128 + 64]
                                Kg = ukg.ap[ps_, hi * 128 + 64:hi * 128 + 128]
                                Vm = vtm16.ap[ps_, h * 64:(h + 1) * 64]
                                s.pe(MM(pQ.ap[hb, kcl * 64:(kcl + 1) * 64], Wm, AruT), [wz, A.r(0)], [pQ])
                                s.pe(MM(pQ.ap[hb, 128 + kcl * 64:128 + (kcl + 1) * 64], Wm, Ug), [wz, ukg], [pQ])
                                s.pe(MM(pN.ap[hb, kcl * 64:(kcl + 1) * 64], Ug, Zm, True, False), [ukg, wz], [pN])
                                s.pe(MM(pN.ap[hb, kcl * 64:(kcl + 1) * 64], Kg, Vm, False, True), [ukg, vtm16], [pN])
                                s.pe(MM(pYL.ap[ps_, hi * 64:(hi + 1) * 64], AruT, Zm, True, False), [A.r(0), wz], [pYL])
                                s.pe(MM(pYL.ap[ps_, hi * 64:(hi + 1) * 64], ArkT, Vm, False, True), [A.r(1), vtm16], [pYL])
                        gs = slice(g * 256, (g + 1) * 256)
                        for par in range(2):
                            qo = QT16.ap[:, gs].rearrange("p (k a t) -> p k a t", k=2, a=2)[:, :, par, :]
                            rt = RT32.v3(4, NB)[:, 2 * g:2 * g + 2, tt * 128 + par * 64:tt * 128 + (par + 1) * 64]
                            s.dve(TT(qo, pQs[par].ap[:, 0:128].rearrange("p (k t) -> p k t", k=2), rt, ALU.add),
                                  [pQs[par], RT32], [QT16.r(g)])
                            mo = MT16.ap[:, g * 256 + par * 128:g * 256 + (par + 1) * 128]
                            s.act(ACP(mo, pQs[par].ap[:, 128:256]), [pQs[par]], [MT16.r(g)])
                            no = N32.ap[:, g * 256 + par * 128:g * 256 + (par + 1) * 128]
                            s.act(ACP(no, pNs[par].ap[:, 0:128]), [pNs[par]], [N32.r(g)])
                        s.dve(CP(YL32.ap[:, gs], pYL.ap[:, 0:256]), [pYL], [YL32.r(g)])
                    pYh = [bank(), bank()]
                    Hv = H32.ap[:, 0:256]
                    H16v = H16.ap[:, 0:256]
                    for par in range(2):
                        ps_ = slice(par * 64, (par + 1) * 64)
                        pH = bank()
                        for h in range(8):
                            kc, hh = h // 2, h % 2
                            g, kcl = kc // 2, kc % 2
                            hb = slice(hh * 64, (hh + 1) * 64)
                            qc = g * 256 + kcl * 128 + par * 64
                            mc = g * 256 + par * 128 + kcl * 64
                            Hh = H16v[hb, kc * 64:(kc + 1) * 64]
                            s.pe(MM(pYh[hh].ap[ps_, kc * 64:(kc + 1) * 64], QT16.ap[hb, qc:qc + 64], Hh), [QT16.r(g), H16], [pYh[hh]])
                            s.pe(MM(pH.ap[hb, kc * 64:(kc + 1) * 64], MT16.ap[hb, mc:mc + 64], Hh), [MT16.r(g), H16], [pH])
                        clast = tt * 128 + par * 64 + 63
                        gcb = eb.v3(4, NB)[:, :, clast:clast + 1].to_broadcast([128, 4, 64])
                        s.pool(TT(Ht.v3(4, 64), H32.ap[:, 0:256].rearrange("p (k v) -> p k v", k=4), gcb, ALU.mult), [H32, eb], [Ht])
                        n4 = N32.ap.rearrange("p (g a k v) -> p g a k v", g=2, a=2, k=2)[:, :, par]
                        s.pool(TT(Ht.ap.rearrange("p (g k v) -> p g k v", g=2, k=2), Ht.ap.rearrange("p (g k v) -> p g k v", g=2, k=2),
                                  n4, ALU.add), [Ht, N32.r(0), N32.r(1)], [Ht])
                        s.dve(TT(Hv, pH.ap[:, 0:256], Ht.ap, ALU.add), [pH, Ht], [H32])
                        s.act(ACP(H16v, Hv), [H32], [H16])
                    if CUT == 13:
                        s.barrier(); return
                    for hh in range(2):
                        yo = ysb.ap.rearrange("p (k a v) -> p k a v", k=4, a=2)[:, :, hh, :]
                        yl = YL32.ap.rearrange("p (k a v) -> p k a v", k=4, a=2)[:, :, hh, :]
                        s.dve(TT(yo, pYh[hh].ap[:, 0:256].rearrange("p (k v) -> p k v", k=4), yl, ALU.add),
                              [pYh[hh], YL32.r(0), YL32.r(1)], [ysb])
                    def post_b2(ysb, st8, yc, ysq, bon, zsl, yf, row0):
                        s.dve(RED(st8.ap[:, 0:8], ysb.v3(8, 64)), [ysb], [st8]); yield
                        s.dve(TS(st8.ap[:, 0:8], st8.ap[:, 0:8], -1.0 / 64.0, None, ALU.mult), [st8], [st8]); yield
                        s.dve(TT(yc.v3(8, 64), ysb.v3(8, 64), st8.ap[:, 0:8].unsqueeze(2).to_broadcast([128, 8, 64]), ALU.add),
                              [ysb, st8], [yc]); yield
                        s.pool(TT(yf.ap, yc.ap, yc.ap, ALU.mult), [yc], [yf]); yield
                        s.dve(RED(st8.ap[:, 8:16], yf.v3(8, 64)), [yf], [st8]); yield
                        s.dve(TS(st8.ap[:, 8:16], st8.ap[:, 8:16], 1.0 / 64.0, GN_EPS), [st8], [st8]); yield
                        s.act(ACTF(st8.ap[:, 8:16], st8.ap[:, 8:16], AF.Sqrt), [st8], [st8]); yield
                        s.dve(RCP(st8.ap[:, 16:24], st8.ap[:, 8:16]), [st8], [st8]); yield
                        s.dve(TT(yc.v3(8, 64), yc.v3(8, 64), st8.ap[:, 16:24].unsqueeze(2).to_broadcast([128, 8, 64]), ALU.mult),
                              [yc, st8], [yc]); yield
                        s.pool(TT(yc.ap, yc.ap, gng_b.ap, ALU.mult), [yc, gng_b], [yc]); yield
                        s.pool(TT(yc.ap, yc.ap, gnb_b.ap, ALU.add), [yc, gnb_b], [yc]); yield
                        s.pool(TT(yc.ap, yc.ap, bon.ap, ALU.add), [yc, bon], [yc]); yield
                        s.dve(TT(yf.ap, yc.ap, zsl.ap, ALU.mult), [yc, zsl], [yf]); yield
                        s.dma("pool", OM[row0:row0 + 128, 512:1024], yf.ap, reads=[yf]); yield

                    posts.append(post_b2(ysb, st8, yc, ysq, bon, zsl, yfin[tt], t0 + tt * 128))
                drain(posts)
            s.barrier()

        def pass_C(l, xin, xout):
            ar.reset()
            wA = ar.bf("wA", 8 * 1024)
            wO = ar.bf("wO", 8 * 1024)
            wst = [ar.f32(f"cwst{i}", 1024) for i in range(3)]
            n = 0
            for kc in range(8):
                for which in range(2):
                    st = wst[n % 3]
                    if which == 0:
                        src = (W["w_branch_hg"][l] if kc < 4 else W["w_branch_rw"][l])[(kc % 4) * 128:(kc % 4 + 1) * 128, :]
                        dst = wA.v3(8, 1024)[:, kc, :]
                        dr = wA.r(kc)
                    else:
                        src = W["w_out"][l][kc * 128:(kc + 1) * 128, :]
                        dst = wO.v3(8, 1024)[:, kc, :]
                        dr = wO.r(kc)
                    s.dma(dmaq(), st.ap, src, writes=[st])
                    if n % 2:
                        s.dve(CP(dst, st.ap), [st], [dr])
                    else:
                        s.act(ACP(dst, st.ap), [st], [dr])
                    n += 1
            om = [ar.f32(f"om{i}", 1024) for i in range(2)]
            xt = [ar.f32(f"cx{i}", 1024) for i in range(2)]
            gt = [ar.f32(f"gt{i}", 2048) for i in range(2)]
            omT_ = [ar.bf(f"omT{i}", 1024) for i in range(2)]
            mg_ = [ar.f32(f"mg{i}", 1024) for i in range(2)]
            m2_ = [ar.f32(f"m2{i}", 1024) for i in range(2)]
            mT_ = [ar.bf(f"mT{i}", 1024) for i in range(2)]
            zr_ = [ar.f32(f"zr{i}", 1024) for i in range(2)]
            bst_ = [ar.f32(f"bst{i}", 16) for i in range(2)]
            xo = [ar.f32(f"xo{i}", 1024) for i in range(2)]
            for it in range(T // 128):
                r0 = it * 128
                omT, mg, m2, mT, zr, bst = omT_[it % 2], mg_[it % 2], m2_[it % 2], mT_[it % 2], zr_[it % 2], bst_[it % 2]
                o_ = om[it % 2]
                x_ = xt[it % 2]
                g_ = gt[it % 2]
                s.dma("sp", o_.ap, OM[r0:r0 + 128, :], writes=[o_])
                s.dma("sp", x_.ap, xin[r0:r0 + 128, :], writes=[x_])
                s.dma("sp", g_.ap, PT[r0:r0 + 128, 1024:3072], writes=[g_])
                for half in range(2):
                    pb = bank()
                    for j in range(4):
                        kc = half * 4 + j
                        s.pe(TR(pb.ap[:, j * 128:(j + 1) * 128], o_.ap[:, kc * 128:(kc + 1) * 128], ident), [o_, cst], [pb])
                    evac(omT.ap[:, half * 512:(half + 1) * 512], pb.ap[:, 0:512], [pb], [omT.r(half)])
                s.act(ACTF(g_.ap, g_.ap, AF.Sigmoid), [g_], [g_])
                omv = omT.v3(8, 128)
                for hf in range(2):
                    cs_ = slice(hf * 512, (hf + 1) * 512)
                    pa_ = bank()
                    pb_ = bank()
                    for kc in range(4):
                        s.pe(MM(pa_.ap[:, 0:512], omv[:, kc, :], wA.v3(8, 1024)[:, kc, cs_], kc == 0, kc == 3),
                             [omT.r(0), wA.r(kc)], [pa_])
                    for kc in range(4, 8):
                        s.pe(MM(pb_.ap[:, 0:512], omv[:, kc, :], wA.v3(8, 1024)[:, kc, cs_], kc == 4, kc == 7),
                             [omT.r(1), wA.r(kc)], [pb_])
                    s.dve(TT(mg.ap[:, cs_], pa_.ap[:, 0:512], g_.ap[:, hf * 512:(hf + 1) * 512], ALU.mult), [pa_, g_], [mg.r(hf)])
                    s.dve(TT(m2.ap[:, cs_], pb_.ap[:, 0:512], g_.ap[:, 1024 + hf * 512:1024 + (hf + 1) * 512], ALU.mult),
                          [pb_, g_], [m2.r(hf)])
                    s.dve(TT(mg.ap[:, cs_], mg.ap[:, cs_], m2.ap[:, cs_], ALU.add), [mg.r(hf), m2.r(hf)], [mg.r(hf)])
                for half in range(2):
                    pb = bank()
                    for j in range(4):
                        kc = half * 4 + j
                        s.pe(TR(pb.ap[:, j * 128:(j + 1) * 128], mg.ap[:, kc * 128:(kc + 1) * 128], ident), [mg.r(half), cst], [pb])
                    evac(mT.ap[:, half * 512:(half + 1) * 512], pb.ap[:, 0:512], [pb], [mT.r(half)])
                mv = mT.v3(8, 128)
                for hf in range(2):
                    cs_ = slice(hf * 512, (hf + 1) * 512)
                    po_ = bank()
                    for kc in range(8):
                        s.pe(MM(po_.ap[:, 0:512], mv[:, kc, :], wO.v3(8, 1024)[:, kc, cs_], kc == 0, kc == 7),
                             [mT.r(kc // 4), wO.r(kc)], [po_])
                    s.dve(TT(zr.ap[:, cs_], po_.ap[:, 0:512], gate_b.ap[:, cs_], ALU.mult), [po_, gate_b], [zr.r(hf)])
                    s.dve(STT(zr.ap[:, cs_], x_.ap[:, cs_], DN_ALPHA, zr.ap[:, cs_], ALU.mult, ALU.add), [x_, zr.r(hf)], [zr.r(hf)])
                    s.dve(BNS(bst.ap[:, hf * 6:(hf + 1) * 6], zr.ap[:, cs_]), [zr.r(hf)], [bst])
                s.dve(BNA(bst.ap[:, 12:14], bst.ap[:, 0:12]), [bst], [bst])
                s.dve(TS(bst.ap[:, 14:15], bst.ap[:, 13:14], LN_EPS, None, ALU.add), [bst], [bst])
                s.act(ACTF(bst.ap[:, 14:15], bst.ap[:, 14:15], AF.Sqrt), [bst], [bst])
                s.dve(RCP(bst.ap[:, 15:16], bst.ap[:, 14:15]), [bst], [bst])
                xo_ = xo[it % 2]
                s.dve(TS(xo_.ap, zr.ap, bst.ap[:, 12:13], bst.ap[:, 15:16], ALU.subtract, ALU.mult),
                      [zr.r(0), zr.r(1), bst], [xo_])
                s.dve(TT(xo_.ap, xo_.ap, lng_b.ap, ALU.mult), [xo_, lng_b], [xo_])
                s.dve(TT(xo_.ap, xo_.ap, lnb_b.ap, ALU.add), [xo_, lnb_b], [xo_])
                s.dma("pool", xout[r0:r0 + 128, :], xo_.ap, reads=[xo_])
            s.barrier()

        s.barrier()
        stages = []
        for l in range(NL):
            xin = x_in if l == 0 else X1
            xout = out if l == NL - 1 else X1
            stages += [("setup", l), ("A", l), ("B1", l), ("B2", l), ("C", l)]
        for name, l in stages:
            xin = x_in if l == 0 else X1
            xout = out if l == NL - 1 else X1
            if name == "setup":
                layer_setup(l)
            elif name == "A":
                pass_A(l, xin)
            elif name == "B1":
                pass_B1(l)
            elif name == "B2":
                pass_B2(l)
            else:
                pass_C(l, xin, xout)
            if stop_after == (name, l):
                break
        s.barrier()
        s.emit()
    return nc, s


_CACHE = {}


def _prep_weights(inputs):
    m = {}
    for n in WEIGHT_NAMES:
        a = np.ascontiguousarray(np.asarray(inputs[n], dtype=np.float32))
        m[n] = a.reshape(WEIGHT_SHAPES[n])
    m["consts"] = make_consts()
    return m


def kernel(**inputs):
    x = np.asarray(inputs["x"], dtype=np.float32)
    c = np.asarray(inputs["c"], dtype=np.float32)
    B, T, _ = x.shape
    key = (T,)
    if key not in _CACHE:
        _CACHE[key] = build(T)[0]
    nc = _CACHE[key]
    wm = _prep_weights(inputs)
    in_maps = []
    for core in range(8):
        b = core % B
        mm = dict(wm)
        mm["x"] = np.ascontiguousarray(x[b])
        mm["c"] = np.ascontiguousarray(c[b:b + 1])
        in_maps.append(mm)
    res = run_bass_kernel_spmd(nc, in_maps, core_ids=list(range(8)))
    return np.stack([res.results[b]["out"] for b in range(B)], axis=0).astype(np.float32)
```

```python
import contextlib
import math
import numpy as np
import concourse.bass as bass
import concourse.mybir as mybir
from concourse.bass_utils import run_bass_kernel_spmd

F32 = mybir.dt.float32
BF16 = mybir.dt.bfloat16
AF = mybir.ActivationFunctionType
ALU = mybir.AluOpType
AX = mybir.AxisListType

ENGS = ("pe", "dve", "act", "pool", "sp")
EPOCH = 7000
NDMASEM = 8

D = 1024
NIN = 6272
DN_ALPHA = 4.0 ** 0.25
LN_EPS = 1e-5
RMS_EPS = 1e-6
GN_EPS = 64e-5
NB = 256
import os
CUT = int(os.environ.get('KCUT', '0'))
SKIP = int(os.environ.get('KSKIP', '0'))
SKIPM = int(os.environ.get('KSKIPM', '0'))
KV = int(os.environ.get('KV', '0'))
MERGE_PR = int(os.environ.get('KMERGE', '1'))
CH = 64


class Res:
    __slots__ = ("last_w", "readers", "name", "excl")

    def __init__(self, name="", excl=False):
        self.last_w = None
        self.readers = []
        self.name = name
        self.excl = excl


class Tl:
    def __init__(self, ap, name):
        self.ap = ap
        self.res = Res(name)
        self.name = name
        self.sub = {}

    def v3(self, a, b):
        return self.ap.rearrange("p (a b) -> p a b", a=a, b=b)

    def v4(self, a, b, c):
        return self.ap.rearrange("p (a b c) -> p a b c", a=a, b=b, c=c)

    def r(self, key):
        x = self.sub.get(key)
        if x is None:
            x = Res(f"{self.name}:{key}")
            self.sub[key] = x
        return x


class Sched:
    def __init__(self, nc, stack):
        self.nc = nc
        self.stack = stack
        self.q = {e: [] for e in ENGS}
        self.cnt = {e: 0 for e in ENGS}
        self.seen = {e: {} for e in ENGS}
        self.esem = {}
        self.dsem = {}
        self.dcnt = {}
        self.dnext = {e: 0 for e in ENGS}
        self.same_engine_sync = {"pe": False, "dve": True, "act": True, "pool": True, "sp": True}
        self.nops = 0

    def sem(self, name):
        return self.stack.enter_context(self.nc.semaphore(name))

    def R(self, obj):
        if isinstance(obj, Res):
            return obj
        return obj.res

    def _esem(self, eng, count):
        ep = (count - 1) // EPOCH
        k = (eng, ep)
        if k not in self.esem:
            self.esem[k] = self.sem(f"e_{eng}_{ep}")
        return self.esem[k], count - ep * EPOCH

    def _add_wait(self, waits, eng, dep):
        key, val = dep
        if key[0] == "eng" and key[1] == eng and not self.same_engine_sync[eng]:
            return
        if self.seen[eng].get(key, 0) >= val:
            return
        if val > waits.get(key, 0):
            waits[key] = val

    def _lower_waits(self, eng, waits):
        wl = []
        for key, val in waits.items():
            self.seen[eng][key] = val
            if key[0] == "eng":
                wl.append(self._esem(key[1], val))
            else:
                wl.append((self.dsem[(key[1], key[2])], val))
        return wl

    def op(self, eng, fn, reads=(), writes=(), dma=False):
        reads = [self.R(r) for r in reads]
        writes = [self.R(w) for w in writes]
        ex = [r for r in reads if r.excl]
        if ex:
            reads = [r for r in reads if not r.excl]
            writes = writes + [r for r in ex if r not in writes]
        waits = {}
        for r in reads:
            if r.last_w is not None:
                self._add_wait(waits, eng, r.last_w)
        for w in writes:
            if w.last_w is not None:
                self._add_wait(waits, eng, w.last_w)
            for rd in w.readers:
                self._add_wait(waits, eng, rd)
        if dma:
            idx = self.dnext[eng]
            self.dnext[eng] = (idx + 1) % NDMASEM
            dk = (eng, idx)
            if dk not in self.dsem:
                self.dsem[dk] = self.sem(f"d_{eng}_{idx}")
                self.dcnt[dk] = 0
            key = ("dma", eng, idx)
            if self.dcnt[dk] > 0:
                self._add_wait(waits, eng, (key, self.dcnt[dk] * 16))
            self.dcnt[dk] += 1
            done = (key, self.dcnt[dk] * 16)
            inc = (self.dsem[dk], 16)
        else:
            self.cnt[eng] += 1
            c = self.cnt[eng]
            done = (("eng", eng), c)
            inc = (self._esem(eng, c)[0], 1)
        wl = self._lower_waits(eng, waits)
        self.q[eng].append((fn, wl, inc))
        self.nops += 1
        for r in reads:
            r.readers.append(done)
            if len(r.readers) > 64:
                best = {}
                for k, v in r.readers:
                    if v > best.get(k, 0):
                        best[k] = v
                r.readers = list(best.items())
        for w in writes:
            w.last_w = done
            w.readers = []
        return done

    def pe(self, fn, reads=(), writes=()):
        return self.op("pe", fn, reads, writes)

    def dve(self, fn, reads=(), writes=()):
        return self.op("dve", fn, reads, writes)

    def act(self, fn, reads=(), writes=()):
        return self.op("act", fn, reads, writes)

    def pool(self, fn, reads=(), writes=()):
        return self.op("pool", fn, reads, writes)

    def dma(self, eng, out, in_, reads=(), writes=(), **kw):
        return self.op(eng, lambda e: e.dma_start(out=out, in_=in_, **kw), reads, writes, dma=True)

    def wait_all(self, eng, deps):
        waits = {}
        for d in deps:
            self._add_wait(waits, eng, d)
        wl = self._lower_waits(eng, waits)
        if wl:
            self.q[eng].append((None, wl, None))

    def barrier(self):
        deps = [(("eng", e), self.cnt[e]) for e in ENGS if self.cnt[e] > 0]
        deps += [(("dma", k[0], k[1]), v * 16) for k, v in self.dcnt.items()]
        for e in ENGS:
            self.wait_all(e, deps)

    def emit(self):
        nc = self.nc
        with nc.Block() as block:
            def run(eng_name):
                def body(eng):
                    for fn, wl, inc in self.q[eng_name]:
                        for sm, v in wl:
                            eng.wait_ge(sm, v)
                        if fn is not None:
                            fn(eng).then_inc(inc[0], inc[1])
                return body
            block.sync(run("sp"))
            block.tensor(run("pe"))
            block.vector(run("dve"))
            block.scalar(run("act"))
            block.gpsimd(run("pool"))


def MM(out, lhsT, rhs, st=True, sp=True):
    return lambda e: e.matmul(out, lhsT, rhs, start=st, stop=sp)


def TR(out, in_, ident):
    return lambda e: e.transpose(out, in_, ident)


def ACTF(out, in_, func, bias=0.0, scale=1.0):
    return lambda e: e.activation(out=out, in_=in_, func=func, bias=bias, scale=scale)


def CP(out, in_):
    return lambda e: e.tensor_copy(out=out, in_=in_)


def ACP(out, in_):
    return lambda e: e.copy(out=out, in_=in_)


def TT(out, a, b, op):
    return lambda e: e.tensor_tensor(out=out, in0=a, in1=b, op=op)


def TS(out, in0, s1, s2=None, op0=ALU.mult, op1=ALU.add):
    if s2 is None:
        return lambda e: e.tensor_scalar(out=out, in0=in0, scalar1=s1, scalar2=None, op0=op0)
    return lambda e: e.tensor_scalar(out=out, in0=in0, scalar1=s1, scalar2=s2, op0=op0, op1=op1)


def STT(out, in0, scalar, in1, op0, op1):
    return lambda e: e.scalar_tensor_tensor(out=out, in0=in0, scalar=scalar, in1=in1, op0=op0, op1=op1)


def MS(ap, v):
    return lambda e: e.memset(ap, v)


def RED(out, in_, op=ALU.add):
    return lambda e: e.tensor_reduce(out=out, in_=in_, axis=AX.X, op=op)


def RCP(out, in_):
    return lambda e: e.reciprocal(out=out, in_=in_)


def SCAN(out, d0, d1):
    return lambda e: e.tensor_tensor_scan(out=out, data0=d0, data1=d1, initial=0.0, op0=ALU.mult, op1=ALU.add)


def BNS(out, in_):
    return lambda e: e.bn_stats(out=out, in_=in_)


def BNA(out, in_):
    return lambda e: e.bn_aggr(out=out, in_=in_)


C_ID = 0
C_BLK = 128
C_HSEL = 256
C_LE = 258
C_LT = 322
C_GT = 386
C_SM = 450
C_N = 450 + 256


def make_consts():
    cst = np.zeros((128, C_N), np.float32)
    p = np.arange(128)[:, None]
    j = np.arange(128)[None, :]
    cst[:, C_ID:C_ID + 128] = (p == j)
    cst[:, C_BLK:C_BLK + 128] = (p // 64 == j // 64)
    cst[:, C_HSEL:C_HSEL + 2] = (p // 64 == np.arange(2)[None, :])
    j64 = np.arange(64)[None, :]
    cst[:, C_LE:C_LE + 64] = ((p % 64) <= j64)
    cst[:, C_LT:C_LT + 64] = ((p % 64) < j64)
    cst[:, C_GT:C_GT + 64] = ((p % 64) > j64)
    cst[:, C_SM:C_SM + 256] = ((np.arange(256)[None, :] % 64) != 0)
    return cst


class Arena:
    def __init__(self, tensor, words):
        self.t = tensor
        self.words = words
        self.off = 0
        self.gen = 0

    def reset(self):
        self.off = 0
        self.gen += 1

    def f32(self, name, cols, parts=128):
        assert self.off + cols <= self.words, (name, self.off, cols, self.words)
        ap = self.t[0:parts, self.off:self.off + cols]
        self.off += (cols + 15) // 16 * 16
        return Tl(ap, f"{name}_{self.gen}")

    def bf(self, name, cols, parts=128):
        w = (cols + 1) // 2
        assert self.off + w <= self.words, (name, self.off, w, self.words)
        ap = self.t[0:parts, self.off:self.off + w].bitcast(BF16)[:, 0:cols]
        self.off += (w + 15) // 16 * 16
        return Tl(ap, f"{name}_{self.gen}")


WEIGHT_NAMES = ["w_ada", "b_ada", "w_in", "hg_lower_bounds", "hg_norm_g", "rw_mu", "rw_w0", "rw_w_up",
                "rw_a0", "rw_a_up", "rw_k_k", "rw_k_a", "rw_r_k", "rw_v0", "rw_v_down", "rw_v_up",
                "rw_gn_g", "rw_gn_b", "w_branch_hg", "w_branch_rw", "w_out", "ln_g", "ln_b"]
WEIGHT_SHAPES = {
    "w_ada": [2, 1024, 3072], "b_ada": [2, 3072], "w_in": [2, 1024, NIN], "hg_lower_bounds": [2, 512],
    "hg_norm_g": [2, 512], "rw_mu": [2, 2176], "rw_w0": [2, 512], "rw_w_up": [2, 64, 512], "rw_a0": [2, 512],
    "rw_a_up": [2, 64, 512], "rw_k_k": [2, 512], "rw_k_a": [2, 512], "rw_r_k": [2, 512], "rw_v0": [1, 512],
    "rw_v_down": [1, 1024, 32], "rw_v_up": [1, 32, 512], "rw_gn_g": [2, 512], "rw_gn_b": [2, 512],
    "w_branch_hg": [2, 512, 1024], "w_branch_rw": [2, 512, 1024], "w_out": [2, 1024, 1024],
    "ln_g": [2, 1024], "ln_b": [2, 1024],
}

ARENA_WORDS = 45500


def build(T, NL=2, dbg=False, stop_after=None):
    assert T % NB == 0
    NBLK = T // NB
    nc = bass.Bass("TRN2", target_bir_lowering=False)
    x_in = nc.dram_tensor("x", [T, D], F32, kind="ExternalInput").ap()
    c_in = nc.dram_tensor("c", [1, D], F32, kind="ExternalInput").ap()
    cst_in = nc.dram_tensor("consts", [128, C_N], F32, kind="ExternalInput").ap()
    W = {n: nc.dram_tensor(n, WEIGHT_SHAPES[n], F32, kind="ExternalInput").ap() for n in WEIGHT_NAMES}
    out = nc.dram_tensor("out", [T, D], F32, kind="ExternalOutput").ap()
    sk = "ExternalOutput" if dbg else "Internal"
    PF = nc.dram_tensor("PF", [26, 128, T], F32, kind=sk).ap()
    PT = nc.dram_tensor("PT", [T, 3072], F32, kind=sk).ap()
    OM = nc.dram_tensor("OM", [T, 1024], F32, kind=sk).ap()
    VF = nc.dram_tensor("VF", [4, 128, T], F32, kind=sk).ap()
    X1 = nc.dram_tensor("X1", [T, D], F32, kind=sk).ap()

    with contextlib.ExitStack() as stack:
        s = Sched(nc, stack)

        def sbt(name, cols, dtype=F32, parts=128):
            t = stack.enter_context(nc.sbuf_tensor(name, [128, cols], dtype))
            return Tl(t[0:parts, :], name)

        arena_t = stack.enter_context(nc.sbuf_tensor("arena", [128, ARENA_WORDS], F32))
        ar = Arena(arena_t, ARENA_WORDS)
        banks = []
        for i in range(8):
            t = stack.enter_context(nc.psum_tensor(f"bank{i}", [128, 512], F32))
            banks.append(Tl(t[:, :], f"bank{i}"))
            banks[-1].res.excl = True
        bank_i = [0]

        def bank():
            b = banks[bank_i[0] % 8]
            bank_i[0] += 1
            return b

        def drain(gens):
            gens = list(gens)
            while gens:
                nxt = []
                for g_ in gens:
                    try:
                        next(g_)
                        nxt.append(g_)
                    except StopIteration:
                        pass
                gens = nxt

        dq = [0]

        def dmaq():
            dq[0] += 1
            return "sp" if dq[0] % 2 else "pool"

        ev = [0]

        def evac(out_ap, in_ap, reads, writes):
            ev[0] += 1
            if ev[0] % 2:
                return s.dve(CP(out_ap, in_ap), reads, writes)
            return s.act(ACP(out_ap, in_ap), reads, writes)

        cst = sbt("cst", C_N)
        s.dma("sp", cst.ap, cst_in, writes=[cst])
        ident = cst.ap[:, C_ID:C_ID + 128]
        id16 = sbt("id16", 128, BF16)
        s.dve(CP(id16.ap, ident), [cst], [id16])
        hsel16 = sbt("hsel16", 2, BF16)
        s.dve(CP(hsel16.ap, cst.ap[:, C_HSEL:C_HSEL + 2]), [cst], [hsel16])
        blk32 = cst.ap[:, C_BLK:C_BLK + 128]
        blk16 = sbt("blk16", 128, BF16)
        s.dve(CP(blk16.ap, cst.ap[:, C_BLK:C_BLK + 128]), [cst], [blk16])
        scanmask = cst.ap[:, C_SM:C_SM + 256]
        hgmask = sbt("hgmask", 256)
        rwmXY = sbt("rwmXY", 512)
        rwmZ = sbt("rwmZ", 256)
        for h in range(4):
            s.pool(CP(hgmask.ap[:, h * 64:(h + 1) * 64], cst.ap[:, C_LE:C_LE + 64]), [cst], [hgmask])
            s.pool(CP(rwmXY.ap[:, h * 128:h * 128 + 64], cst.ap[:, C_LT:C_LT + 64]), [cst], [rwmXY])
            s.pool(CP(rwmXY.ap[:, h * 128 + 64:h * 128 + 128], cst.ap[:, C_LE:C_LE + 64]), [cst], [rwmXY])
            s.pool(CP(rwmZ.ap[:, h * 64:(h + 1) * 64], cst.ap[:, C_GT:C_GT + 64]), [cst], [rwmZ])

        cols = sbt("cols", 96)
        K_SH, K_SC = 0, 8
        K_LB, K_OMLB, K_NOMLB = 16, 20, 24
        K_W0, K_A0, K_KK, K_KA, K_OMKA, K_RK, K_V0 = 28, 32, 36, 40, 44, 48, 52
        K_MU, K_OMU = 56, 73
        tmp_cols = sbt("tmp_cols", 64)
        gate_b = sbt("gate_b", 1024)
        hgg_b = sbt("hgg_b", 512)
        gng_b = sbt("gng_b", 512)
        gnb_b = sbt("gnb_b", 512)
        lng_b = sbt("lng_b", 1024)
        lnb_b = sbt("lnb_b", 1024)
        lora16 = sbt("lora16", 512, BF16)
        vup16 = sbt("vup16", 512, BF16)

        def colload(dst_c, src_ap_1d, n):
            s.dma(dmaq(), cols.ap[:, dst_c:dst_c + n], src_ap_1d.rearrange("(k p) -> p k", p=128),
                  writes=[cols], allow_slow_non_contiguous=True)

        def layer_setup(l):
            ar.reset()
            colload(K_W0, W["rw_w0"][l], 4)
            colload(K_A0, W["rw_a0"][l], 4)
            colload(K_KK, W["rw_k_k"][l], 4)
            colload(K_KA, W["rw_k_a"][l], 4)
            colload(K_RK, W["rw_r_k"][l], 4)
            colload(K_MU, W["rw_mu"][l], 17)
            if l >= 1:
                colload(K_V0, W["rw_v0"][l - 1], 4)
            s.dve(TS(cols.ap[:, K_OMU:K_OMU + 17], cols.ap[:, K_MU:K_MU + 17], -1.0, 1.0), [cols], [cols])
            s.dve(TS(cols.ap[:, K_OMKA:K_OMKA + 4], cols.ap[:, K_KA:K_KA + 4], -1.0, 1.0), [cols], [cols])
            if l == 0:
                s.dve(MS(cols.ap[:, K_LB:K_LB + 4], 0.0), [], [cols])
            else:
                s.dma(dmaq(), tmp_cols.ap[:, 0:4], W["hg_lower_bounds"][0].rearrange("(k p) -> p k", p=128),
                      writes=[tmp_cols], allow_slow_non_contiguous=True)
                s.dma(dmaq(), tmp_cols.ap[:, 4:8], W["hg_lower_bounds"][1].rearrange("(k p) -> p k", p=128),
                      writes=[tmp_cols], allow_slow_non_contiguous=True)
                s.dve(TT(tmp_cols.ap[:, 8:12], tmp_cols.ap[:, 4:8], tmp_cols.ap[:, 0:4], ALU.subtract),
                      [tmp_cols], [tmp_cols])
                s.act(ACTF(cols.ap[:, K_LB:K_LB + 4], tmp_cols.ap[:, 8:12], AF.Sigmoid), [tmp_cols], [cols])
            s.dve(TS(cols.ap[:, K_OMLB:K_OMLB + 4], cols.ap[:, K_LB:K_LB + 4], -1.0, 1.0), [cols], [cols])
            s.dve(TS(cols.ap[:, K_NOMLB:K_NOMLB + 4], cols.ap[:, K_OMLB:K_OMLB + 4], -1.0), [cols], [cols])
            if CUT == 1:
                s.barrier(); return
            s.dma(dmaq(), hgg_b.ap, W["hg_norm_g"][l].partition_broadcast(128), writes=[hgg_b])
            s.dma(dmaq(), gng_b.ap, W["rw_gn_g"][l].partition_broadcast(128), writes=[gng_b])
            s.dma(dmaq(), gnb_b.ap, W["rw_gn_b"][l].partition_broadcast(128), writes=[gnb_b])
            s.dma(dmaq(), lng_b.ap, W["ln_g"][l].partition_broadcast(128), writes=[lng_b])
            s.dma(dmaq(), lnb_b.ap, W["ln_b"][l].partition_broadcast(128), writes=[lnb_b])
            if CUT == 2:
                s.barrier(); return
            st = ar.f32("lst", 512)
            s.dma(dmaq(), st.ap[0:64, :], W["rw_w_up"][l], writes=[st])
            s.dma(dmaq(), st.ap[64:128, :], W["rw_a_up"][l], writes=[st])
            s.dve(CP(lora16.ap, st.ap), [st], [lora16])
            if l >= 1:
                st2 = ar.f32("lst2", 512)
                s.dma(dmaq(), st2.ap[0:32, :], W["rw_v_up"][l - 1], writes=[st2])
                s.dve(CP(vup16.ap[0:32, :], st2.ap[0:32, :]), [st2], [vup16])
            if CUT == 3:
                s.barrier(); return
            ccol = ar.f32("ccol", 8)
            s.dma(dmaq(), ccol.ap, c_in[0].rearrange("(k p) -> p k", p=128), writes=[ccol],
                  allow_slow_non_contiguous=True)
            scol = ar.f32("scol", 8)
            s.act(ACTF(scol.ap, ccol.ap, AF.Silu), [ccol], [scol])
            srep = ar.f32("srep", 8 * 128)
            s.dve(CP(srep.v3(8, 128), scol.ap.unsqueeze(2).to_broadcast([128, 8, 128])), [scol], [srep])
            bcol = ar.f32("bcol", 24)
            s.dma(dmaq(), bcol.ap, W["b_ada"][l].rearrange("(k p) -> p k", p=128), writes=[bcol],
                  allow_slow_non_contiguous=True)
            brow = ar.f32("brow", 3072)
            s.dma(dmaq(), brow.ap, W["b_ada"][l].partition_broadcast(128), writes=[brow])
            crow = ar.f32("crow", 2048)
            ctmp = ar.f32("ctmp", 1024)
            if CUT == 4:
                s.barrier(); return
            wst = [ar.f32(f"adaw{i}", 8 * 512) for i in range(2)]
            for cb in range(6):
                wt = wst[cb % 2]
                s.dma(dmaq(), wt.v3(8, 512),
                      W["w_ada"][l][:, cb * 512:(cb + 1) * 512].rearrange("(k p) c -> p k c", p=128), writes=[wt])
                pb = bank()
                for kc in range(8):
                    s.pe(MM(pb.ap[:, 0:512], srep.v3(8, 128)[:, kc, :], wt.v3(8, 512)[:, kc, :],
                            kc == 0, kc == 7), [wt, srep], [pb])
                if cb < 4:
                    s.dve(TT(crow.ap[:, cb * 512:(cb + 1) * 512], pb.ap[:, 0:512], brow.ap[:, cb * 512:(cb + 1) * 512],
                             ALU.add), [pb, brow], [crow])
                else:
                    g0 = (cb - 4) * 512
                    s.dve(TT(gate_b.ap[:, g0:g0 + 512], pb.ap[:, 0:512], brow.ap[:, cb * 512:(cb + 1) * 512], ALU.add),
                          [pb, brow], [gate_b])
            for which in range(2):
                s.dve(TT(ctmp.v3(8, 128), crow.ap[:, which * 1024:(which + 1) * 1024].rearrange("p (k c) -> p k c", k=8),
                         ident.unsqueeze(1).to_broadcast([128, 8, 128]), ALU.mult), [crow, cst], [ctmp])
                dstc = K_SH if which == 0 else K_SC
                s.dve(RED(cols.ap[:, dstc:dstc + 8], ctmp.v3(8, 128)), [ctmp], [cols])
            s.dve(TS(cols.ap[:, K_SC:K_SC + 8], cols.ap[:, K_SC:K_SC + 8], 1.0, None, ALU.add), [cols], [cols])
            s.dve(TS(gate_b.ap, gate_b.ap, 1.0, None, ALU.add), [gate_b], [gate_b])
            s.barrier()

        def fmcol(ci):
            if ci < 4:
                return ci * 128
            if ci < 8:
                return 512 + (ci - 4) * 128
            return 2048 + (ci - 8) * 128

        def tmcol(g):
            return (1024, 1536, 4224, 4736, 5248, 5760)[g]

        def pass_A(l, xin):
            ar.reset()
            w16 = ar.bf("w16", 8 * NIN)
            w16v = w16.v3(8, NIN)
            hT = [ar.bf(f"hT{i}", 8 * NB) for i in range(2)]
            xt = [ar.f32(f"xt{i}", 1024) for i in range(2)]
            wst = [ar.f32(f"wst{i}", 1568) for i in range(3)]
            fst = [ar.f32(f"fst{i}", 512) for i in range(3)]
            tst = [ar.f32(f"tst{i}", 3072) for i in range(2)]
            vd16 = ar.bf("vd16", 8 * 32) if l >= 1 else None
            n = 0
            for kc in range(8):
                for qd in range(4):
                    st = wst[n % 3]
                    s.dma(dmaq(), st.ap, W["w_in"][l][kc * 128:(kc + 1) * 128, qd * 1568:(qd + 1) * 1568],
                          writes=[st])
                    dst = w16v[:, kc, qd * 1568:(qd + 1) * 1568]
                    eng = ("dve", "act", "pool")[n % 3]
                    if eng == "act":
                        s.act(ACP(dst, st.ap), [st], [w16.r((kc, qd))])
                    else:
                        s.op(eng, CP(dst, st.ap), [st], [w16.r((kc, qd))])
                    n += 1
            if l >= 1:
                vst = ar.f32("vst", 8 * 32)
                s.dma(dmaq(), vst.v3(8, 32), W["rw_v_down"][l - 1].rearrange("(k p) c -> p k c", p=128),
                      writes=[vst])
                s.dve(CP(vd16.ap, vst.ap), [vst], [vd16])

            def wres(kc, c0, c1):
                return [w16.r((kc, q)) for q in range(c0 // 1568, (c1 - 1) // 1568 + 1)]

            nfm = 25
            fi = 0
            for blk in range(NBLK):
                t0 = blk * NB
                h = hT[blk % 2]
                hv = h.v3(8, NB)
                for tt in range(2):
                    xtile = xt[tt]
                    s.dma("sp", xtile.ap, xin[t0 + tt * 128:t0 + (tt + 1) * 128, :], writes=[xtile])
                    for half in range(2):
                        pb = bank()
                        for j in range(4):
                            kc = half * 4 + j
                            s.pe(TR(pb.ap[:, j * 128:(j + 1) * 128], xtile.ap[:, kc * 128:(kc + 1) * 128], ident),
                                 [xtile, cst], [pb])
                        for j in range(4):
                            kc = half * 4 + j
                            s.act(ACTF(hv[:, kc, tt * 128:(tt + 1) * 128], pb.ap[:, j * 128:(j + 1) * 128],
                                       AF.Identity, bias=cols.ap[:, K_SH + kc:K_SH + kc + 1],
                                       scale=cols.ap[:, K_SC + kc:K_SC + kc + 1]), [pb, cols], [h])
                for pr in range((nfm + 1) // 2):
                    cis = [ci for ci in (2 * pr, 2 * pr + 1) if ci < nfm]
                    pb = bank()
                    for j, ci in enumerate(cis):
                        c0 = fmcol(ci)
                        for kc in range(8):
                            s.pe(MM(pb.ap[:, j * NB:(j + 1) * NB], w16v[:, kc, c0:c0 + 128], hv[:, kc, :],
                                    kc == 0, kc == 7), wres(kc, c0, c0 + 128) + [h], [pb])
                    st = fst[fi % 3]
                    fi += 1
                    nn = len(cis) * NB
                    evac(st.ap[:, 0:nn], pb.ap[:, 0:nn], [pb], [st])
                    s.dma("pool", PF[cis[0]:cis[0] + len(cis), :, t0:t0 + NB].rearrange("c p t -> p c t"),
                          st.ap[:, 0:nn].rearrange("p (c t) -> p c t", c=len(cis)), reads=[st])
                if l >= 1:
                    pb = bank()
                    for kc in range(8):
                        s.pe(MM(pb.ap[0:32, 0:NB], vd16.v3(8, 32)[:, kc, :], hv[:, kc, :], kc == 0, kc == 7),
                             [vd16, h], [pb])
                    st = fst[fi % 3]
                    fi += 1
                    evac(st.ap[0:32, 0:NB], pb.ap[0:32, 0:NB], [pb], [st])
                    s.dma("pool", PF[25, 0:32, t0:t0 + NB], st.ap[0:32, 0:NB], reads=[st])
                for tt in range(2):
                    ts_ = tst[tt]
                    for g in range(6):
                        pb = bank()
                        c0 = tmcol(g)
                        for kc in range(8):
                            s.pe(MM(pb.ap[:, 0:512], hv[:, kc, tt * 128:(tt + 1) * 128], w16v[:, kc, c0:c0 + 512],
                                    kc == 0, kc == 7), wres(kc, c0, c0 + 512) + [h], [pb])
                        evac(ts_.ap[:, g * 512:(g + 1) * 512], pb.ap[:, 0:512], [pb], [ts_.r(g)])
                    s.dma("pool", PT[t0 + tt * 128:t0 + (tt + 1) * 128, :], ts_.ap,
                          reads=[ts_.r(g) for g in range(6)], writes=[ts_.r(g) for g in range(0)])
            s.barrier()

        def pass_B1(l):
            ar.reset()
            NCH = NB // CH
            S32 = [ar.f32(f"S32_{h}", 128) for h in range(4)]
            S16 = [ar.bf(f"S16_{h}", 128) for h in range(4)]
            for h in range(4):
                s.dve(MS(S32[h].ap, 0.0), [], [S32[h]])
                s.dve(MS(S16[h].ap, 0.0), [], [S16[h]])
            qf = [ar.f32(f"qf{i}", 8 * NB) for i in range(2)]
            tmv = [[ar.f32(f"tmv{i}_{tt}", 1024) for tt in range(2)] for i in range(2)]
            sig_ = [ar.f32(f"sig{i}", 4 * NB) for i in range(2)]
            fv_ = [ar.f32(f"fv{i}", 4 * NB) for i in range(2)]
            lgf_ = [ar.f32(f"lgf{i}", 4 * NB) for i in range(2)]
            khg_ = [ar.f32(f"khg{i}", 4 * NB) for i in range(2)]
            bcs_ = [ar.f32(f"bcs{i}", 4 * NB) for i in range(2)]
            eb_ = [ar.f32(f"eb{i}", 4 * NB) for i in range(2)]
            enb_ = [ar.f32(f"enb{i}", 4 * NB) for i in range(2)]
            kt32_ = [ar.f32(f"kt32{i}", 4 * NB) for i in range(2)]
            q16_ = [ar.bf(f"q16{i}", 4 * NB) for i in range(2)]
            k16_ = [ar.bf(f"k16{i}", 4 * NB) for i in range(2)]
            kh16_ = [ar.bf(f"kh16{i}", 4 * NB) for i in range(2)]
            v16 = [ar.bf(f"v16_{tt}", 512) for tt in range(2)]
            sT16 = [ar.bf(f"sT16_{tt}", 256) for tt in range(2)]
            khtm = [ar.bf(f"khtm_{tt}", 512) for tt in range(2)]
            osb_ = [ar.f32(f"osb{i}", 512) for i in range(2)]
            osq_ = [ar.f32(f"osq{i}", 512) for i in range(2)]
            ss_ = [ar.f32(f"ss{i}", 8) for i in range(2)]
            zs_ = [ar.f32(f"zs{i}", 512) for i in range(2)]
            ofin = [ar.f32(f"ofin{i}", 512) for i in range(2)]
            def prep_b1(blk):
                t0 = blk * NB
                sig, fv, lgf, khg, bcs = sig_[blk % 2], fv_[blk % 2], lgf_[blk % 2], khg_[blk % 2], bcs_[blk % 2]
                eb, enb, kt32 = eb_[blk % 2], enb_[blk % 2], kt32_[blk % 2]
                q16, k16, kh16 = q16_[blk % 2], k16_[blk % 2], kh16_[blk % 2]
                qt = qf[blk % 2]
                tm = tmv[blk % 2]
                s.dma("sp", qt.v3(8, NB), PF[0:8, :, t0:t0 + NB].rearrange("c p t -> p c t"), writes=[qt]); yield
                for tt in range(2):
                    s.dma("sp", tm[tt].ap, PT[t0 + tt * 128:t0 + (tt + 1) * 128, 0:1024], writes=[tm[tt]]); yield
                s.act(ACTF(sig.ap, qt.ap[:, 4 * NB:8 * NB], AF.Sigmoid), [qt], [sig]); yield
                for h in range(4):
                    sl = slice(h * NB, (h + 1) * NB)
                    s.dve(TS(fv.ap[:, sl], sig.ap[:, sl], cols.ap[:, K_OMLB + h:K_OMLB + h + 1],
                             cols.ap[:, K_LB + h:K_LB + h + 1]), [sig, cols], [fv]); yield
                    s.pool(TS(khg.ap[:, sl], sig.ap[:, sl], cols.ap[:, K_NOMLB + h:K_NOMLB + h + 1],
                              cols.ap[:, K_OMLB + h:K_OMLB + h + 1]), [sig, cols], [khg]); yield
                s.act(ACTF(lgf.ap, fv.ap, AF.Ln), [fv], [lgf]); yield
                for h in range(4):
                    sl = slice(h * NB, (h + 1) * NB)
                    s.dve(SCAN(bcs.ap[:, sl], scanmask, lgf.ap[:, sl]), [lgf, cst], [bcs]); yield
                s.act(ACTF(eb.ap, bcs.ap, AF.Exp), [bcs], [eb]); yield
                s.act(ACTF(enb.ap, bcs.ap, AF.Exp, scale=-1.0), [bcs], [enb]); yield
                s.dve(TT(q16.ap, qt.ap[:, 0:4 * NB], eb.ap, ALU.mult), [qt, eb], [q16]); yield
                s.dve(TT(kt32.ap, khg.ap, enb.ap, ALU.mult), [khg, enb], [kt32]); yield
                s.act(ACP(k16.ap, kt32.ap), [kt32], [k16]); yield
                e4 = eb.v4(4, NCH, CH)
                s.dve(TT(kh16.v4(4, NCH, CH), kt32.v4(4, NCH, CH), e4[:, :, :, CH - 1:CH].to_broadcast([128, 4, NCH, CH]),
                         ALU.mult), [kt32, eb], [kh16]); yield

            def drip(gen, n):
                if gen is None:
                    return
                for _ in range(n):
                    try:
                        next(gen)
                    except StopIteration:
                        return

            drain([prep_b1(0)])
            for blk in range(NBLK):
                t0 = blk * NB
                posts = []
                nxt = prep_b1(blk + 1) if blk + 1 < NBLK else None
                eb = eb_[blk % 2]
                q16, k16, kh16 = q16_[blk % 2], k16_[blk % 2], kh16_[blk % 2]
                tm = tmv[blk % 2]
                for tt in range(2):
                    osb, osq, ss, zs = osb_[tt], osq_[tt], ss_[tt], zs_[tt]
                    vt = tm[tt]
                    s.act(ACP(v16[tt].ap, vt.ap[:, 0:512]), [vt], [v16[tt]])
                    pbs = bank()
                    pbk = bank()
                    pbk16 = pbk.ap.bitcast(BF16)
                    for par in range(2):
                        cc = tt * 2 + par
                        ps_ = slice(par * 64, (par + 1) * 64)
                        for h in range(4):
                            c0 = h * NB + cc * CH
                            s.pe(MM(pbs.ap[ps_, h * 64:(h + 1) * 64], k16.ap[:, c0:c0 + CH], q16.ap[:, c0:c0 + CH]),
                                 [k16, q16], [pbs])
                            s.pe(TR(pbk16[ps_, h * 128:(h + 1) * 128], kh16.ap[:, c0:c0 + CH], id16.ap),
                                 [kh16, id16], [pbk])
                    s.dve(TT(sT16[tt].ap, pbs.ap[:, 0:256], hgmask.ap, ALU.mult), [pbs, hgmask], [sT16[tt]])
                    s.act(ACP(khtm[tt].ap, pbk16[:, 0:512]), [pbk], [khtm[tt]])
                    pbo = bank()
                    for par in range(2):
                        cc = tt * 2 + par
                        ps_ = slice(par * 64, (par + 1) * 64)
                        for h in range(4):
                            c0 = h * NB + cc * CH
                            hs = slice(h * 128, (h + 1) * 128)
                            s.pe(MM(pbo.ap[ps_, hs], sT16[tt].ap[ps_, h * 64:(h + 1) * 64], v16[tt].ap[ps_, hs],
                                    True, False), [sT16[tt], v16[tt]], [pbo])
                            s.pe(MM(pbo.ap[ps_, hs], q16.ap[:, c0:c0 + CH], S16[h].ap, False, True),
                                 [q16, S16[h]], [pbo])
                        pkv = bank()
                        for h in range(4):
                            hs = slice(h * 128, (h + 1) * 128)
                            s.pe(MM(pkv.ap[:, hs], khtm[tt].ap[ps_, hs], v16[tt].ap[ps_, hs]),
                                 [khtm[tt], v16[tt]], [pkv])
                        for h in range(4):
                            c0 = h * NB + cc * CH
                            hs = slice(h * 128, (h + 1) * 128)
                            s.dve(STT(S32[h].ap, S32[h].ap, eb.ap[:, c0 + CH - 1:c0 + CH], pkv.ap[:, hs],
                                      ALU.mult, ALU.add), [S32[h], eb, pkv], [S32[h]])
                            s.act(ACP(S16[h].ap, S32[h].ap), [S32[h]], [S16[h]])
                        drip(nxt, 7)
                    s.act(ACP(osb.ap, pbo.ap[:, 0:512]), [pbo], [osb])

                    def post_b1(osb, osq, ss, zs, vt, of, row0):
                        s.pool(TT(osq.ap, osb.ap, osb.ap, ALU.mult), [osb], [osq]); yield
                        s.dve(RED(ss.ap[:, 0:4], osq.v3(4, 128)), [osq], [ss]); yield
                        s.dve(TS(ss.ap[:, 0:4], ss.ap[:, 0:4], 1.0 / 128.0, RMS_EPS), [ss], [ss]); yield
                        s.act(ACTF(ss.ap[:, 0:4], ss.ap[:, 0:4], AF.Sqrt), [ss], [ss]); yield
                        s.dve(RCP(ss.ap[:, 4:8], ss.ap[:, 0:4]), [ss], [ss]); yield
                        s.act(ACTF(zs.ap, vt.ap[:, 512:1024], AF.Silu), [vt], [zs]); yield
                        s.pool(TT(zs.ap, zs.ap, hgg_b.ap, ALU.mult), [zs, hgg_b], [zs]); yield
                        s.dve(TT(of.v3(4, 128), osb.v3(4, 128), ss.ap[:, 4:8].unsqueeze(2).to_broadcast([128, 4, 128]),
                                 ALU.mult), [osb, ss], [of]); yield
                        s.dve(TT(of.ap, of.ap, zs.ap, ALU.mult), [of, zs], [of]); yield
                        s.dma("pool", OM[row0:row0 + 128, 0:512], of.ap, reads=[of]); yield

                    posts.append(post_b1(osb, osq, ss, zs, vt, ofin[tt], t0 + tt * 128))
                if nxt is not None:
                    drain([nxt])
                drain(posts)
            s.barrier()

        def pass_B2(l):
            ar.reset()
            NCH = NB // CH
            W4 = 4 * NB
            H32 = ar.f32("H32", 512)
            H16 = ar.bf("H16", 512)
            s.dve(MS(H32.ap, 0.0), [], [H32])
            s.dve(MS(H16.ap, 0.0), [], [H16])
            rwl = [ar.f32(f"rwl{i}", 17 * (NB + 1)) for i in range(1)]
            sh = ar.f32("sh", 17 * NB)
            shv = sh.v3(17, NB)
            lw = ar.f32("lw", W4)
            av = ar.f32("av", W4)
            bcs = ar.f32("bcs", W4)
            t1 = ar.f32("t1", W4)
            t2 = ar.f32("t2", W4)
            t3 = ar.f32("t3", W4)
            eb = ar.f32("eb", W4)
            enb = ar.f32("enb", W4)
            ebx = ar.f32("ebx", W4)
            egb = ar.f32("egb", W4)
            kkn = ar.f32("kkn", W4)
            kmod = ar.f32("kmod", W4)
            uu = ar.f32("uu", W4)
            PR16 = ar.bf("PR16", 2 * W4)
            PT16 = Tl(PR16.ap[:, 0:W4], "PT16v")
            RT16 = Tl(PR16.ap[:, W4:2 * W4], "RT16v")
            UT16 = ar.bf("UT16", W4)
            KT16 = ar.bf("KT16", W4)
            UTz = [ar.bf(f"UTz{i}", W4) for i in range(2)]
            KTz = [ar.bf(f"KTz{i}", W4) for i in range(2)]
            PTz = [ar.bf(f"PTz{i}", W4) for i in range(2)]
            for zt in UTz + KTz + PTz:
                s.pool(MS(zt.ap, 0.0), [], [zt])
            UG16 = ar.bf("UG16", W4)
            KG16 = ar.bf("KG16", W4)
            rk16 = ar.bf("rk16", W4)
            lo16 = ar.bf("lo16", NB)
            vr16 = ar.bf("vr16", NB) if l >= 1 else None
            vrs = ar.f32("vrs", NB) if l >= 1 else None
            vfl = ar.f32("vfl", W4) if l >= 1 else None
            vtm32 = ar.f32("vtm32", 512)
            vtm16 = ar.bf("vtm16", 512)
            zsl_ = [ar.f32(f"zsl{i}", 512) for i in range(2)]
            bon_ = [ar.f32(f"bon{i}", 512) for i in range(2)]
            AXY = [ar.bf(f"AXY{g}", 1024) for g in range(2)]
            RT32 = ar.f32("RT32", W4)
            XY = [[ar.bf(f"XY{g}_{i}", 512) for i in range(2)] for g in range(2)]
            RR = [[ar.bf(f"RR{g}_{i}", 512) for i in range(2)] for g in range(2)]
            QT16_ = [ar.bf(f"QT16{i}", 512) for i in range(2)]
            MT16_ = [ar.bf(f"MT16{i}", 512) for i in range(2)]
            UKG = [ar.bf(f"UKG{g}", 512) for g in range(2)]
            YL32_ = [ar.f32(f"YL32{i}", 512) for i in range(2)]
            N32_ = [ar.f32(f"N32{i}", 512) for i in range(2)]
            Ht = ar.f32("Ht", 256)
            ysb_ = [ar.f32(f"ysb{i}", 512) for i in range(2)]
            yc_ = [ar.f32(f"yc{i}", 512) for i in range(2)]
            ysq_ = [ar.f32("ysq", 512)] * 2
            st8_ = [ar.f32(f"st8{i}", 32) for i in range(2)]
            yfin = [ar.f32(f"yfin{i}", 512) for i in range(2)]
            cvf = cols.ap
            for blk in range(NBLK):
                t0 = blk * NB
                posts = []
                rl = rwl[0]
                rlv = rl.v3(17, NB + 1)
                if t0 == 0:
                    s.dve(MS(rlv[:, :, 0:1], 0.0), [], [rl])
                    s.dma("sp", rlv[:, :, 1:NB + 1], PF[8:25, :, 0:NB].rearrange("c p t -> p c t"), writes=[rl])
                else:
                    s.dma("sp", rlv, PF[8:25, :, t0 - 1:t0 + NB].rearrange("c p t -> p c t"), writes=[rl])
                for kc in range(17):
                    s.act(ACTF(shv[:, kc, :], rlv[:, kc, 1:NB + 1], AF.Identity,
                               scale=cvf[:, K_OMU + kc:K_OMU + kc + 1]), [rl, cols], [sh.r(kc)])
                    s.dve(STT(shv[:, kc, :], rlv[:, kc, 0:NB], cvf[:, K_MU + kc:K_MU + kc + 1], shv[:, kc, :],
                              ALU.mult, ALU.add), [rl, cols, sh.r(kc)], [sh.r(kc)])
                rsl = lambda kc: shv[:, kc, :]
                R_ = sh.ap[:, 0:W4]
                K_ = sh.ap[:, W4:2 * W4]
                V_ = sh.ap[:, 2 * W4:3 * W4]
                Z_ = sh.ap[:, 3 * W4:4 * W4]
                rR = [sh.r(kc) for kc in range(0, 4)]
                rK = [sh.r(kc) for kc in range(4, 8)]
                rV = [sh.r(kc) for kc in range(8, 12)]
                rZ = [sh.r(kc) for kc in range(12, 16)]
                s.act(ACTF(lo16.ap[0:64, :], shv[0:64, 16, :], AF.Tanh), [sh.r(16)], [lo16])
                s.dve(CP(lo16.ap[64:128, :], shv[64:128, 16, :]), [sh.r(16)], [lo16])
                pw = [bank(), bank()]
                pa = [bank(), bank()]
                for kc in range(4):
                    s.pe(MM(pw[kc // 2].ap[:, (kc % 2) * NB:(kc % 2 + 1) * NB], lora16.ap[0:64, kc * 128:(kc + 1) * 128],
                            lo16.ap[0:64, :]), [lora16, lo16], [pw[kc // 2]])
                    s.pe(MM(pa[kc // 2].ap[:, (kc % 2) * NB:(kc % 2 + 1) * NB], lora16.ap[64:128, kc * 128:(kc + 1) * 128],
                            lo16.ap[64:128, :]), [lora16, lo16], [pa[kc // 2]])
                for kc in range(4):
                    sl = slice(kc * NB, (kc + 1) * NB)
                    psl = slice((kc % 2) * NB, (kc % 2 + 1) * NB)
                    s.act(ACTF(lw.ap[:, sl], pw[kc // 2].ap[:, psl], AF.Sigmoid, bias=cvf[:, K_W0 + kc:K_W0 + kc + 1]),
                          [pw[kc // 2], cols], [lw])
                    s.act(ACTF(av.ap[:, sl], pa[kc // 2].ap[:, psl], AF.Sigmoid, bias=cvf[:, K_A0 + kc:K_A0 + kc + 1]),
                          [pa[kc // 2], cols], [av])
                s.dve(TS(lw.ap, lw.ap, -math.exp(-0.5), None, ALU.mult), [lw], [lw])
                if l == 0:
                    s.dma("pool", VF[:, :, t0:t0 + NB].rearrange("c p t -> p c t"), V_.rearrange("p (c t) -> p c t", c=4),
                          reads=rV)
                else:
                    s.dma("sp", vrs.ap[0:32, :], PF[25, 0:32, t0:t0 + NB], writes=[vrs])
                    s.dma("sp", vfl.v3(4, NB), VF[:, :, t0:t0 + NB].rearrange("c p t -> p c t"), writes=[vfl])
                    s.dve(CP(vr16.ap[0:32, :], vrs.ap[0:32, :]), [vrs], [vr16])
                    pv = [bank(), bank()]
                    for kc in range(4):
                        s.pe(MM(pv[kc // 2].ap[:, (kc % 2) * NB:(kc % 2 + 1) * NB], vup16.ap[0:32, kc * 128:(kc + 1) * 128],
                                vr16.ap[0:32, :]), [vup16, vr16], [pv[kc // 2]])
                    for kc in range(4):
                        sl = slice(kc * NB, (kc + 1) * NB)
                        psl = slice((kc % 2) * NB, (kc % 2 + 1) * NB)
                        s.act(ACTF(t1.ap[:, sl], pv[kc // 2].ap[:, psl], AF.Sigmoid,
                                   bias=cvf[:, K_V0 + kc:K_V0 + kc + 1]), [pv[kc // 2], cols], [t1])
                    s.pool(TT(vfl.ap, vfl.ap, V_, ALU.subtract), [vfl] + rV, [vfl])
                    s.dve(TT(vfl.ap, vfl.ap, t1.ap, ALU.mult), [vfl, t1], [vfl])
                    s.pool(TT(V_, V_, vfl.ap, ALU.add), [vfl] + rV, rV)
                for kc in range(4):
                    sl = slice(kc * NB, (kc + 1) * NB)
                    s.dve(SCAN(bcs.ap[:, sl], scanmask, lw.ap[:, sl]), [lw, cst], [bcs])
                s.dve(TT(t2.ap, bcs.ap, lw.ap, ALU.subtract), [bcs, lw], [t2])
                b4 = bcs.v4(4, NCH, CH)
                s.dve(TT(t3.v4(4, NCH, CH), b4[:, :, :, CH - 1:CH].to_broadcast([128, 4, NCH, CH]), b4, ALU.subtract),
                      [bcs], [t3])
                s.act(ACTF(eb.ap, bcs.ap, AF.Exp), [bcs], [eb])
                s.act(ACTF(enb.ap, bcs.ap, AF.Exp, scale=-1.0), [bcs], [enb])
                s.act(ACTF(ebx.ap, t2.ap, AF.Exp), [t2], [ebx])
                s.act(ACTF(egb.ap, t3.ap, AF.Exp), [t3], [egb])
                for kc in range(4):
                    sl = slice(kc * NB, (kc + 1) * NB)
                    s.act(ACTF(t1.ap[:, sl], K_[:, sl], AF.Square, scale=cvf[:, K_KK + kc:K_KK + kc + 1]),
                          [rK[kc], cols], [t1])
                pq = [bank(), bank()]
                for kc in range(4):
                    s.pe(MM(pq[kc // 2].ap[:, (kc % 2) * NB:(kc % 2 + 1) * NB], blk32, t1.ap[:, kc * NB:(kc + 1) * NB]),
                         [cst, t1], [pq[kc // 2]])
                for hf in range(2):
                    sl = slice(hf * 2 * NB, (hf + 1) * 2 * NB)
                    s.dve(TS(t2.ap[:, sl], pq[hf].ap[:, 0:2 * NB], 1e-24, None, ALU.max), [pq[hf]], [t2])
                s.act(ACTF(t2.ap, t2.ap, AF.Ln), [t2], [t2])
                s.act(ACTF(t3.ap, t2.ap, AF.Exp, scale=-0.5), [t2], [t3])
                for kc in range(4):
                    sl = slice(kc * NB, (kc + 1) * NB)
                    s.dve(STT(kkn.ap[:, sl], K_[:, sl], cvf[:, K_KK + kc:K_KK + kc + 1], t3.ap[:, sl], ALU.mult, ALU.mult),
                          [rK[kc], cols, t3], [kkn])
                    s.pool(TS(t1.ap[:, sl], av.ap[:, sl], cvf[:, K_KA + kc:K_KA + kc + 1],
                              cvf[:, K_OMKA + kc:K_OMKA + kc + 1]), [av, cols], [t1])
                s.dve(TT(kmod.ap, K_, t1.ap, ALU.mult), rK + [t1], [kmod])
                s.dve(TT(uu.ap, av.ap, kkn.ap, ALU.mult), [av, kkn], [uu])
                s.dve(STT(PT16.ap, kkn.ap, -1.0, ebx.ap, ALU.mult, ALU.mult), [kkn, ebx], [PT16])
                for hh_ in range(2):
                    hp_ = slice(hh_ * 64, (hh_ + 1) * 64)
                    s.dve(TT(UTz[hh_].ap[hp_, :], uu.ap[hp_, :], enb.ap[hp_, :], ALU.mult), [uu, enb], [UTz[hh_]])
                    s.dve(TT(KTz[hh_].ap[hp_, :], kmod.ap[hp_, :], enb.ap[hp_, :], ALU.mult), [kmod, enb], [KTz[hh_]])
                    s.act(ACP(PTz[hh_].ap[hp_, :], PT16.ap[hp_, :]), [PT16], [PTz[hh_]])
                s.pool(TT(RT32.ap, R_, eb.ap, ALU.mult), rR + [eb], [RT32])
                s.act(ACP(RT16.ap, RT32.ap), [RT32], [RT16])
                s.dve(TT(UG16.ap, uu.ap, egb.ap, ALU.mult), [uu, egb], [UG16])
                s.dve(TT(KG16.ap, kmod.ap, egb.ap, ALU.mult), [kmod, egb], [KG16])
                s.dve(TT(t2.ap, R_, kmod.ap, ALU.mult), rR + [kmod], [t2])
                for kc in range(4):
                    sl = slice(kc * NB, (kc + 1) * NB)
                    s.act(ACTF(rk16.ap[:, sl], t2.ap[:, sl], AF.Identity, scale=cvf[:, K_RK + kc:K_RK + kc + 1]),
                          [t2, cols], [rk16])

                if CUT == 10:
                    s.barrier(); return
                for tt in range(2):
                    zsl, bon, QT16, MT16, YL32 = zsl_[tt], bon_[tt], QT16_[tt], MT16_[tt], YL32_[tt]
                    N32, ysb, yc, ysq, st8 = N32_[tt], ysb_[tt], yc_[tt], ysq_[tt], st8_[tt]
                    tsl = lambda kc: slice(kc * NB + tt * 128, kc * NB + (tt + 1) * 128)
                    pvt = bank()
                    pzt = bank()
                    for kc in range(4):
                        s.pe(TR(pvt.ap[:, kc * 128:(kc + 1) * 128], V_[:, tsl(kc)], ident), rV + [cst], [pvt])
                    for kc in range(4):
                        s.pe(TR(pzt.ap[:, kc * 128:(kc + 1) * 128], Z_[:, tsl(kc)], ident), rZ + [cst], [pzt])
                    if CUT == 111:
                        s.barrier(); return
                    s.dve(CP(vtm32.ap, pvt.ap[:, 0:512]), [pvt], [vtm32])
                    s.act(ACP(vtm16.ap, pvt.ap[:, 0:512]), [pvt], [vtm16])
                    s.act(ACTF(zsl.ap, pzt.ap[:, 0:512], AF.Silu), [pzt], [zsl])
                    if CUT == 112:
                        s.barrier(); return
                    pst = bank()
                    for kc in range(4):
                        s.pe(MM(pst.ap[:, kc * 128:(kc + 1) * 128], rk16.ap[:, tsl(kc)], blk16.ap), [rk16, blk16], [pst])
                    s.dve(TT(bon.ap, pst.ap[:, 0:512], vtm32.ap, ALU.mult), [pst, vtm32], [bon])

                    if CUT == 11:
                        s.barrier(); return
                    def fm(tile16, h, par):
                        kc, hh = h // 2, h % 2
                        c0 = kc * NB + tt * 128 + par * 64
                        return tile16.ap[hh * 64:(hh + 1) * 64, c0:c0 + 64]

                    def ch(tile16, h, par):
                        kc = h // 2
                        c0 = kc * NB + tt * 128 + par * 64
                        return tile16.ap[:, c0:c0 + 64]

                    ids = lambda hh: id16.ap[:, hh * 64:(hh + 1) * 64]
                    PRv = PR16.ap.rearrange("p (two w) -> p two w", two=2)
                    G = [dict(), dict()]
                    rd = [PT16, RT16] + UTz + KTz + PTz
                    for g in range(2):
                        pX, pY, pZ = bank(), bank(), bank()
                        for par in range(2):
                            ps_ = slice(par * 64, (par + 1) * 64)
                            for hi in range(4):
                                h = g * 4 + hi
                                hh = h % 2
                                c0 = (h // 2) * NB + tt * 128 + par * 64
                                pr = PRv[:, :, c0:c0 + 64]
                                ox = pX.ap[ps_, hi * 128:(hi + 1) * 128].rearrange("p (a b) -> p a b", a=2)
                                oy = pY.ap[ps_, hi * 128:(hi + 1) * 128].rearrange("p (a b) -> p a b", a=2)
                                if MERGE_PR:
                                    s.pe(MM(ox, ch(UTz[hh], h, par), pr), rd, [pX])
                                    s.pe(MM(oy, ch(KTz[hh], h, par), pr), rd, [pY])
                                else:
                                    s.pe(MM(pX.ap[ps_, hi * 128:hi * 128 + 64], ch(UTz[hh], h, par), ch(PT16, h, par)), rd, [pX])
                                    s.pe(MM(pX.ap[ps_, hi * 128 + 64:hi * 128 + 128], ch(UTz[hh], h, par), ch(RT16, h, par)), rd, [pX])
                                    s.pe(MM(pY.ap[ps_, hi * 128:hi * 128 + 64], ch(KTz[hh], h, par), ch(PT16, h, par)), rd, [pY])
                                    s.pe(MM(pY.ap[ps_, hi * 128 + 64:hi * 128 + 128], ch(KTz[hh], h, par), ch(RT16, h, par)), rd, [pY])
                                s.pe(MM(pZ.ap[ps_, hi * 64:(hi + 1) * 64], ch(PTz[hh], h, par), ch(UTz[hh], h, par)), rd, [pZ])
                        G[g]["pA"] = (pX, pY, pZ)
                    for g in range(2):
                        pX, pY, pZ = G[g]["pA"]
                        A = AXY[g]
                        s.dve(TT(A.ap[:, 0:512], pX.ap[:, 0:512], rwmXY.ap, ALU.mult), [pX, rwmXY], [A.r(0)])
                        s.dve(TT(A.ap[:, 512:1024], pY.ap[:, 0:512], rwmXY.ap, ALU.mult), [pY, rwmXY], [A.r(1)])
                        xy = XY[g][0]
                        s.dve(TT(xy.v3(4, 128)[:, :, 0:64], pZ.ap[:, 0:256].rearrange("p (h c) -> p h c", h=4),
                                 rwmZ.v3(4, 64), ALU.mult), [pZ, rwmZ], [xy.r(0)])
                        s.act(ACP(xy.v3(4, 128)[:, :, 64:128], A.ap[:, 0:512].rearrange("p (h c) -> p h c", h=4)[:, :, 0:64]),
                              [A.r(0)], [xy.r(1)])
                    for g in range(2):
                        A = AXY[g]
                        pR = bank()
                        for par in range(2):
                            ps_ = slice(par * 64, (par + 1) * 64)
                            for hi in range(4):
                                h = g * 4 + hi
                                s.pe(MM(pR.ap[ps_, hi * 128:hi * 128 + 64], ch(PT16, h, par), ids(h % 2)), [PT16, id16], [pR])
                                s.pe(MM(pR.ap[ps_, hi * 128 + 64:hi * 128 + 128],
                                        A.ap[ps_, 512 + hi * 128:512 + hi * 128 + 64],
                                        vtm16.ap[ps_, h * 64:(h + 1) * 64]), [A.r(1), vtm16], [pR])
                        G[g]["pR"] = pR
                    for g in range(2):
                        evac(RR[g][0].ap, G[g]["pR"].ap[:, 0:512], [G[g]["pR"]], [RR[g][0]])
                    cur = 0
                    for j in range(6):
                        for g in range(2):
                            xy = XY[g][cur]
                            rr = RR[g][cur]
                            pR = bank()
                            for par in range(2):
                                ps_ = slice(par * 64, (par + 1) * 64)
                                for hi in range(4):
                                    s.pe(MM(pR.ap[ps_, hi * 128:(hi + 1) * 128], xy.ap[ps_, hi * 128 + 64:hi * 128 + 128],
                                            rr.ap[ps_, hi * 128:(hi + 1) * 128]), [xy.r(1), rr], [pR])
                            G[g]["pR"] = pR
                            if j < 5:
                                pS = bank()
                                for par in range(2):
                                    ps_ = slice(par * 64, (par + 1) * 64)
                                    for hi in range(4):
                                        X_ = xy.ap[ps_, hi * 128:hi * 128 + 64]
                                        Y_ = xy.ap[ps_, hi * 128 + 64:hi * 128 + 128]
                                        s.pe(MM(pS.ap[ps_, hi * 128:hi * 128 + 64], Y_, X_), [xy.r(0), xy.r(1)], [pS])
                                        s.pe(MM(pS.ap[ps_, hi * 128 + 64:hi * 128 + 128], X_, Y_), [xy.r(0), xy.r(1)], [pS])
                                G[g]["pS"] = pS
                        for g in range(2):
                            rr = RR[g][cur]
                            rrn = RR[g][1 - cur]
                            xyn = XY[g][1 - cur]
                            if j < 5:
                                s.act(ACP(xyn.ap, G[g]["pS"].ap[:, 0:512]), [G[g]["pS"]], [xyn.r(0), xyn.r(1)])
                            s.dve(TT(rrn.ap, G[g]["pR"].ap[:, 0:512], rr.ap, ALU.add), [G[g]["pR"], rr], [rrn])
                        cur = 1 - cur
                    for g in range(2):
                        pU = bank()
                        for par in range(2):
                            ps_ = slice(par * 64, (par + 1) * 64)
                            for hi in range(4):
                                h = g * 4 + hi
                                s.pe(MM(pU.ap[ps_, hi * 128:hi * 128 + 64], ch(UG16, h, par), ids(h % 2)), [UG16, id16], [pU])
                                s.pe(MM(pU.ap[ps_, hi * 128 + 64:hi * 128 + 128], ch(KG16, h, par), ids(h % 2)), [KG16, id16], [pU])
                        G[g]["pU"] = pU
                    for g in range(2):
                        s.act(ACP(UKG[g].ap, G[g]["pU"].ap[:, 0:512]), [G[g]["pU"]], [UKG[g]])
                    for g in range(2):
                        A = AXY[g]
                        wz = RR[g][cur]
                        ukg = UKG[g]
                        pQs = [bank(), bank()]
                        pNs = [bank(), bank()]
                        pYL = bank()
                        for par in range(2):
                            ps_ = slice(par * 64, (par + 1) * 64)
                            pQ = pQs[par]
                            pN = pNs[par]
                            for hi in range(4):
                                h = g * 4 + hi
                                kcl, hh = hi // 2, hi % 2
                                hb = slice(hh * 64, (hh + 1) * 64)
                                Wm = wz.ap[ps_, hi * 128:hi * 128 + 64]
                                Zm = wz.ap[ps_, hi * 128 + 64:hi * 128 + 128]
                                AruT = A.ap[ps_, hi * 128 + 64:hi * 128 + 128]
                                ArkT = A.ap[ps_, 512 + hi * 128 + 64:512 + hi * 128 + 128]
                                Ug = ukg.ap[ps_, hi * 128:hi * 128 + 64]
                                Kg = ukg.ap[ps_, hi * 128 + 64:hi * 128 + 128]
                                Vm = vtm16.ap[ps_, h * 64:(h + 1) * 64]
                                s.pe(MM(pQ.ap[hb, kcl * 64:(kcl + 1) * 64], Wm, AruT), [wz, A.r(0)], [pQ])
                                s.pe(MM(pQ.ap[hb, 128 + kcl * 64:128 + (kcl + 1) * 64], Wm, Ug), [wz, ukg], [pQ])
                                s.pe(MM(pN.ap[hb, kcl * 64:(kcl + 1) * 64], Ug, Zm, True, False), [ukg, wz], [pN])
                                s.pe(MM(pN.ap[hb, kcl * 64:(kcl + 1) * 64], Kg, Vm, False, True), [ukg, vtm16], [pN])
                                s.pe(MM(pYL.ap[ps_, hi * 64:(hi + 1) * 64], AruT, Zm, True, False), [A.r(0), wz], [pYL])
                                s.pe(MM(pYL.ap[ps_, hi * 64:(hi + 1) * 64], ArkT, Vm, False, True), [A.r(1), vtm16], [pYL])
                        gs = slice(g * 256, (g + 1) * 256)
                        for par in range(2):
                            qo = QT16.ap[:, gs].rearrange("p (k a t) -> p k a t", k=2, a=2)[:, :, par, :]
                            rt = RT32.v3(4, NB)[:, 2 * g:2 * g + 2, tt * 128 + par * 64:tt * 128 + (par + 1) * 64]
                            s.dve(TT(qo, pQs[par].ap[:, 0:128].rearrange("p (k t) -> p k t", k=2), rt, ALU.add),
                                  [pQs[par], RT32], [QT16.r(g)])
                            mo = MT16.ap[:, g * 256 + par * 128:g * 256 + (par + 1) * 128]
                            s.act(ACP(mo, pQs[par].ap[:, 128:256]), [pQs[par]], [MT16.r(g)])
                            no = N32.ap[:, g * 256 + par * 128:g * 256 + (par + 1) * 128]
                            s.act(ACP(no, pNs[par].ap[:, 0:128]), [pNs[par]], [N32.r(g)])
                        s.dve(CP(YL32.ap[:, gs], pYL.ap[:, 0:256]), [pYL], [YL32.r(g)])
                    pYh = [bank(), bank()]
                    Hv = H32.ap[:, 0:256]
                    H16v = H16.ap[:, 0:256]
                    for par in range(2):
                        ps_ = slice(par * 64, (par + 1) * 64)
                        pH = bank()
                        for h in range(8):
                            kc, hh = h // 2, h % 2
                            g, kcl = kc // 2, kc % 2
                            hb = slice(hh * 64, (hh + 1) * 64)
                            qc = g * 256 + kcl * 128 + par * 64
                            mc = g * 256 + par * 128 + kcl * 64
                            Hh = H16v[hb, kc * 64:(kc + 1) * 64]
                            s.pe(MM(pYh[hh].ap[ps_, kc * 64:(kc + 1) * 64], QT16.ap[hb, qc:qc + 64], Hh), [QT16.r(g), H16], [pYh[hh]])
                            s.pe(MM(pH.ap[hb, kc * 64:(kc + 1) * 64], MT16.ap[hb, mc:mc + 64], Hh), [MT16.r(g), H16], [pH])
                        clast = tt * 128 + par * 64 + 63
                        gcb = eb.v3(4, NB)[:, :, clast:clast + 1].to_broadcast([128, 4, 64])
                        s.pool(TT(Ht.v3(4, 64), H32.ap[:, 0:256].rearrange("p (k v) -> p k v", k=4), gcb, ALU.mult), [H32, eb], [Ht])
                        n4 = N32.ap.rearrange("p (g a k v) -> p g a k v", g=2, a=2, k=2)[:, :, par]
                        s.pool(TT(Ht.ap.rearrange("p (g k v) -> p g k v", g=2, k=2), Ht.ap.rearrange("p (g k v) -> p g k v", g=2, k=2),
                                  n4, ALU.add), [Ht, N32.r(0), N32.r(1)], [Ht])
                        s.dve(TT(Hv, pH.ap[:, 0:256], Ht.ap, ALU.add), [pH, Ht], [H32])
                        s.act(ACP(H16v, Hv), [H32], [H16])
                    if CUT == 13:
                        s.barrier(); return
                    for hh in range(2):
                        yo = ysb.ap.rearrange("p (k a v) -> p k a v", k=4, a=2)[:, :, hh, :]
                        yl = YL32.ap.rearrange("p (k a v) -> p k a v", k=4, a=2)[:, :, hh, :]
                        s.dve(TT(yo, pYh[hh].ap[:, 0:256].rearrange("p (k v) -> p k v", k=4), yl, ALU.add),
                              [pYh[hh], YL32.r(0), YL32.r(1)], [ysb])
                    def post_b2(ysb, st8, yc, ysq, bon, zsl, yf, row0):
                        s.dve(RED(st8.ap[:, 0:8], ysb.v3(8, 64)), [ysb], [st8]); yield
                        s.dve(TS(st8.ap[:, 0:8], st8.ap[:, 0:8], -1.0 / 64.0, None, ALU.mult), [st8], [st8]); yield
                        s.dve(TT(yc.v3(8, 64), ysb.v3(8, 64), st8.ap[:, 0:8].unsqueeze(2).to_broadcast([128, 8, 64]), ALU.add),
                              [ysb, st8], [yc]); yield
                        s.pool(TT(yf.ap, yc.ap, yc.ap, ALU.mult), [yc], [yf]); yield
                        s.dve(RED(st8.ap[:, 8:16], yf.v3(8, 64)), [yf], [st8]); yield
                        s.dve(TS(st8.ap[:, 8:16], st8.ap[:, 8:16], 1.0 / 64.0, GN_EPS), [st8], [st8]); yield
                        s.act(ACTF(st8.ap[:, 8:16], st8.ap[:, 8:16], AF.Sqrt), [st8], [st8]); yield
                        s.dve(RCP(st8.ap[:, 16:24], st8.ap[:, 8:16]), [st8], [st8]); yield
                        s.dve(TT(yc.v3(8, 64), yc.v3(8, 64), st8.ap[:, 16:24].unsqueeze(2).to_broadcast([128, 8, 64]), ALU.mult),
                              [yc, st8], [yc]); yield
                        s.pool(TT(yc.ap, yc.ap, gng_b.ap, ALU.mult), [yc, gng_b], [yc]); yield
                        s.pool(TT(yc.ap, yc.ap, gnb_b.ap, ALU.add), [yc, gnb_b], [yc]); yield
                        s.pool(TT(yc.ap, yc.ap, bon.ap, ALU.add), [yc, bon], [yc]); yield
                        s.dve(TT(yf.ap, yc.ap, zsl.ap, ALU.mult), [yc, zsl], [yf]); yield
                        s.dma("pool", OM[row0:row0 + 128, 512:1024], yf.ap, reads=[yf]); yield

                    posts.append(post_b2(ysb, st8, yc, ysq, bon, zsl, yfin[tt], t0 + tt * 128))
                drain(posts)
            s.barrier()

        def pass_C(l, xin, xout):
            ar.reset()
            wA = ar.bf("wA", 8 * 1024)
            wO = ar.bf("wO", 8 * 1024)
            wst = [ar.f32(f"cwst{i}", 1024) for i in range(3)]
            n = 0
            for kc in range(8):
                for which in range(2):
                    st = wst[n % 3]
                    if which == 0:
                        src = (W["w_branch_hg"][l] if kc < 4 else W["w_branch_rw"][l])[(kc % 4) * 128:(kc % 4 + 1) * 128, :]
                        dst = wA.v3(8, 1024)[:, kc, :]
                        dr = wA.r(kc)
                    else:
                        src = W["w_out"][l][kc * 128:(kc + 1) * 128, :]
                        dst = wO.v3(8, 1024)[:, kc, :]
                        dr = wO.r(kc)
                    s.dma(dmaq(), st.ap, src, writes=[st])
                    if n % 2:
                        s.dve(CP(dst, st.ap), [st], [dr])
                    else:
                        s.act(ACP(dst, st.ap), [st], [dr])
                    n += 1
            om = [ar.f32(f"om{i}", 1024) for i in range(2)]
            xt = [ar.f32(f"cx{i}", 1024) for i in range(2)]
            gt = [ar.f32(f"gt{i}", 2048) for i in range(2)]
            omT_ = [ar.bf(f"omT{i}", 1024) for i in range(2)]
            mg_ = [ar.f32(f"mg{i}", 1024) for i in range(2)]
            m2_ = [ar.f32(f"m2{i}", 1024) for i in range(2)]
            mT_ = [ar.bf(f"mT{i}", 1024) for i in range(2)]
            zr_ = [ar.f32(f"zr{i}", 1024) for i in range(2)]
            bst_ = [ar.f32(f"bst{i}", 16) for i in range(2)]
            xo = [ar.f32(f"xo{i}", 1024) for i in range(2)]
            for it in range(T // 128):
                r0 = it * 128
                omT, mg, m2, mT, zr, bst = omT_[it % 2], mg_[it % 2], m2_[it % 2], mT_[it % 2], zr_[it % 2], bst_[it % 2]
                o_ = om[it % 2]
                x_ = xt[it % 2]
                g_ = gt[it % 2]
                s.dma("sp", o_.ap, OM[r0:r0 + 128, :], writes=[o_])
                s.dma("sp", x_.ap, xin[r0:r0 + 128, :], writes=[x_])
                s.dma("sp", g_.ap, PT[r0:r0 + 128, 1024:3072], writes=[g_])
                for half in range(2):
                    pb = bank()
                    for j in range(4):
                        kc = half * 4 + j
                        s.pe(TR(pb.ap[:, j * 128:(j + 1) * 128], o_.ap[:, kc * 128:(kc + 1) * 128], ident), [o_, cst], [pb])
                    evac(omT.ap[:, half * 512:(half + 1) * 512], pb.ap[:, 0:512], [pb], [omT.r(half)])
                s.act(ACTF(g_.ap, g_.ap, AF.Sigmoid), [g_], [g_])
                omv = omT.v3(8, 128)
                for hf in range(2):
                    cs_ = slice(hf * 512, (hf + 1) * 512)
                    pa_ = bank()
                    pb_ = bank()
                    for kc in range(4):
                        s.pe(MM(pa_.ap[:, 0:512], omv[:, kc, :], wA.v3(8, 1024)[:, kc, cs_], kc == 0, kc == 3),
                             [omT.r(0), wA.r(kc)], [pa_])
                    for kc in range(4, 8):
                        s.pe(MM(pb_.ap[:, 0:512], omv[:, kc, :], wA.v3(8, 1024)[:, kc, cs_], kc == 4, kc == 7),
                             [omT.r(1), wA.r(kc)], [pb_])
                    s.dve(TT(mg.ap[:, cs_], pa_.ap[:, 0:512], g_.ap[:, hf * 512:(hf + 1) * 512], ALU.mult), [pa_, g_], [mg.r(hf)])
                    s.dve(TT(m2.ap[:, cs_], pb_.ap[:, 0:512], g_.ap[:, 1024 + hf * 512:1024 + (hf + 1) * 512], ALU.mult),
                          [pb_, g_], [m2.r(hf)])
                    s.dve(TT(mg.ap[:, cs_], mg.ap[:, cs_], m2.ap[:, cs_], ALU.add), [mg.r(hf), m2.r(hf)], [mg.r(hf)])
                for half in range(2):
                    pb = bank()
                    for j in range(4):
                        kc = half * 4 + j
                        s.pe(TR(pb.ap[:, j * 128:(j + 1) * 128], mg.ap[:, kc * 128:(kc + 1) * 128], ident), [mg.r(half), cst], [pb])
                    evac(mT.ap[:, half * 512:(half + 1) * 512], pb.ap[:, 0:512], [pb], [mT.r(half)])
                mv = mT.v3(8, 128)
                for hf in range(2):
                    cs_ = slice(hf * 512, (hf + 1) * 512)
                    po_ = bank()
                    for kc in range(8):
                        s.pe(MM(po_.ap[:, 0:512], mv[:, kc, :], wO.v3(8, 1024)[:, kc, cs_], kc == 0, kc == 7),
                             [mT.r(kc // 4), wO.r(kc)], [po_])
                    s.dve(TT(zr.ap[:, cs_], po_.ap[:, 0:512], gate_b.ap[:, cs_], ALU.mult), [po_, gate_b], [zr.r(hf)])
                    s.dve(STT(zr.ap[:, cs_], x_.ap[:, cs_], DN_ALPHA, zr.ap[:, cs_], ALU.mult, ALU.add), [x_, zr.r(hf)], [zr.r(hf)])
                    s.dve(BNS(bst.ap[:, hf * 6:(hf + 1) * 6], zr.ap[:, cs_]), [zr.r(hf)], [bst])
                s.dve(BNA(bst.ap[:, 12:14], bst.ap[:, 0:12]), [bst], [bst])
                s.dve(TS(bst.ap[:, 14:15], bst.ap[:, 13:14], LN_EPS, None, ALU.add), [bst], [bst])
                s.act(ACTF(bst.ap[:, 14:15], bst.ap[:, 14:15], AF.Sqrt), [bst], [bst])
                s.dve(RCP(bst.ap[:, 15:16], bst.ap[:, 14:15]), [bst], [bst])
                xo_ = xo[it % 2]
                s.dve(TS(xo_.ap, zr.ap, bst.ap[:, 12:13], bst.ap[:, 15:16], ALU.subtract, ALU.mult),
                      [zr.r(0), zr.r(1), bst], [xo_])
                s.dve(TT(xo_.ap, xo_.ap, lng_b.ap, ALU.mult), [xo_, lng_b], [xo_])
                s.dve(TT(xo_.ap, xo_.ap, lnb_b.ap, ALU.add), [xo_, lnb_b], [xo_])
                s.dma("pool", xout[r0:r0 + 128, :], xo_.ap, reads=[xo_])
            s.barrier()

        s.barrier()
        stages = []
        for l in range(NL):
            xin = x_in if l == 0 else X1
            xout = out if l == NL - 1 else X1
            stages += [("setup", l), ("A", l), ("B1", l), ("B2", l), ("C", l)]
        for name, l in stages:
            xin = x_in if l == 0 else X1
            xout = out if l == NL - 1 else X1
            if name == "setup":
                layer_setup(l)
            elif name == "A":
                pass_A(l, xin)
            elif name == "B1":
                pass_B1(l)
            elif name == "B2":
                pass_B2(l)
            else:
                pass_C(l, xin, xout)
            if stop_after == (name, l):
                break
        s.barrier()
        s.emit()
    return nc, s


_CACHE = {}


def _prep_weights(inputs):
    m = {}
    for n in WEIGHT_NAMES:
        a = np.ascontiguousarray(np.asarray(inputs[n], dtype=np.float32))
        m[n] = a.reshape(WEIGHT_SHAPES[n])
    m["consts"] = make_consts()
    return m


def kernel(**inputs):
    x = np.asarray(inputs["x"], dtype=np.float32)
    c = np.asarray(inputs["c"], dtype=np.float32)
    B, T, _ = x.shape
    key = (T,)
    if key not in _CACHE:
        _CACHE[key] = build(T)[0]
    nc = _CACHE[key]
    wm = _prep_weights(inputs)
    in_maps = []
    for core in range(8):
        b = core % B
        mm = dict(wm)
        mm["x"] = np.ascontiguousarray(x[b])
        mm["c"] = np.ascontiguousarray(c[b:b + 1])
        in_maps.append(mm)
    res = run_bass_kernel_spmd(nc, in_maps, core_ids=list(range(8)))
    return np.stack([res.results[b]["out"] for b in range(B)], axis=0).astype(np.float32)
```

```python
import contextlib
import math
import numpy as np
import concourse.bass as bass
import concourse.mybir as mybir
from concourse.bass_utils import run_bass_kernel_spmd

F32 = mybir.dt.float32
BF16 = mybir.dt.bfloat16
AF = mybir.ActivationFunctionType
ALU = mybir.AluOpType
AX = mybir.AxisListType

ENGS = ("pe", "dve", "act", "pool", "sp")
EPOCH = 7000
NDMASEM = 8

D = 1024
NIN = 6272
DN_ALPHA = 4.0 ** 0.25
LN_EPS = 1e-5
RMS_EPS = 1e-6
GN_EPS = 64e-5
NB = 256
import os
CUT = int(os.environ.get('KCUT', '0'))
SKIP = int(os.environ.get('KSKIP', '0'))
SKIPM = int(os.environ.get('KSKIPM', '0'))
KV = int(os.environ.get('KV', '0'))
MERGE_PR = int(os.environ.get('KMERGE', '1'))
CH = 64


class Res:
    __slots__ = ("last_w", "readers", "name", "excl")

    def __init__(self, name="", excl=False):
        self.last_w = None
        self.readers = []
        self.name = name
        self.excl = excl


class Tl:
    def __init__(self, ap, name):
        self.ap = ap
        self.res = Res(name)
        self.name = name
        self.sub = {}

    def v3(self, a, b):
        return self.ap.rearrange("p (a b) -> p a b", a=a, b=b)

    def v4(self, a, b, c):
        return self.ap.rearrange("p (a b c) -> p a b c", a=a, b=b, c=c)

    def r(self, key):
        x = self.sub.get(key)
        if x is None:
            x = Res(f"{self.name}:{key}")
            self.sub[key] = x
        return x


class Sched:
    def __init__(self, nc, stack):
        self.nc = nc
        self.stack = stack
        self.q = {e: [] for e in ENGS}
        self.cnt = {e: 0 for e in ENGS}
        self.seen = {e: {} for e in ENGS}
        self.esem = {}
        self.dsem = {}
        self.dcnt = {}
        self.dnext = {e: 0 for e in ENGS}
        self.same_engine_sync = {"pe": False, "dve": True, "act": True, "pool": True, "sp": True}
        self.nops = 0
        self.hook = None
        self.in_hook = False
        self.hcount = 0
        self.hevery = 3

    def sem(self, name):
        return self.stack.enter_context(self.nc.semaphore(name))

    def R(self, obj):
        if isinstance(obj, Res):
            return obj
        return obj.res

    def _esem(self, eng, count):
        ep = (count - 1) // EPOCH
        k = (eng, ep)
        if k not in self.esem:
            self.esem[k] = self.sem(f"e_{eng}_{ep}")
        return self.esem[k], count - ep * EPOCH

    def _add_wait(self, waits, eng, dep):
        key, val = dep
        if key[0] == "eng" and key[1] == eng and not self.same_engine_sync[eng]:
            return
        if self.seen[eng].get(key, 0) >= val:
            return
        if val > waits.get(key, 0):
            waits[key] = val

    def _lower_waits(self, eng, waits):
        wl = []
        for key, val in waits.items():
            self.seen[eng][key] = val
            if key[0] == "eng":
                wl.append(self._esem(key[1], val))
            else:
                wl.append((self.dsem[(key[1], key[2])], val))
        return wl

    def op(self, eng, fn, reads=(), writes=(), dma=False):
        if self.hook is not None and not self.in_hook:
            self.hcount += 1
            if self.hcount % self.hevery == 0:
                self.in_hook = True
                try:
                    self.hook()
                finally:
                    self.in_hook = False
        reads = [self.R(r) for r in reads]
        writes = [self.R(w) for w in writes]
        ex = [r for r in reads if r.excl]
        if ex:
            reads = [r for r in reads if not r.excl]
            writes = writes + [r for r in ex if r not in writes]
        waits = {}
        for r in reads:
            if r.last_w is not None:
                self._add_wait(waits, eng, r.last_w)
        for w in writes:
            if w.last_w is not None:
                self._add_wait(waits, eng, w.last_w)
            for rd in w.readers:
                self._add_wait(waits, eng, rd)
        if dma:
            idx = self.dnext[eng]
            self.dnext[eng] = (idx + 1) % NDMASEM
            dk = (eng, idx)
            if dk not in self.dsem:
                self.dsem[dk] = self.sem(f"d_{eng}_{idx}")
                self.dcnt[dk] = 0
            key = ("dma", eng, idx)
            if self.dcnt[dk] > 0:
                self._add_wait(waits, eng, (key, self.dcnt[dk] * 16))
            self.dcnt[dk] += 1
            done = (key, self.dcnt[dk] * 16)
            inc = (self.dsem[dk], 16)
        else:
            self.cnt[eng] += 1
            c = self.cnt[eng]
            done = (("eng", eng), c)
            inc = (self._esem(eng, c)[0], 1)
        wl = self._lower_waits(eng, waits)
        self.q[eng].append((fn, wl, inc))
        self.nops += 1
        for r in reads:
            r.readers.append(done)
            if len(r.readers) > 64:
                best = {}
                for k, v in r.readers:
                    if v > best.get(k, 0):
                        best[k] = v
                r.readers = list(best.items())
        for w in writes:
            w.last_w = done
            w.readers = []
        return done

    def pe(self, fn, reads=(), writes=()):
        return self.op("pe", fn, reads, writes)

    def dve(self, fn, reads=(), writes=()):
        return self.op("dve", fn, reads, writes)

    def act(self, fn, reads=(), writes=()):
        return self.op("act", fn, reads, writes)

    def pool(self, fn, reads=(), writes=()):
        return self.op("pool", fn, reads, writes)

    def dma(self, eng, out, in_, reads=(), writes=(), **kw):
        return self.op(eng, lambda e: e.dma_start(out=out, in_=in_, **kw), reads, writes, dma=True)

    def wait_all(self, eng, deps):
        waits = {}
        for d in deps:
            self._add_wait(waits, eng, d)
        wl = self._lower_waits(eng, waits)
        if wl:
            self.q[eng].append((None, wl, None))

    def barrier(self):
        deps = [(("eng", e), self.cnt[e]) for e in ENGS if self.cnt[e] > 0]
        deps += [(("dma", k[0], k[1]), v * 16) for k, v in self.dcnt.items()]
        for e in ENGS:
            self.wait_all(e, deps)

    def emit(self):
        nc = self.nc
        with nc.Block() as block:
            def run(eng_name):
                def body(eng):
                    for fn, wl, inc in self.q[eng_name]:
                        for sm, v in wl:
                            eng.wait_ge(sm, v)
                        if fn is not None:
                            fn(eng).then_inc(inc[0], inc[1])
                return body
            block.sync(run("sp"))
            block.tensor(run("pe"))
            block.vector(run("dve"))
            block.scalar(run("act"))
            block.gpsimd(run("pool"))


def MM(out, lhsT, rhs, st=True, sp=True):
    return lambda e: e.matmul(out, lhsT, rhs, start=st, stop=sp)


def TR(out, in_, ident):
    return lambda e: e.transpose(out, in_, ident)


def ACTF(out, in_, func, bias=0.0, scale=1.0):
    return lambda e: e.activation(out=out, in_=in_, func=func, bias=bias, scale=scale)


def CP(out, in_):
    return lambda e: e.tensor_copy(out=out, in_=in_)


def ACP(out, in_):
    return lambda e: e.copy(out=out, in_=in_)


def TT(out, a, b, op):
    return lambda e: e.tensor_tensor(out=out, in0=a, in1=b, op=op)


def TS(out, in0, s1, s2=None, op0=ALU.mult, op1=ALU.add):
    if s2 is None:
        return lambda e: e.tensor_scalar(out=out, in0=in0, scalar1=s1, scalar2=None, op0=op0)
    return lambda e: e.tensor_scalar(out=out, in0=in0, scalar1=s1, scalar2=s2, op0=op0, op1=op1)


def STT(out, in0, scalar, in1, op0, op1):
    return lambda e: e.scalar_tensor_tensor(out=out, in0=in0, scalar=scalar, in1=in1, op0=op0, op1=op1)


def MS(ap, v):
    return lambda e: e.memset(ap, v)


def RED(out, in_, op=ALU.add):
    return lambda e: e.tensor_reduce(out=out, in_=in_, axis=AX.X, op=op)


def RCP(out, in_):
    return lambda e: e.reciprocal(out=out, in_=in_)


def SCAN(out, d0, d1):
    return lambda e: e.tensor_tensor_scan(out=out, data0=d0, data1=d1, initial=0.0, op0=ALU.mult, op1=ALU.add)


def BNS(out, in_):
    return lambda e: e.bn_stats(out=out, in_=in_)


def BNA(out, in_):
    return lambda e: e.bn_aggr(out=out, in_=in_)


C_ID = 0
C_BLK = 128
C_HSEL = 256
C_LE = 258
C_LT = 322
C_GT = 386
C_SM = 450
C_N = 450 + 256


def make_consts():
    cst = np.zeros((128, C_N), np.float32)
    p = np.arange(128)[:, None]
    j = np.arange(128)[None, :]
    cst[:, C_ID:C_ID + 128] = (p == j)
    cst[:, C_BLK:C_BLK + 128] = (p // 64 == j // 64)
    cst[:, C_HSEL:C_HSEL + 2] = (p // 64 == np.arange(2)[None, :])
    j64 = np.arange(64)[None, :]
    cst[:, C_LE:C_LE + 64] = ((p % 64) <= j64)
    cst[:, C_LT:C_LT + 64] = ((p % 64) < j64)
    cst[:, C_GT:C_GT + 64] = ((p % 64) > j64)
    cst[:, C_SM:C_SM + 256] = ((np.arange(256)[None, :] % 64) != 0)
    return cst


class Arena:
    def __init__(self, tensor, words):
        self.t = tensor
        self.words = words
        self.off = 0
        self.gen = 0

    def reset(self):
        self.off = 0
        self.gen += 1

    def f32(self, name, cols, parts=128):
        assert self.off + cols <= self.words, (name, self.off, cols, self.words)
        ap = self.t[0:parts, self.off:self.off + cols]
        self.off += (cols + 15) // 16 * 16
        return Tl(ap, f"{name}_{self.gen}")

    def bf(self, name, cols, parts=128):
        w = (cols + 1) // 2
        assert self.off + w <= self.words, (name, self.off, w, self.words)
        ap = self.t[0:parts, self.off:self.off + w].bitcast(BF16)[:, 0:cols]
        self.off += (w + 15) // 16 * 16
        return Tl(ap, f"{name}_{self.gen}")


WEIGHT_NAMES = ["w_ada", "b_ada", "w_in", "hg_lower_bounds", "hg_norm_g", "rw_mu", "rw_w0", "rw_w_up",
                "rw_a0", "rw_a_up", "rw_k_k", "rw_k_a", "rw_r_k", "rw_v0", "rw_v_down", "rw_v_up",
                "rw_gn_g", "rw_gn_b", "w_branch_hg", "w_branch_rw", "w_out", "ln_g", "ln_b"]
WEIGHT_SHAPES = {
    "w_ada": [2, 1024, 3072], "b_ada": [2, 3072], "w_in": [2, 1024, NIN], "hg_lower_bounds": [2, 512],
    "hg_norm_g": [2, 512], "rw_mu": [2, 2176], "rw_w0": [2, 512], "rw_w_up": [2, 64, 512], "rw_a0": [2, 512],
    "rw_a_up": [2, 64, 512], "rw_k_k": [2, 512], "rw_k_a": [2, 512], "rw_r_k": [2, 512], "rw_v0": [1, 512],
    "rw_v_down": [1, 1024, 32], "rw_v_up": [1, 32, 512], "rw_gn_g": [2, 512], "rw_gn_b": [2, 512],
    "w_branch_hg": [2, 512, 1024], "w_branch_rw": [2, 512, 1024], "w_out": [2, 1024, 1024],
    "ln_g": [2, 1024], "ln_b": [2, 1024],
}

ARENA_WORDS = 45500


def build(T, NL=2, dbg=False, stop_after=None):
    assert T % NB == 0
    NBLK = T // NB
    nc = bass.Bass("TRN2", target_bir_lowering=False)
    x_in = nc.dram_tensor("x", [T, D], F32, kind="ExternalInput").ap()
    c_in = nc.dram_tensor("c", [1, D], F32, kind="ExternalInput").ap()
    cst_in = nc.dram_tensor("consts", [128, C_N], F32, kind="ExternalInput").ap()
    W = {n: nc.dram_tensor(n, WEIGHT_SHAPES[n], F32, kind="ExternalInput").ap() for n in WEIGHT_NAMES}
    out = nc.dram_tensor("out", [T, D], F32, kind="ExternalOutput").ap()
    sk = "ExternalOutput" if dbg else "Internal"
    PF = nc.dram_tensor("PF", [26, 128, T], F32, kind=sk).ap()
    PT = nc.dram_tensor("PT", [T, 3072], F32, kind=sk).ap()
    OM = nc.dram_tensor("OM", [T, 1024], F32, kind=sk).ap()
    VF = nc.dram_tensor("VF", [4, 128, T], F32, kind=sk).ap()
    X1 = nc.dram_tensor("X1", [T, D], F32, kind=sk).ap()

    with contextlib.ExitStack() as stack:
        s = Sched(nc, stack)

        def sbt(name, cols, dtype=F32, parts=128):
            t = stack.enter_context(nc.sbuf_tensor(name, [128, cols], dtype))
            return Tl(t[0:parts, :], name)

        arena_t = stack.enter_context(nc.sbuf_tensor("arena", [128, ARENA_WORDS], F32))
        ar = Arena(arena_t, ARENA_WORDS)
        banks = []
        for i in range(8):
            t = stack.enter_context(nc.psum_tensor(f"bank{i}", [128, 512], F32))
            banks.append(Tl(t[:, :], f"bank{i}"))
            banks[-1].res.excl = True
        bank_i = [0]

        def bank():
            b = banks[bank_i[0] % 8]
            bank_i[0] += 1
            return b

        def drain(gens):
            gens = list(gens)
            while gens:
                nxt = []
                for g_ in gens:
                    try:
                        next(g_)
                        nxt.append(g_)
                    except StopIteration:
                        pass
                gens = nxt

        dq = [0]

        def dmaq():
            dq[0] += 1
            return "sp" if dq[0] % 2 else "pool"

        ev = [0]

        def evac(out_ap, in_ap, reads, writes):
            ev[0] += 1
            if ev[0] % 2:
                return s.dve(CP(out_ap, in_ap), reads, writes)
            return s.act(ACP(out_ap, in_ap), reads, writes)

        cst = sbt("cst", C_N)
        s.dma("sp", cst.ap, cst_in, writes=[cst])
        ident = cst.ap[:, C_ID:C_ID + 128]
        id16 = sbt("id16", 128, BF16)
        s.dve(CP(id16.ap, ident), [cst], [id16])
        hsel16 = sbt("hsel16", 2, BF16)
        s.dve(CP(hsel16.ap, cst.ap[:, C_HSEL:C_HSEL + 2]), [cst], [hsel16])
        blk32 = cst.ap[:, C_BLK:C_BLK + 128]
        blk16 = sbt("blk16", 128, BF16)
        s.dve(CP(blk16.ap, cst.ap[:, C_BLK:C_BLK + 128]), [cst], [blk16])
        scanmask = cst.ap[:, C_SM:C_SM + 256]
        hgmask = sbt("hgmask", 256)
        rwmXY = sbt("rwmXY", 512)
        rwmZ = sbt("rwmZ", 256)
        for h in range(4):
            s.pool(CP(hgmask.ap[:, h * 64:(h + 1) * 64], cst.ap[:, C_LE:C_LE + 64]), [cst], [hgmask])
            s.pool(CP(rwmXY.ap[:, h * 128:h * 128 + 64], cst.ap[:, C_LT:C_LT + 64]), [cst], [rwmXY])
            s.pool(CP(rwmXY.ap[:, h * 128 + 64:h * 128 + 128], cst.ap[:, C_LE:C_LE + 64]), [cst], [rwmXY])
            s.pool(CP(rwmZ.ap[:, h * 64:(h + 1) * 64], cst.ap[:, C_GT:C_GT + 64]), [cst], [rwmZ])

        cols = sbt("cols", 96)
        K_SH, K_SC = 0, 8
        K_LB, K_OMLB, K_NOMLB = 16, 20, 24
        K_W0, K_A0, K_KK, K_KA, K_OMKA, K_RK, K_V0 = 28, 32, 36, 40, 44, 48, 52
        K_MU, K_OMU = 56, 73
        tmp_cols = sbt("tmp_cols", 64)
        gate_b = sbt("gate_b", 1024)
        hgg_b = sbt("hgg_b", 512)
        gng_b = sbt("gng_b", 512)
        gnb_b = sbt("gnb_b", 512)
        lng_b = sbt("lng_b", 1024)
        lnb_b = sbt("lnb_b", 1024)
        lora16 = sbt("lora16", 512, BF16)
        vup16 = sbt("vup16", 512, BF16)

        def colload(dst_c, src_ap_1d, n):
            s.dma(dmaq(), cols.ap[:, dst_c:dst_c + n], src_ap_1d.rearrange("(k p) -> p k", p=128),
                  writes=[cols], allow_slow_non_contiguous=True)

        def layer_setup(l):
            ar.reset()
            colload(K_W0, W["rw_w0"][l], 4)
            colload(K_A0, W["rw_a0"][l], 4)
            colload(K_KK, W["rw_k_k"][l], 4)
            colload(K_KA, W["rw_k_a"][l], 4)
            colload(K_RK, W["rw_r_k"][l], 4)
            colload(K_MU, W["rw_mu"][l], 17)
            if l >= 1:
                colload(K_V0, W["rw_v0"][l - 1], 4)
            s.dve(TS(cols.ap[:, K_OMU:K_OMU + 17], cols.ap[:, K_MU:K_MU + 17], -1.0, 1.0), [cols], [cols])
            s.dve(TS(cols.ap[:, K_OMKA:K_OMKA + 4], cols.ap[:, K_KA:K_KA + 4], -1.0, 1.0), [cols], [cols])
            if l == 0:
                s.dve(MS(cols.ap[:, K_LB:K_LB + 4], 0.0), [], [cols])
            else:
                s.dma(dmaq(), tmp_cols.ap[:, 0:4], W["hg_lower_bounds"][0].rearrange("(k p) -> p k", p=128),
                      writes=[tmp_cols], allow_slow_non_contiguous=True)
                s.dma(dmaq(), tmp_cols.ap[:, 4:8], W["hg_lower_bounds"][1].rearrange("(k p) -> p k", p=128),
                      writes=[tmp_cols], allow_slow_non_contiguous=True)
                s.dve(TT(tmp_cols.ap[:, 8:12], tmp_cols.ap[:, 4:8], tmp_cols.ap[:, 0:4], ALU.subtract),
                      [tmp_cols], [tmp_cols])
                s.act(ACTF(cols.ap[:, K_LB:K_LB + 4], tmp_cols.ap[:, 8:12], AF.Sigmoid), [tmp_cols], [cols])
            s.dve(TS(cols.ap[:, K_OMLB:K_OMLB + 4], cols.ap[:, K_LB:K_LB + 4], -1.0, 1.0), [cols], [cols])
            s.dve(TS(cols.ap[:, K_NOMLB:K_NOMLB + 4], cols.ap[:, K_OMLB:K_OMLB + 4], -1.0), [cols], [cols])
            if CUT == 1:
                s.barrier(); return
            s.dma(dmaq(), hgg_b.ap, W["hg_norm_g"][l].partition_broadcast(128), writes=[hgg_b])
            s.dma(dmaq(), gng_b.ap, W["rw_gn_g"][l].partition_broadcast(128), writes=[gng_b])
            s.dma(dmaq(), gnb_b.ap, W["rw_gn_b"][l].partition_broadcast(128), writes=[gnb_b])
            s.dma(dmaq(), lng_b.ap, W["ln_g"][l].partition_broadcast(128), writes=[lng_b])
            s.dma(dmaq(), lnb_b.ap, W["ln_b"][l].partition_broadcast(128), writes=[lnb_b])
            if CUT == 2:
                s.barrier(); return
            st = ar.f32("lst", 512)
            s.dma(dmaq(), st.ap[0:64, :], W["rw_w_up"][l], writes=[st])
            s.dma(dmaq(), st.ap[64:128, :], W["rw_a_up"][l], writes=[st])
            s.dve(CP(lora16.ap, st.ap), [st], [lora16])
            if l >= 1:
                st2 = ar.f32("lst2", 512)
                s.dma(dmaq(), st2.ap[0:32, :], W["rw_v_up"][l - 1], writes=[st2])
                s.dve(CP(vup16.ap[0:32, :], st2.ap[0:32, :]), [st2], [vup16])
            if CUT == 3:
                s.barrier(); return
            ccol = ar.f32("ccol", 8)
            s.dma(dmaq(), ccol.ap, c_in[0].rearrange("(k p) -> p k", p=128), writes=[ccol],
                  allow_slow_non_contiguous=True)
            scol = ar.f32("scol", 8)
            s.act(ACTF(scol.ap, ccol.ap, AF.Silu), [ccol], [scol])
            srep = ar.f32("srep", 8 * 128)
            s.dve(CP(srep.v3(8, 128), scol.ap.unsqueeze(2).to_broadcast([128, 8, 128])), [scol], [srep])
            bcol = ar.f32("bcol", 24)
            s.dma(dmaq(), bcol.ap, W["b_ada"][l].rearrange("(k p) -> p k", p=128), writes=[bcol],
                  allow_slow_non_contiguous=True)
            brow = ar.f32("brow", 3072)
            s.dma(dmaq(), brow.ap, W["b_ada"][l].partition_broadcast(128), writes=[brow])
            crow = ar.f32("crow", 2048)
            ctmp = ar.f32("ctmp", 1024)
            if CUT == 4:
                s.barrier(); return
            wst = [ar.f32(f"adaw{i}", 8 * 512) for i in range(2)]
            for cb in range(6):
                wt = wst[cb % 2]
                s.dma(dmaq(), wt.v3(8, 512),
                      W["w_ada"][l][:, cb * 512:(cb + 1) * 512].rearrange("(k p) c -> p k c", p=128), writes=[wt])
                pb = bank()
                for kc in range(8):
                    s.pe(MM(pb.ap[:, 0:512], srep.v3(8, 128)[:, kc, :], wt.v3(8, 512)[:, kc, :],
                            kc == 0, kc == 7), [wt, srep], [pb])
                if cb < 4:
                    s.dve(TT(crow.ap[:, cb * 512:(cb + 1) * 512], pb.ap[:, 0:512], brow.ap[:, cb * 512:(cb + 1) * 512],
                             ALU.add), [pb, brow], [crow])
                else:
                    g0 = (cb - 4) * 512
                    s.dve(TT(gate_b.ap[:, g0:g0 + 512], pb.ap[:, 0:512], brow.ap[:, cb * 512:(cb + 1) * 512], ALU.add),
                          [pb, brow], [gate_b])
            for which in range(2):
                s.dve(TT(ctmp.v3(8, 128), crow.ap[:, which * 1024:(which + 1) * 1024].rearrange("p (k c) -> p k c", k=8),
                         ident.unsqueeze(1).to_broadcast([128, 8, 128]), ALU.mult), [crow, cst], [ctmp])
                dstc = K_SH if which == 0 else K_SC
                s.dve(RED(cols.ap[:, dstc:dstc + 8], ctmp.v3(8, 128)), [ctmp], [cols])
            s.dve(TS(cols.ap[:, K_SC:K_SC + 8], cols.ap[:, K_SC:K_SC + 8], 1.0, None, ALU.add), [cols], [cols])
            s.dve(TS(gate_b.ap, gate_b.ap, 1.0, None, ALU.add), [gate_b], [gate_b])
            s.barrier()

        def fmcol(ci):
            if ci < 4:
                return ci * 128
            if ci < 8:
                return 512 + (ci - 4) * 128
            return 2048 + (ci - 8) * 128

        def tmcol(g):
            return (1024, 1536, 4224, 4736, 5248, 5760)[g]

        def pass_A(l, xin):
            ar.reset()
            w16 = ar.bf("w16", 8 * NIN)
            w16v = w16.v3(8, NIN)
            hT = [ar.bf(f"hT{i}", 8 * NB) for i in range(2)]
            xt = [ar.f32(f"xt{i}", 1024) for i in range(2)]
            wst = [ar.f32(f"wst{i}", 1568) for i in range(3)]
            fst = [ar.f32(f"fst{i}", 512) for i in range(3)]
            tst = [ar.f32(f"tst{i}", 3072) for i in range(2)]
            vd16 = ar.bf("vd16", 8 * 32) if l >= 1 else None
            n = 0
            for kc in range(8):
                for qd in range(4):
                    st = wst[n % 3]
                    s.dma(dmaq(), st.ap, W["w_in"][l][kc * 128:(kc + 1) * 128, qd * 1568:(qd + 1) * 1568],
                          writes=[st])
                    dst = w16v[:, kc, qd * 1568:(qd + 1) * 1568]
                    eng = ("dve", "act", "pool")[n % 3]
                    if eng == "act":
                        s.act(ACP(dst, st.ap), [st], [w16.r((kc, qd))])
                    else:
                        s.op(eng, CP(dst, st.ap), [st], [w16.r((kc, qd))])
                    n += 1
            if l >= 1:
                vst = ar.f32("vst", 8 * 32)
                s.dma(dmaq(), vst.v3(8, 32), W["rw_v_down"][l - 1].rearrange("(k p) c -> p k c", p=128),
                      writes=[vst])
                s.dve(CP(vd16.ap, vst.ap), [vst], [vd16])

            def wres(kc, c0, c1):
                return [w16.r((kc, q)) for q in range(c0 // 1568, (c1 - 1) // 1568 + 1)]

            nfm = 25
            fi = 0
            for blk in range(NBLK):
                t0 = blk * NB
                h = hT[blk % 2]
                hv = h.v3(8, NB)
                for tt in range(2):
                    xtile = xt[tt]
                    s.dma("sp", xtile.ap, xin[t0 + tt * 128:t0 + (tt + 1) * 128, :], writes=[xtile])
                    for half in range(2):
                        pb = bank()
                        for j in range(4):
                            kc = half * 4 + j
                            s.pe(TR(pb.ap[:, j * 128:(j + 1) * 128], xtile.ap[:, kc * 128:(kc + 1) * 128], ident),
                                 [xtile, cst], [pb])
                        for j in range(4):
                            kc = half * 4 + j
                            s.act(ACTF(hv[:, kc, tt * 128:(tt + 1) * 128], pb.ap[:, j * 128:(j + 1) * 128],
                                       AF.Identity, bias=cols.ap[:, K_SH + kc:K_SH + kc + 1],
                                       scale=cols.ap[:, K_SC + kc:K_SC + kc + 1]), [pb, cols], [h])
                for pr in range((nfm + 1) // 2):
                    cis = [ci for ci in (2 * pr, 2 * pr + 1) if ci < nfm]
                    pb = bank()
                    for j, ci in enumerate(cis):
                        c0 = fmcol(ci)
                        for kc in range(8):
                            s.pe(MM(pb.ap[:, j * NB:(j + 1) * NB], w16v[:, kc, c0:c0 + 128], hv[:, kc, :],
                                    kc == 0, kc == 7), wres(kc, c0, c0 + 128) + [h], [pb])
                    st = fst[fi % 3]
                    fi += 1
                    nn = len(cis) * NB
                    evac(st.ap[:, 0:nn], pb.ap[:, 0:nn], [pb], [st])
                    s.dma("pool", PF[cis[0]:cis[0] + len(cis), :, t0:t0 + NB].rearrange("c p t -> p c t"),
                          st.ap[:, 0:nn].rearrange("p (c t) -> p c t", c=len(cis)), reads=[st])
                if l >= 1:
                    pb = bank()
                    for kc in range(8):
                        s.pe(MM(pb.ap[0:32, 0:NB], vd16.v3(8, 32)[:, kc, :], hv[:, kc, :], kc == 0, kc == 7),
                             [vd16, h], [pb])
                    st = fst[fi % 3]
                    fi += 1
                    evac(st.ap[0:32, 0:NB], pb.ap[0:32, 0:NB], [pb], [st])
                    s.dma("pool", PF[25, 0:32, t0:t0 + NB], st.ap[0:32, 0:NB], reads=[st])
                for tt in range(2):
                    ts_ = tst[tt]
                    for g in range(6):
                        pb = bank()
                        c0 = tmcol(g)
                        for kc in range(8):
                            s.pe(MM(pb.ap[:, 0:512], hv[:, kc, tt * 128:(tt + 1) * 128], w16v[:, kc, c0:c0 + 512],
                                    kc == 0, kc == 7), wres(kc, c0, c0 + 512) + [h], [pb])
                        evac(ts_.ap[:, g * 512:(g + 1) * 512], pb.ap[:, 0:512], [pb], [ts_.r(g)])
                    s.dma("pool", PT[t0 + tt * 128:t0 + (tt + 1) * 128, :], ts_.ap,
                          reads=[ts_.r(g) for g in range(6)], writes=[ts_.r(g) for g in range(0)])
            s.barrier()

        def pass_B1(l):
            ar.reset()
            NCH = NB // CH
            S32 = [ar.f32(f"S32_{h}", 128) for h in range(4)]
            S16 = [ar.bf(f"S16_{h}", 128) for h in range(4)]
            for h in range(4):
                s.dve(MS(S32[h].ap, 0.0), [], [S32[h]])
                s.dve(MS(S16[h].ap, 0.0), [], [S16[h]])
            qf = [ar.f32(f"qf{i}", 8 * NB) for i in range(2)]
            tmv = [[ar.f32(f"tmv{i}_{tt}", 1024) for tt in range(2)] for i in range(2)]
            sig_ = [ar.f32(f"sig{i}", 4 * NB) for i in range(2)]
            fv_ = [ar.f32(f"fv{i}", 4 * NB) for i in range(2)]
            lgf_ = [ar.f32(f"lgf{i}", 4 * NB) for i in range(2)]
            khg_ = [ar.f32(f"khg{i}", 4 * NB) for i in range(2)]
            bcs_ = [ar.f32(f"bcs{i}", 4 * NB) for i in range(2)]
            eb_ = [ar.f32(f"eb{i}", 4 * NB) for i in range(2)]
            enb_ = [ar.f32(f"enb{i}", 4 * NB) for i in range(2)]
            kt32_ = [ar.f32(f"kt32{i}", 4 * NB) for i in range(2)]
            q16_ = [ar.bf(f"q16{i}", 4 * NB) for i in range(2)]
            k16_ = [ar.bf(f"k16{i}", 4 * NB) for i in range(2)]
            kh16_ = [ar.bf(f"kh16{i}", 4 * NB) for i in range(2)]
            v16 = [ar.bf(f"v16_{tt}", 512) for tt in range(2)]
            sT16 = [ar.bf(f"sT16_{tt}", 256) for tt in range(2)]
            khtm = [ar.bf(f"khtm_{tt}", 512) for tt in range(2)]
            osb_ = [ar.f32(f"osb{i}", 512) for i in range(2)]
            osq_ = [ar.f32(f"osq{i}", 512) for i in range(2)]
            ss_ = [ar.f32(f"ss{i}", 8) for i in range(2)]
            zs_ = [ar.f32(f"zs{i}", 512) for i in range(2)]
            ofin = [ar.f32(f"ofin{i}", 512) for i in range(2)]
            def prep_b1(blk):
                t0 = blk * NB
                sig, fv, lgf, khg, bcs = sig_[blk % 2], fv_[blk % 2], lgf_[blk % 2], khg_[blk % 2], bcs_[blk % 2]
                eb, enb, kt32 = eb_[blk % 2], enb_[blk % 2], kt32_[blk % 2]
                q16, k16, kh16 = q16_[blk % 2], k16_[blk % 2], kh16_[blk % 2]
                qt = qf[blk % 2]
                tm = tmv[blk % 2]
                s.dma("sp", qt.v3(8, NB), PF[0:8, :, t0:t0 + NB].rearrange("c p t -> p c t"), writes=[qt]); yield
                for tt in range(2):
                    s.dma("sp", tm[tt].ap, PT[t0 + tt * 128:t0 + (tt + 1) * 128, 0:1024], writes=[tm[tt]]); yield
                s.act(ACTF(sig.ap, qt.ap[:, 4 * NB:8 * NB], AF.Sigmoid), [qt], [sig]); yield
                for h in range(4):
                    sl = slice(h * NB, (h + 1) * NB)
                    s.dve(TS(fv.ap[:, sl], sig.ap[:, sl], cols.ap[:, K_OMLB + h:K_OMLB + h + 1],
                             cols.ap[:, K_LB + h:K_LB + h + 1]), [sig, cols], [fv]); yield
                    s.pool(TS(khg.ap[:, sl], sig.ap[:, sl], cols.ap[:, K_NOMLB + h:K_NOMLB + h + 1],
                              cols.ap[:, K_OMLB + h:K_OMLB + h + 1]), [sig, cols], [khg]); yield
                s.act(ACTF(lgf.ap, fv.ap, AF.Ln), [fv], [lgf]); yield
                for h in range(4):
                    sl = slice(h * NB, (h + 1) * NB)
                    s.dve(SCAN(bcs.ap[:, sl], scanmask, lgf.ap[:, sl]), [lgf, cst], [bcs]); yield
                s.act(ACTF(eb.ap, bcs.ap, AF.Exp), [bcs], [eb]); yield
                s.act(ACTF(enb.ap, bcs.ap, AF.Exp, scale=-1.0), [bcs], [enb]); yield
                s.dve(TT(q16.ap, qt.ap[:, 0:4 * NB], eb.ap, ALU.mult), [qt, eb], [q16]); yield
                s.dve(TT(kt32.ap, khg.ap, enb.ap, ALU.mult), [khg, enb], [kt32]); yield
                s.act(ACP(k16.ap, kt32.ap), [kt32], [k16]); yield
                e4 = eb.v4(4, NCH, CH)
                s.dve(TT(kh16.v4(4, NCH, CH), kt32.v4(4, NCH, CH), e4[:, :, :, CH - 1:CH].to_broadcast([128, 4, NCH, CH]),
                         ALU.mult), [kt32, eb], [kh16]); yield

            def drip(gen, n):
                if gen is None:
                    return
                for _ in range(n):
                    try:
                        next(gen)
                    except StopIteration:
                        return

            drain([prep_b1(0)])
            for blk in range(NBLK):
                t0 = blk * NB
                posts = []
                nxt = prep_b1(blk + 1) if blk + 1 < NBLK else None
                eb = eb_[blk % 2]
                q16, k16, kh16 = q16_[blk % 2], k16_[blk % 2], kh16_[blk % 2]
                tm = tmv[blk % 2]
                for tt in range(2):
                    osb, osq, ss, zs = osb_[tt], osq_[tt], ss_[tt], zs_[tt]
                    vt = tm[tt]
                    s.act(ACP(v16[tt].ap, vt.ap[:, 0:512]), [vt], [v16[tt]])
                    pbs = bank()
                    pbk = bank()
                    pbk16 = pbk.ap.bitcast(BF16)
                    for par in range(2):
                        cc = tt * 2 + par
                        ps_ = slice(par * 64, (par + 1) * 64)
                        for h in range(4):
                            c0 = h * NB + cc * CH
                            s.pe(MM(pbs.ap[ps_, h * 64:(h + 1) * 64], k16.ap[:, c0:c0 + CH], q16.ap[:, c0:c0 + CH]),
                                 [k16, q16], [pbs])
                            s.pe(TR(pbk16[ps_, h * 128:(h + 1) * 128], kh16.ap[:, c0:c0 + CH], id16.ap),
                                 [kh16, id16], [pbk])
                    s.dve(TT(sT16[tt].ap, pbs.ap[:, 0:256], hgmask.ap, ALU.mult), [pbs, hgmask], [sT16[tt]])
                    s.act(ACP(khtm[tt].ap, pbk16[:, 0:512]), [pbk], [khtm[tt]])
                    pbo = bank()
                    for par in range(2):
                        cc = tt * 2 + par
                        ps_ = slice(par * 64, (par + 1) * 64)
                        for h in range(4):
                            c0 = h * NB + cc * CH
                            hs = slice(h * 128, (h + 1) * 128)
                            s.pe(MM(pbo.ap[ps_, hs], sT16[tt].ap[ps_, h * 64:(h + 1) * 64], v16[tt].ap[ps_, hs],
                                    True, False), [sT16[tt], v16[tt]], [pbo])
                            s.pe(MM(pbo.ap[ps_, hs], q16.ap[:, c0:c0 + CH], S16[h].ap, False, True),
                                 [q16, S16[h]], [pbo])
                        pkv = bank()
                        for h in range(4):
                            hs = slice(h * 128, (h + 1) * 128)
                            s.pe(MM(pkv.ap[:, hs], khtm[tt].ap[ps_, hs], v16[tt].ap[ps_, hs]),
                                 [khtm[tt], v16[tt]], [pkv])
                        for h in range(4):
                            c0 = h * NB + cc * CH
                            hs = slice(h * 128, (h + 1) * 128)
                            s.dve(STT(S32[h].ap, S32[h].ap, eb.ap[:, c0 + CH - 1:c0 + CH], pkv.ap[:, hs],
                                      ALU.mult, ALU.add), [S32[h], eb, pkv], [S32[h]])
                            s.act(ACP(S16[h].ap, S32[h].ap), [S32[h]], [S16[h]])
                        drip(nxt, 7)
                    s.act(ACP(osb.ap, pbo.ap[:, 0:512]), [pbo], [osb])

                    def post_b1(osb, osq, ss, zs, vt, of, row0):
                        s.pool(TT(osq.ap, osb.ap, osb.ap, ALU.mult), [osb], [osq]); yield
                        s.dve(RED(ss.ap[:, 0:4], osq.v3(4, 128)), [osq], [ss]); yield
                        s.dve(TS(ss.ap[:, 0:4], ss.ap[:, 0:4], 1.0 / 128.0, RMS_EPS), [ss], [ss]); yield
                        s.act(ACTF(ss.ap[:, 0:4], ss.ap[:, 0:4], AF.Sqrt), [ss], [ss]); yield
                        s.dve(RCP(ss.ap[:, 4:8], ss.ap[:, 0:4]), [ss], [ss]); yield
                        s.act(ACTF(zs.ap, vt.ap[:, 512:1024], AF.Silu), [vt], [zs]); yield
                        s.pool(TT(zs.ap, zs.ap, hgg_b.ap, ALU.mult), [zs, hgg_b], [zs]); yield
                        s.dve(TT(of.v3(4, 128), osb.v3(4, 128), ss.ap[:, 4:8].unsqueeze(2).to_broadcast([128, 4, 128]),
                                 ALU.mult), [osb, ss], [of]); yield
                        s.dve(TT(of.ap, of.ap, zs.ap, ALU.mult), [of, zs], [of]); yield
                        s.dma("pool", OM[row0:row0 + 128, 0:512], of.ap, reads=[of]); yield

                    posts.append(post_b1(osb, osq, ss, zs, vt, ofin[tt], t0 + tt * 128))
                if nxt is not None:
                    drain([nxt])
                drain(posts)
            s.barrier()

        def pass_B2(l):
            ar.reset()
            NCH = NB // CH
            W4 = 4 * NB
            H32 = ar.f32("H32", 512)
            H16 = ar.bf("H16", 512)
            s.dve(MS(H32.ap, 0.0), [], [H32])
            s.dve(MS(H16.ap, 0.0), [], [H16])
            rwl = [ar.f32(f"rwl{i}", 17 * (NB + 1)) for i in range(1)]
            sh = ar.f32("sh", 17 * NB)
            shv = sh.v3(17, NB)
            lw = ar.f32("lw", W4)
            av = ar.f32("av", W4)
            bcs = ar.f32("bcs", W4)
            t1 = ar.f32("t1", W4)
            t2 = ar.f32("t2", W4)
            t3 = ar.f32("t3", W4)
            eb = ar.f32("eb", W4)
            enb = ar.f32("enb", W4)
            ebx = ar.f32("ebx", W4)
            egb = ar.f32("egb", W4)
            kkn = ar.f32("kkn", W4)
            kmod = ar.f32("kmod", W4)
            uu = ar.f32("uu", W4)
            PR16 = ar.bf("PR16", 2 * W4)
            PT16 = Tl(PR16.ap[:, 0:W4], "PT16v")
            RT16 = Tl(PR16.ap[:, W4:2 * W4], "RT16v")
            UT16 = ar.bf("UT16", W4)
            KT16 = ar.bf("KT16", W4)
            UTz = [ar.bf(f"UTz{i}", W4) for i in range(2)]
            KTz = [ar.bf(f"KTz{i}", W4) for i in range(2)]
            PTz = [ar.bf(f"PTz{i}", W4) for i in range(2)]
            for zt in UTz + KTz + PTz:
                s.pool(MS(zt.ap, 0.0), [], [zt])
            UG16 = ar.bf("UG16", W4)
            KG16 = ar.bf("KG16", W4)
            rk16 = ar.bf("rk16", W4)
            lo16 = ar.bf("lo16", NB)
            vr16 = ar.bf("vr16", NB) if l >= 1 else None
            vrs = ar.f32("vrs", NB) if l >= 1 else None
            vfl = ar.f32("vfl", W4) if l >= 1 else None
            vtm32 = ar.f32("vtm32", 512)
            vtm16 = ar.bf("vtm16", 512)
            zsl_ = [ar.f32(f"zsl{i}", 512) for i in range(2)]
            bon_ = [ar.f32(f"bon{i}", 512) for i in range(2)]
            AXY = [ar.bf(f"AXY{g}", 1024) for g in range(2)]
            RT32 = ar.f32("RT32", W4)
            XY = [[ar.bf(f"XY{g}_{i}", 512) for i in range(2)] for g in range(2)]
            RR = [[ar.bf(f"RR{g}_{i}", 512) for i in range(2)] for g in range(2)]
            QT16_ = [ar.bf(f"QT16{i}", 512) for i in range(2)]
            MT16_ = [ar.bf(f"MT16{i}", 512) for i in range(2)]
            UKG = [ar.bf(f"UKG{g}", 512) for g in range(2)]
            YL32_ = [ar.f32(f"YL32{i}", 512) for i in range(2)]
            N32_ = [ar.f32(f"N32{i}", 512) for i in range(2)]
            Ht = ar.f32("Ht", 256)
            ysb_ = [ar.f32(f"ysb{i}", 512) for i in range(2)]
            yc_ = [ar.f32(f"yc{i}", 512) for i in range(2)]
            ysq_ = [ar.f32("ysq", 512)] * 2
            st8_ = [ar.f32(f"st8{i}", 32) for i in range(2)]
            yfin = [ar.f32(f"yfin{i}", 512) for i in range(2)]
            cvf = cols.ap
            pending = []
            for blk in range(NBLK):
                t0 = blk * NB
                posts = []
                rl = rwl[0]
                if pending:
                    pend = list(pending)

                    def _drip_pending(pend=pend):
                        while pend:
                            try:
                                next(pend[0])
                                pend.append(pend.pop(0))
                                return
                            except StopIteration:
                                pend.pop(0)

                    s.hook = _drip_pending
                rlv = rl.v3(17, NB + 1)
                if t0 == 0:
                    s.dve(MS(rlv[:, :, 0:1], 0.0), [], [rl])
                    s.dma("sp", rlv[:, :, 1:NB + 1], PF[8:25, :, 0:NB].rearrange("c p t -> p c t"), writes=[rl])
                else:
                    s.dma("sp", rlv, PF[8:25, :, t0 - 1:t0 + NB].rearrange("c p t -> p c t"), writes=[rl])
                for kc in range(17):
                    s.act(ACTF(shv[:, kc, :], rlv[:, kc, 1:NB + 1], AF.Identity,
                               scale=cvf[:, K_OMU + kc:K_OMU + kc + 1]), [rl, cols], [sh.r(kc)])
                    s.dve(STT(shv[:, kc, :], rlv[:, kc, 0:NB], cvf[:, K_MU + kc:K_MU + kc + 1], shv[:, kc, :],
                              ALU.mult, ALU.add), [rl, cols, sh.r(kc)], [sh.r(kc)])
                rsl = lambda kc: shv[:, kc, :]
                R_ = sh.ap[:, 0:W4]
                K_ = sh.ap[:, W4:2 * W4]
                V_ = sh.ap[:, 2 * W4:3 * W4]
                Z_ = sh.ap[:, 3 * W4:4 * W4]
                rR = [sh.r(kc) for kc in range(0, 4)]
                rK = [sh.r(kc) for kc in range(4, 8)]
                rV = [sh.r(kc) for kc in range(8, 12)]
                rZ = [sh.r(kc) for kc in range(12, 16)]
                s.act(ACTF(lo16.ap[0:64, :], shv[0:64, 16, :], AF.Tanh), [sh.r(16)], [lo16])
                s.dve(CP(lo16.ap[64:128, :], shv[64:128, 16, :]), [sh.r(16)], [lo16])
                pw = [bank(), bank()]
                pa = [bank(), bank()]
                for kc in range(4):
                    s.pe(MM(pw[kc // 2].ap[:, (kc % 2) * NB:(kc % 2 + 1) * NB], lora16.ap[0:64, kc * 128:(kc + 1) * 128],
                            lo16.ap[0:64, :]), [lora16, lo16], [pw[kc // 2]])
                    s.pe(MM(pa[kc // 2].ap[:, (kc % 2) * NB:(kc % 2 + 1) * NB], lora16.ap[64:128, kc * 128:(kc + 1) * 128],
                            lo16.ap[64:128, :]), [lora16, lo16], [pa[kc // 2]])
                for kc in range(4):
                    sl = slice(kc * NB, (kc + 1) * NB)
                    psl = slice((kc % 2) * NB, (kc % 2 + 1) * NB)
                    s.act(ACTF(lw.ap[:, sl], pw[kc // 2].ap[:, psl], AF.Sigmoid, bias=cvf[:, K_W0 + kc:K_W0 + kc + 1]),
                          [pw[kc // 2], cols], [lw])
                    s.act(ACTF(av.ap[:, sl], pa[kc // 2].ap[:, psl], AF.Sigmoid, bias=cvf[:, K_A0 + kc:K_A0 + kc + 1]),
                          [pa[kc // 2], cols], [av])
                s.dve(TS(lw.ap, lw.ap, -math.exp(-0.5), None, ALU.mult), [lw], [lw])
                if l == 0:
                    s.dma("pool", VF[:, :, t0:t0 + NB].rearrange("c p t -> p c t"), V_.rearrange("p (c t) -> p c t", c=4),
                          reads=rV)
                else:
                    s.dma("sp", vrs.ap[0:32, :], PF[25, 0:32, t0:t0 + NB], writes=[vrs])
                    s.dma("sp", vfl.v3(4, NB), VF[:, :, t0:t0 + NB].rearrange("c p t -> p c t"), writes=[vfl])
                    s.dve(CP(vr16.ap[0:32, :], vrs.ap[0:32, :]), [vrs], [vr16])
                    pv = [bank(), bank()]
                    for kc in range(4):
                        s.pe(MM(pv[kc // 2].ap[:, (kc % 2) * NB:(kc % 2 + 1) * NB], vup16.ap[0:32, kc * 128:(kc + 1) * 128],
                                vr16.ap[0:32, :]), [vup16, vr16], [pv[kc // 2]])
                    for kc in range(4):
                        sl = slice(kc * NB, (kc + 1) * NB)
                        psl = slice((kc % 2) * NB, (kc % 2 + 1) * NB)
                        s.act(ACTF(t1.ap[:, sl], pv[kc // 2].ap[:, psl], AF.Sigmoid,
                                   bias=cvf[:, K_V0 + kc:K_V0 + kc + 1]), [pv[kc // 2], cols], [t1])
                    s.pool(TT(vfl.ap, vfl.ap, V_, ALU.subtract), [vfl] + rV, [vfl])
                    s.dve(TT(vfl.ap, vfl.ap, t1.ap, ALU.mult), [vfl, t1], [vfl])
                    s.pool(TT(V_, V_, vfl.ap, ALU.add), [vfl] + rV, rV)
                for kc in range(4):
                    sl = slice(kc * NB, (kc + 1) * NB)
                    s.dve(SCAN(bcs.ap[:, sl], scanmask, lw.ap[:, sl]), [lw, cst], [bcs])
                s.dve(TT(t2.ap, bcs.ap, lw.ap, ALU.subtract), [bcs, lw], [t2])
                b4 = bcs.v4(4, NCH, CH)
                s.dve(TT(t3.v4(4, NCH, CH), b4[:, :, :, CH - 1:CH].to_broadcast([128, 4, NCH, CH]), b4, ALU.subtract),
                      [bcs], [t3])
                s.act(ACTF(eb.ap, bcs.ap, AF.Exp), [bcs], [eb])
                s.act(ACTF(enb.ap, bcs.ap, AF.Exp, scale=-1.0), [bcs], [enb])
                s.act(ACTF(ebx.ap, t2.ap, AF.Exp), [t2], [ebx])
                s.act(ACTF(egb.ap, t3.ap, AF.Exp), [t3], [egb])
                for kc in range(4):
                    sl = slice(kc * NB, (kc + 1) * NB)
                    s.act(ACTF(t1.ap[:, sl], K_[:, sl], AF.Square, scale=cvf[:, K_KK + kc:K_KK + kc + 1]),
                          [rK[kc], cols], [t1])
                pq = [bank(), bank()]
                for kc in range(4):
                    s.pe(MM(pq[kc // 2].ap[:, (kc % 2) * NB:(kc % 2 + 1) * NB], blk32, t1.ap[:, kc * NB:(kc + 1) * NB]),
                         [cst, t1], [pq[kc // 2]])
                for hf in range(2):
                    sl = slice(hf * 2 * NB, (hf + 1) * 2 * NB)
                    s.dve(TS(t2.ap[:, sl], pq[hf].ap[:, 0:2 * NB], 1e-24, None, ALU.max), [pq[hf]], [t2])
                s.act(ACTF(t2.ap, t2.ap, AF.Ln), [t2], [t2])
                s.act(ACTF(t3.ap, t2.ap, AF.Exp, scale=-0.5), [t2], [t3])
                for kc in range(4):
                    sl = slice(kc * NB, (kc + 1) * NB)
                    s.dve(STT(kkn.ap[:, sl], K_[:, sl], cvf[:, K_KK + kc:K_KK + kc + 1], t3.ap[:, sl], ALU.mult, ALU.mult),
                          [rK[kc], cols, t3], [kkn])
                    s.pool(TS(t1.ap[:, sl], av.ap[:, sl], cvf[:, K_KA + kc:K_KA + kc + 1],
                              cvf[:, K_OMKA + kc:K_OMKA + kc + 1]), [av, cols], [t1])
                s.dve(TT(kmod.ap, K_, t1.ap, ALU.mult), rK + [t1], [kmod])
                s.dve(TT(uu.ap, av.ap, kkn.ap, ALU.mult), [av, kkn], [uu])
                s.dve(STT(PT16.ap, kkn.ap, -1.0, ebx.ap, ALU.mult, ALU.mult), [kkn, ebx], [PT16])
                for hh_ in range(2):
                    hp_ = slice(hh_ * 64, (hh_ + 1) * 64)
                    s.dve(TT(UTz[hh_].ap[hp_, :], uu.ap[hp_, :], enb.ap[hp_, :], ALU.mult), [uu, enb], [UTz[hh_]])
                    s.dve(TT(KTz[hh_].ap[hp_, :], kmod.ap[hp_, :], enb.ap[hp_, :], ALU.mult), [kmod, enb], [KTz[hh_]])
                    s.act(ACP(PTz[hh_].ap[hp_, :], PT16.ap[hp_, :]), [PT16], [PTz[hh_]])
                s.pool(TT(RT32.ap, R_, eb.ap, ALU.mult), rR + [eb], [RT32])
                s.act(ACP(RT16.ap, RT32.ap), [RT32], [RT16])
                s.dve(TT(UG16.ap, uu.ap, egb.ap, ALU.mult), [uu, egb], [UG16])
                s.dve(TT(KG16.ap, kmod.ap, egb.ap, ALU.mult), [kmod, egb], [KG16])
                s.dve(TT(t2.ap, R_, kmod.ap, ALU.mult), rR + [kmod], [t2])
                for kc in range(4):
                    sl = slice(kc * NB, (kc + 1) * NB)
                    s.act(ACTF(rk16.ap[:, sl], t2.ap[:, sl], AF.Identity, scale=cvf[:, K_RK + kc:K_RK + kc + 1]),
                          [t2, cols], [rk16])

                s.hook = None
                drain(pending)
                pending = []
                if CUT == 10:
                    s.barrier(); return
                for tt in range(2):
                    zsl, bon, QT16, MT16, YL32 = zsl_[tt], bon_[tt], QT16_[tt], MT16_[tt], YL32_[tt]
                    N32, ysb, yc, ysq, st8 = N32_[tt], ysb_[tt], yc_[tt], ysq_[tt], st8_[tt]
                    tsl = lambda kc: slice(kc * NB + tt * 128, kc * NB + (tt + 1) * 128)
                    pvt = bank()
                    pzt = bank()
                    for kc in range(4):
                        s.pe(TR(pvt.ap[:, kc * 128:(kc + 1) * 128], V_[:, tsl(kc)], ident), rV + [cst], [pvt])
                    for kc in range(4):
                        s.pe(TR(pzt.ap[:, kc * 128:(kc + 1) * 128], Z_[:, tsl(kc)], ident), rZ + [cst], [pzt])
                    if CUT == 111:
                        s.barrier(); return
                    s.dve(CP(vtm32.ap, pvt.ap[:, 0:512]), [pvt], [vtm32])
                    s.act(ACP(vtm16.ap, pvt.ap[:, 0:512]), [pvt], [vtm16])
                    s.act(ACTF(zsl.ap, pzt.ap[:, 0:512], AF.Silu), [pzt], [zsl])
                    if CUT == 112:
                        s.barrier(); return
                    pst = bank()
                    for kc in range(4):
                        s.pe(MM(pst.ap[:, kc * 128:(kc + 1) * 128], rk16.ap[:, tsl(kc)], blk16.ap), [rk16, blk16], [pst])
                    s.dve(TT(bon.ap, pst.ap[:, 0:512], vtm32.ap, ALU.mult), [pst, vtm32], [bon])

                    if CUT == 11:
                        s.barrier(); return
                    def fm(tile16, h, par):
                        kc, hh = h // 2, h % 2
                        c0 = kc * NB + tt * 128 + par * 64
                        return tile16.ap[hh * 64:(hh + 1) * 64, c0:c0 + 64]

                    def ch(tile16, h, par):
                        kc = h // 2
                        c0 = kc * NB + tt * 128 + par * 64
                        return tile16.ap[:, c0:c0 + 64]

                    ids = lambda hh: id16.ap[:, hh * 64:(hh + 1) * 64]
                    PRv = PR16.ap.rearrange("p (two w) -> p two w", two=2)
                    G = [dict(), dict()]
                    rd = [PT16, RT16] + UTz + KTz + PTz
                    for g in range(2):
                        pX, pY, pZ = bank(), bank(), bank()
                        for par in range(2):
                            ps_ = slice(par * 64, (par + 1) * 64)
                            for hi in range(4):
                                h = g * 4 + hi
                                hh = h % 2
                                c0 = (h // 2) * NB + tt * 128 + par * 64
                                pr = PRv[:, :, c0:c0 + 64]
                                ox = pX.ap[ps_, hi * 128:(hi + 1) * 128].rearrange("p (a b) -> p a b", a=2)
                                oy = pY.ap[ps_, hi * 128:(hi + 1) * 128].rearrange("p (a b) -> p a b", a=2)
                                if MERGE_PR:
                                    s.pe(MM(ox, ch(UTz[hh], h, par), pr), rd, [pX])
                                    s.pe(MM(oy, ch(KTz[hh], h, par), pr), rd, [pY])
                                else:
                                    s.pe(MM(pX.ap[ps_, hi * 128:hi * 128 + 64], ch(UTz[hh], h, par), ch(PT16, h, par)), rd, [pX])
                                    s.pe(MM(pX.ap[ps_, hi * 128 + 64:hi * 128 + 128], ch(UTz[hh], h, par), ch(RT16, h, par)), rd, [pX])
                                    s.pe(MM(pY.ap[ps_, hi * 128:hi * 128 + 64], ch(KTz[hh], h, par), ch(PT16, h, par)), rd, [pY])
                                    s.pe(MM(pY.ap[ps_, hi * 128 + 64:hi * 128 + 128], ch(KTz[hh], h, par), ch(RT16, h, par)), rd, [pY])
                                s.pe(MM(pZ.ap[ps_, hi * 64:(hi + 1) * 64], ch(PTz[hh], h, par), ch(UTz[hh], h, par)), rd, [pZ])
                        G[g]["pA"] = (pX, pY, pZ)
                    for g in range(2):
                        pX, pY, pZ = G[g]["pA"]
                        A = AXY[g]
                        s.dve(TT(A.ap[:, 0:512], pX.ap[:, 0:512], rwmXY.ap, ALU.mult), [pX, rwmXY], [A.r(0)])
                        s.dve(TT(A.ap[:, 512:1024], pY.ap[:, 0:512], rwmXY.ap, ALU.mult), [pY, rwmXY], [A.r(1)])
                        xy = XY[g][0]
                        s.dve(TT(xy.v3(4, 128)[:, :, 0:64], pZ.ap[:, 0:256].rearrange("p (h c) -> p h c", h=4),
                                 rwmZ.v3(4, 64), ALU.mult), [pZ, rwmZ], [xy.r(0)])
                        s.act(ACP(xy.v3(4, 128)[:, :, 64:128], A.ap[:, 0:512].rearrange("p (h c) -> p h c", h=4)[:, :, 0:64]),
                              [A.r(0)], [xy.r(1)])
                    for g in range(2):
                        A = AXY[g]
                        pR = bank()
                        for par in range(2):
                            ps_ = slice(par * 64, (par + 1) * 64)
                            for hi in range(4):
                                h = g * 4 + hi
                                s.pe(MM(pR.ap[ps_, hi * 128:hi * 128 + 64], ch(PT16, h, par), ids(h % 2)), [PT16, id16], [pR])
                                s.pe(MM(pR.ap[ps_, hi * 128 + 64:hi * 128 + 128],
                                        A.ap[ps_, 512 + hi * 128:512 + hi * 128 + 64],
                                        vtm16.ap[ps_, h * 64:(h + 1) * 64]), [A.r(1), vtm16], [pR])
                        G[g]["pR"] = pR
                    for g in range(2):
                        evac(RR[g][0].ap, G[g]["pR"].ap[:, 0:512], [G[g]["pR"]], [RR[g][0]])
                    cur = 0
                    for j in range(6):
                        for g in range(2):
                            xy = XY[g][cur]
                            rr = RR[g][cur]
                            pR = bank()
                            for par in range(2):
                                ps_ = slice(par * 64, (par + 1) * 64)
                                for hi in range(4):
                                    s.pe(MM(pR.ap[ps_, hi * 128:(hi + 1) * 128], xy.ap[ps_, hi * 128 + 64:hi * 128 + 128],
                                            rr.ap[ps_, hi * 128:(hi + 1) * 128]), [xy.r(1), rr], [pR])
                            G[g]["pR"] = pR
                            if j < 5:
                                pS = bank()
                                for par in range(2):
                                    ps_ = slice(par * 64, (par + 1) * 64)
                                    for hi in range(4):
                                        X_ = xy.ap[ps_, hi * 128:hi * 128 + 64]
                                        Y_ = xy.ap[ps_, hi * 128 + 64:hi * 128 + 128]
                                        s.pe(MM(pS.ap[ps_, hi * 128:hi * 128 + 64], Y_, X_), [xy.r(0), xy.r(1)], [pS])
                                        s.pe(MM(pS.ap[ps_, hi * 128 + 64:hi * 128 + 128], X_, Y_), [xy.r(0), xy.r(1)], [pS])
                                G[g]["pS"] = pS
                        for g in range(2):
                            rr = RR[g][cur]
                            rrn = RR[g][1 - cur]
                            xyn = XY[g][1 - cur]
                            if j < 5:
                                s.act(ACP(xyn.ap, G[g]["pS"].ap[:, 0:512]), [G[g]["pS"]], [xyn.r(0), xyn.r(1)])
                            s.dve(TT(rrn.ap, G[g]["pR"].ap[:, 0:512], rr.ap, ALU.add), [G[g]["pR"], rr], [rrn])
                        cur = 1 - cur
                    for g in range(2):
                        pU = bank()
                        for par in range(2):
                            ps_ = slice(par * 64, (par + 1) * 64)
                            for hi in range(4):
                                h = g * 4 + hi
                                s.pe(MM(pU.ap[ps_, hi * 128:hi * 128 + 64], ch(UG16, h, par), ids(h % 2)), [UG16, id16], [pU])
                                s.pe(MM(pU.ap[ps_, hi * 128 + 64:hi * 128 + 128], ch(KG16, h, par), ids(h % 2)), [KG16, id16], [pU])
                        G[g]["pU"] = pU
                    for g in range(2):
                        s.act(ACP(UKG[g].ap, G[g]["pU"].ap[:, 0:512]), [G[g]["pU"]], [UKG[g]])
                    for g in range(2):
                        A = AXY[g]
                        wz = RR[g][cur]
                        ukg = UKG[g]
                        pQs = [bank(), bank()]
                        pNs = [bank(), bank()]
                        pYL = bank()
                        for par in range(2):
                            ps_ = slice(par * 64, (par + 1) * 64)
                            pQ = pQs[par]
                            pN = pNs[par]
                            for hi in range(4):
                                h = g * 4 + hi
                                kcl, hh = hi // 2, hi % 2
                                hb = slice(hh * 64, (hh + 1) * 64)
                                Wm = wz.ap[ps_, hi * 128:hi * 128 + 64]
                                Zm = wz.ap[ps_, hi * 128 + 64:hi * 128 + 128]
                                AruT = A.ap[ps_, hi * 128 + 64:hi * 128 + 128]
                                ArkT = A.ap[ps_, 512 + hi * 128 + 64:512 + hi * 128 + 128]
                                Ug = ukg.ap[ps_, hi * 128:hi * 128 + 64]
                                Kg = ukg.ap[ps_, hi * 128 + 64:hi * 128 + 128]
                                Vm = vtm16.ap[ps_, h * 64:(h + 1) * 64]
                                s.pe(MM(pQ.ap[hb, kcl * 64:(kcl + 1) * 64], Wm, AruT), [wz, A.r(0)], [pQ])
                                s.pe(MM(pQ.ap[hb, 128 + kcl * 64:128 + (kcl + 1) * 64], Wm, Ug), [wz, ukg], [pQ])
                                s.pe(MM(pN.ap[hb, kcl * 64:(kcl + 1) * 64], Ug, Zm, True, False), [ukg, wz], [pN])
                                s.pe(MM(pN.ap[hb, kcl * 64:(kcl + 1) * 64], Kg, Vm, False, True), [ukg, vtm16], [pN])
                                s.pe(MM(pYL.ap[ps_, hi * 64:(hi + 1) * 64], AruT, Zm, True, False), [A.r(0), wz], [pYL])
                                s.pe(MM(pYL.ap[ps_, hi * 64:(hi + 1) * 64], ArkT, Vm, False, True), [A.r(1), vtm16], [pYL])
                        gs = slice(g * 256, (g + 1) * 256)
                        for par in range(2):
                            qo = QT16.ap[:, gs].rearrange("p (k a t) -> p k a t", k=2, a=2)[:, :, par, :]
                            rt = RT32.v3(4, NB)[:, 2 * g:2 * g + 2, tt * 128 + par * 64:tt * 128 + (par + 1) * 64]
                            s.dve(TT(qo, pQs[par].ap[:, 0:128].rearrange("p (k t) -> p k t", k=2), rt, ALU.add),
                                  [pQs[par], RT32], [QT16.r(g)])
                            mo = MT16.ap[:, g * 256 + par * 128:g * 256 + (par + 1) * 128]
                            s.act(ACP(mo, pQs[par].ap[:, 128:256]), [pQs[par]], [MT16.r(g)])
                            no = N32.ap[:, g * 256 + par * 128:g * 256 + (par + 1) * 128]
                            s.act(ACP(no, pNs[par].ap[:, 0:128]), [pNs[par]], [N32.r(g)])
                        s.dve(CP(YL32.ap[:, gs], pYL.ap[:, 0:256]), [pYL], [YL32.r(g)])
                    pYh = [bank(), bank()]
                    Hv = H32.ap[:, 0:256]
                    H16v = H16.ap[:, 0:256]
                    for par in range(2):
                        ps_ = slice(par * 64, (par + 1) * 64)
                        pH = bank()
                        for h in range(8):
                            kc, hh = h // 2, h % 2
                            g, kcl = kc // 2, kc % 2
                            hb = slice(hh * 64, (hh + 1) * 64)
                            qc = g * 256 + kcl * 128 + par * 64
                            mc = g * 256 + par * 128 + kcl * 64
                            Hh = H16v[hb, kc * 64:(kc + 1) * 64]
                            s.pe(MM(pYh[hh].ap[ps_, kc * 64:(kc + 1) * 64], QT16.ap[hb, qc:qc + 64], Hh), [QT16.r(g), H16], [pYh[hh]])
                            s.pe(MM(pH.ap[hb, kc * 64:(kc + 1) * 64], MT16.ap[hb, mc:mc + 64], Hh), [MT16.r(g), H16], [pH])
                        clast = tt * 128 + par * 64 + 63
                        gcb = eb.v3(4, NB)[:, :, clast:clast + 1].to_broadcast([128, 4, 64])
                        s.pool(TT(Ht.v3(4, 64), H32.ap[:, 0:256].rearrange("p (k v) -> p k v", k=4), gcb, ALU.mult), [H32, eb], [Ht])
                        n4 = N32.ap.rearrange("p (g a k v) -> p g a k v", g=2, a=2, k=2)[:, :, par]
                        s.pool(TT(Ht.ap.rearrange("p (g k v) -> p g k v", g=2, k=2), Ht.ap.rearrange("p (g k v) -> p g k v", g=2, k=2),
                                  n4, ALU.add), [Ht, N32.r(0), N32.r(1)], [Ht])
                        s.dve(TT(Hv, pH.ap[:, 0:256], Ht.ap, ALU.add), [pH, Ht], [H32])
                        s.act(ACP(H16v, Hv), [H32], [H16])
                    if CUT == 13:
                        s.barrier(); return
                    for hh in range(2):
                        yo = ysb.ap.rearrange("p (k a v) -> p k a v", k=4, a=2)[:, :, hh, :]
                        yl = YL32.ap.rearrange("p (k a v) -> p k a v", k=4, a=2)[:, :, hh, :]
                        s.dve(TT(yo, pYh[hh].ap[:, 0:256].rearrange("p (k v) -> p k v", k=4), yl, ALU.add),
                              [pYh[hh], YL32.r(0), YL32.r(1)], [ysb])
                    def post_b2(ysb, st8, yc, ysq, bon, zsl, yf, row0):
                        s.dve(RED(st8.ap[:, 0:8], ysb.v3(8, 64)), [ysb], [st8]); yield
                        s.dve(TS(st8.ap[:, 0:8], st8.ap[:, 0:8], -1.0 / 64.0, None, ALU.mult), [st8], [st8]); yield
                        s.dve(TT(yc.v3(8, 64), ysb.v3(8, 64), st8.ap[:, 0:8].unsqueeze(2).to_broadcast([128, 8, 64]), ALU.add),
                              [ysb, st8], [yc]); yield
                        s.pool(TT(yf.ap, yc.ap, yc.ap, ALU.mult), [yc], [yf]); yield
                        s.dve(RED(st8.ap[:, 8:16], yf.v3(8, 64)), [yf], [st8]); yield
                        s.dve(TS(st8.ap[:, 8:16], st8.ap[:, 8:16], 1.0 / 64.0, GN_EPS), [st8], [st8]); yield
                        s.act(ACTF(st8.ap[:, 8:16], st8.ap[:, 8:16], AF.Sqrt), [st8], [st8]); yield
                        s.dve(RCP(st8.ap[:, 16:24], st8.ap[:, 8:16]), [st8], [st8]); yield
                        s.dve(TT(yc.v3(8, 64), yc.v3(8, 64), st8.ap[:, 16:24].unsqueeze(2).to_broadcast([128, 8, 64]), ALU.mult),
                              [yc, st8], [yc]); yield
                        s.pool(TT(yc.ap, yc.ap, gng_b.ap, ALU.mult), [yc, gng_b], [yc]); yield
                        s.pool(TT(yc.ap, yc.ap, gnb_b.ap, ALU.add), [yc, gnb_b], [yc]); yield
                        s.pool(TT(yc.ap, yc.ap, bon.ap, ALU.add), [yc, bon], [yc]); yield
                        s.dve(TT(yf.ap, yc.ap, zsl.ap, ALU.mult), [yc, zsl], [yf]); yield
                        s.dma("pool", OM[row0:row0 + 128, 512:1024], yf.ap, reads=[yf]); yield

                    posts.append(post_b2(ysb, st8, yc, ysq, bon, zsl, yfin[tt], t0 + tt * 128))
                if blk + 1 < NBLK:
                    pending = posts
                else:
                    drain(posts)
            s.barrier()

        def pass_C(l, xin, xout):
            ar.reset()
            wA = ar.bf("wA", 8 * 1024)
            wO = ar.bf("wO", 8 * 1024)
            wst = [ar.f32(f"cwst{i}", 1024) for i in range(3)]
            n = 0
            for kc in range(8):
                for which in range(2):
                    st = wst[n % 3]
                    if which == 0:
                        src = (W["w_branch_hg"][l] if kc < 4 else W["w_branch_rw"][l])[(kc % 4) * 128:(kc % 4 + 1) * 128, :]
                        dst = wA.v3(8, 1024)[:, kc, :]
                        dr = wA.r(kc)
                    else:
                        src = W["w_out"][l][kc * 128:(kc + 1) * 128, :]
                        dst = wO.v3(8, 1024)[:, kc, :]
                        dr = wO.r(kc)
                    s.dma(dmaq(), st.ap, src, writes=[st])
                    if n % 2:
                        s.dve(CP(dst, st.ap), [st], [dr])
                    else:
                        s.act(ACP(dst, st.ap), [st], [dr])
                    n += 1
            om = [ar.f32(f"om{i}", 1024) for i in range(2)]
            xt = [ar.f32(f"cx{i}", 1024) for i in range(2)]
            gt = [ar.f32(f"gt{i}", 2048) for i in range(2)]
            omT_ = [ar.bf(f"omT{i}", 1024) for i in range(2)]
            mg_ = [ar.f32(f"mg{i}", 1024) for i in range(2)]
            m2_ = [ar.f32(f"m2{i}", 1024) for i in range(2)]
            mT_ = [ar.bf(f"mT{i}", 1024) for i in range(2)]
            zr_ = [ar.f32(f"zr{i}", 1024) for i in range(2)]
            bst_ = [ar.f32(f"bst{i}", 16) for i in range(2)]
            xo = [ar.f32(f"xo{i}", 1024) for i in range(2)]
            for it in range(T // 128):
                r0 = it * 128
                omT, mg, m2, mT, zr, bst = omT_[it % 2], mg_[it % 2], m2_[it % 2], mT_[it % 2], zr_[it % 2], bst_[it % 2]
                o_ = om[it % 2]
                x_ = xt[it % 2]
                g_ = gt[it % 2]
                s.dma("sp", o_.ap, OM[r0:r0 + 128, :], writes=[o_])
                s.dma("sp", x_.ap, xin[r0:r0 + 128, :], writes=[x_])
                s.dma("sp", g_.ap, PT[r0:r0 + 128, 1024:3072], writes=[g_])
                for half in range(2):
                    pb = bank()
                    for j in range(4):
                        kc = half * 4 + j
                        s.pe(TR(pb.ap[:, j * 128:(j + 1) * 128], o_.ap[:, kc * 128:(kc + 1) * 128], ident), [o_, cst], [pb])
                    evac(omT.ap[:, half * 512:(half + 1) * 512], pb.ap[:, 0:512], [pb], [omT.r(half)])
                s.act(ACTF(g_.ap, g_.ap, AF.Sigmoid), [g_], [g_])
                omv = omT.v3(8, 128)
                for hf in range(2):
                    cs_ = slice(hf * 512, (hf + 1) * 512)
                    pa_ = bank()
                    pb_ = bank()
                    for kc in range(4):
                        s.pe(MM(pa_.ap[:, 0:512], omv[:, kc, :], wA.v3(8, 1024)[:, kc, cs_], kc == 0, kc == 3),
                             [omT.r(0), wA.r(kc)], [pa_])
                    for kc in range(4, 8):
                        s.pe(MM(pb_.ap[:, 0:512], omv[:, kc, :], wA.v3(8, 1024)[:, kc, cs_], kc == 4, kc == 7),
                             [omT.r(1), wA.r(kc)], [pb_])
                    s.dve(TT(mg.ap[:, cs_], pa_.ap[:, 0:512], g_.ap[:, hf * 512:(hf + 1) * 512], ALU.mult), [pa_, g_], [mg.r(hf)])
                    s.dve(TT(m2.ap[:, cs_], pb_.ap[:, 0:512], g_.ap[:, 1024 + hf * 512:1024 + (hf + 1) * 512], ALU.mult),
                          [pb_, g_], [m2.r(hf)])
                    s.dve(TT(mg.ap[:, cs_], mg.ap[:, cs_], m2.ap[:, cs_], ALU.add), [mg.r(hf), m2.r(hf)], [mg.r(hf)])
                for half in range(2):
                    pb = bank()
                    for j in range(4):
                        kc = half * 4 + j
                        s.pe(TR(pb.ap[:, j * 128:(j + 1) * 128], mg.ap[:, kc * 128:(kc + 1) * 128], ident), [mg.r(half), cst], [pb])
                    evac(mT.ap[:, half * 512:(half + 1) * 512], pb.ap[:, 0:512], [pb], [mT.r(half)])
                mv = mT.v3(8, 128)
                for hf in range(2):
                    cs_ = slice(hf * 512, (hf + 1) * 512)
                    po_ = bank()
                    for kc in range(8):
                        s.pe(MM(po_.ap[:, 0:512], mv[:, kc, :], wO.v3(8, 1024)[:, kc, cs_], kc == 0, kc == 7),
                             [mT.r(kc // 4), wO.r(kc)], [po_])
                    s.dve(TT(zr.ap[:, cs_], po_.ap[:, 0:512], gate_b.ap[:, cs_], ALU.mult), [po_, gate_b], [zr.r(hf)])
                    s.dve(STT(zr.ap[:, cs_], x_.ap[:, cs_], DN_ALPHA, zr.ap[:, cs_], ALU.mult, ALU.add), [x_, zr.r(hf)], [zr.r(hf)])
                    s.dve(BNS(bst.ap[:, hf * 6:(hf + 1) * 6], zr.ap[:, cs_]), [zr.r(hf)], [bst])
                s.dve(BNA(bst.ap[:, 12:14], bst.ap[:, 0:12]), [bst], [bst])
                s.dve(TS(bst.ap[:, 14:15], bst.ap[:, 13:14], LN_EPS, None, ALU.add), [bst], [bst])
                s.act(ACTF(bst.ap[:, 14:15], bst.ap[:, 14:15], AF.Sqrt), [bst], [bst])
                s.dve(RCP(bst.ap[:, 15:16], bst.ap[:, 14:15]), [bst], [bst])
                xo_ = xo[it % 2]
                s.dve(TS(xo_.ap, zr.ap, bst.ap[:, 12:13], bst.ap[:, 15:16], ALU.subtract, ALU.mult),
                      [zr.r(0), zr.r(1), bst], [xo_])
                s.pool(TT(xo_.ap, xo_.ap, lng_b.ap, ALU.mult), [xo_, lng_b], [xo_])
                s.pool(TT(xo_.ap, xo_.ap, lnb_b.ap, ALU.add), [xo_, lnb_b], [xo_])
                s.dma("pool", xout[r0:r0 + 128, :], xo_.ap, reads=[xo_])
            s.barrier()

        s.barrier()
        stages = []
        for l in range(NL):
            xin = x_in if l == 0 else X1
            xout = out if l == NL - 1 else X1
            stages += [("setup", l), ("A", l), ("B1", l), ("B2", l), ("C", l)]
        for name, l in stages:
            xin = x_in if l == 0 else X1
            xout = out if l == NL - 1 else X1
            if name == "setup":
                layer_setup(l)
            elif name == "A":
                pass_A(l, xin)
            elif name == "B1":
                pass_B1(l)
            elif name == "B2":
                pass_B2(l)
            else:
                pass_C(l, xin, xout)
            if stop_after == (name, l):
                break
        s.barrier()
        s.emit()
    return nc, s


_CACHE = {}


def _prep_weights(inputs):
    m = {}
    for n in WEIGHT_NAMES:
        a = np.ascontiguousarray(np.asarray(inputs[n], dtype=np.float32))
        m[n] = a.reshape(WEIGHT_SHAPES[n])
    m["consts"] = make_consts()
    return m


def kernel(**inputs):
    x = np.asarray(inputs["x"], dtype=np.float32)
    c = np.asarray(inputs["c"], dtype=np.float32)
    B, T, _ = x.shape
    key = (T,)
    if key not in _CACHE:
        _CACHE[key] = build(T)[0]
    nc = _CACHE[key]
    wm = _prep_weights(inputs)
    in_maps = []
    for core in range(8):
        b = core % B
        mm = dict(wm)
        mm["x"] = np.ascontiguousarray(x[b])
        mm["c"] = np.ascontiguousarray(c[b:b + 1])
        in_maps.append(mm)
    res = run_bass_kernel_spmd(nc, in_maps, core_ids=list(range(8)))
    return np.stack([res.results[b]["out"] for b in range(B)], axis=0).astype(np.float32)
```
